# Optimizing a Trainium2 kernel written in Bass

```python
import math
import jax, jax.numpy as jnp
from jax import lax
import numpy as np

D_MODEL = 1024
BATCH = 2
SEQ = 8192
DEPTH = 1
DEC_BATCH = 8
DEC_SEQ = 4096
PAST_LEN = 128

HEAD_DIM = 64
A_HEADS = 8
A_KV_HEADS = 2
A_GROUP = A_HEADS // A_KV_HEADS
A_Q_W = A_HEADS * HEAD_DIM
A_KV_W = A_KV_HEADS * HEAD_DIM
A_OUT_W = A_Q_W
B_HEADS = 4
B_QK_W = B_HEADS * 2 * HEAD_DIM
B_V_DIM = 2 * HEAD_DIM
B_V_W = B_HEADS * B_V_DIM
IN_W = A_Q_W + 2 * A_KV_W + 2 * B_QK_W + B_V_W
D_FF = 2816
GRID_W = 64
AXIAL_THETA = 10000.0
ROPE_THETA = 500000.0
ROPE_DIM = HEAD_DIM // 4
Q_BLOCK = 128
RMS_EPS = 1e-6
SCALE = 1.0 / math.sqrt(HEAD_DIM)

kernel_name = "hybrid_gqa_axial_diffattn_macaron_encoder"


def rms_norm(x, g):
    xf = x.astype(jnp.float32)
    y = xf * lax.rsqrt(jnp.mean(xf * xf, axis=-1, keepdims=True) + RMS_EPS)
    return (y * g.astype(jnp.float32)).astype(x.dtype)


def swiglu(h, w_gate, w_up, w_down):
    return (jax.nn.silu(h @ w_gate) * (h @ w_up)) @ w_down


def rotate(x, ang):
    m = ang.shape[-1]
    cos = jnp.cos(ang)[:, None, :]
    sin = jnp.sin(ang)[:, None, :]
    xf = x.astype(jnp.float32)
    x1, x2 = xf[..., :m], xf[..., m:]
    return jnp.concatenate([x1 * cos - x2 * sin, x2 * cos + x1 * sin], axis=-1).astype(x.dtype)


def axial_angles(seq_len):
    rows = seq_len // GRID_W
    row = jnp.repeat(jnp.arange(rows, dtype=jnp.float32), GRID_W)
    col = jnp.tile(jnp.arange(GRID_W, dtype=jnp.float32), rows)
    half = HEAD_DIM // 2
    inv = AXIAL_THETA ** (-jnp.arange(0, half, 2, dtype=jnp.float32) / half)
    return jnp.concatenate([row[:, None] * inv, col[:, None] * inv], axis=-1)


def partial_angles(seq_len):
    inv = ROPE_THETA ** (-jnp.arange(0, ROPE_DIM, 2, dtype=jnp.float32) / ROPE_DIM)
    pos = jnp.arange(seq_len, dtype=jnp.float32)
    return pos[:, None] * inv


def gqa_attention(q, k, v):
    bsz, seq_len = q.shape[0], q.shape[1]
    nb = seq_len // Q_BLOCK
    qb = q.reshape(bsz, nb, Q_BLOCK, A_KV_HEADS, A_GROUP, HEAD_DIM).transpose(1, 0, 2, 3, 4, 5)

    def block(qi):
        s = jnp.einsum('bqhgd,bkhd->bhgqk', qi, k).astype(jnp.float32) * SCALE
        p = jax.nn.softmax(s, axis=-1)
        return jnp.einsum('bhgqk,bkhd->bqhgd', p.astype(v.dtype), v)

    o = lax.map(block, qb)
    return o.transpose(1, 0, 2, 3, 4, 5).reshape(bsz, seq_len, A_OUT_W)


def diff_attention(q1, q2, k1, k2, v, lam):
    bsz, seq_len = q1.shape[0], q1.shape[1]
    nb = seq_len // Q_BLOCK

    def to_blocks(t):
        return t.reshape(bsz, nb, Q_BLOCK, B_HEADS, HEAD_DIM).transpose(1, 0, 2, 3, 4)

    def block(qs):
        a, b = qs
        s1 = jnp.einsum('bqhd,bkhd->bhqk', a, k1).astype(jnp.float32) * SCALE
        s2 = jnp.einsum('bqhd,bkhd->bhqk', b, k2).astype(jnp.float32) * SCALE
        p = jax.nn.softmax(s1, axis=-1) - lam * jax.nn.softmax(s2, axis=-1)
        return jnp.einsum('bhqk,bkhe->bqhe', p.astype(v.dtype), v)

    o = lax.map(block, (to_blocks(q1), to_blocks(q2)))
    return o.transpose(1, 0, 2, 3, 4).reshape(bsz, seq_len, B_HEADS, B_V_DIM)


def token_mix(h, w_in, w_branch_gate, a_q_norm, a_k_norm, b_q_norm, b_k_norm,
              b_lambda_q1, b_lambda_k1, b_lambda_q2, b_lambda_k2, b_out_norm,
              w_o_a, w_o_b, w_out, lam_init):
    bsz, seq_len, _ = h.shape
    proj = h @ w_in
    cuts = np.cumsum([A_Q_W, A_KV_W, A_KV_W, B_QK_W, B_QK_W]).tolist()
    qa, ka, va, qb, kb, vb = jnp.split(proj, cuts, axis=-1)

    ang_ax = axial_angles(seq_len)
    qa = rotate(rms_norm(qa.reshape(bsz, seq_len, A_HEADS, HEAD_DIM), a_q_norm), ang_ax)
    ka = rotate(rms_norm(ka.reshape(bsz, seq_len, A_KV_HEADS, HEAD_DIM), a_k_norm), ang_ax)
    va = va.reshape(bsz, seq_len, A_KV_HEADS, HEAD_DIM)
    oa = gqa_attention(qa, ka, va)

    ang_p = partial_angles(seq_len)

    def prep(t, g):
        t = rms_norm(t.reshape(bsz, seq_len, 2 * B_HEADS, HEAD_DIM), g)
        t = jnp.concatenate([rotate(t[..., :ROPE_DIM], ang_p), t[..., ROPE_DIM:]], axis=-1)
        t = t.reshape(bsz, seq_len, B_HEADS, 2, HEAD_DIM)
        return t[..., 0, :], t[..., 1, :]

    q1, q2 = prep(qb, b_q_norm)
    k1, k2 = prep(kb, b_k_norm)
    vb = vb.reshape(bsz, seq_len, B_HEADS, B_V_DIM)
    f32 = jnp.float32
    lam = (jnp.exp(jnp.sum(b_lambda_q1.astype(f32) * b_lambda_k1.astype(f32)))
           - jnp.exp(jnp.sum(b_lambda_q2.astype(f32) * b_lambda_k2.astype(f32))) + lam_init)
    ob = diff_attention(q1, q2, k1, k2, vb, lam)
    ob = (rms_norm(ob, b_out_norm) * (1.0 - lam_init)).reshape(bsz, seq_len, B_V_W)

    ya = oa @ w_o_a
    yb = ob @ w_o_b
    gates = jax.nn.sigmoid(h @ w_branch_gate)
    ga, gb = gates[..., :D_MODEL], gates[..., D_MODEL:]
    return (ga * ya + gb * yb) @ w_out


def run_trunk(x, ffn1_norm, ffn1_w_gate, ffn1_w_up, ffn1_w_down, mix_norm, w_in, w_branch_gate,
              a_q_norm, a_k_norm, b_q_norm, b_k_norm, b_lambda_q1, b_lambda_k1, b_lambda_q2,
              b_lambda_k2, b_out_norm, w_o_a, w_o_b, w_out,
              ffn2_norm, ffn2_w_gate, ffn2_w_up, ffn2_w_down):
    for l in range(DEPTH):
        lam_init = 0.8 - 0.6 * math.exp(-0.3 * l)
        x = x + 0.5 * swiglu(rms_norm(x, ffn1_norm[l]), ffn1_w_gate[l], ffn1_w_up[l], ffn1_w_down[l])
        h = rms_norm(x, mix_norm[l])
        x = x + token_mix(h, w_in[l], w_branch_gate[l], a_q_norm[l], a_k_norm[l], b_q_norm[l],
                          b_k_norm[l], b_lambda_q1[l], b_lambda_k1[l], b_lambda_q2[l],
                          b_lambda_k2[l], b_out_norm[l], w_o_a[l], w_o_b[l], w_out[l], lam_init)
        x = x + 0.5 * swiglu(rms_norm(x, ffn2_norm[l]), ffn2_w_gate[l], ffn2_w_up[l], ffn2_w_down[l])
    return x


def setup_inputs(seed: int = 0) -> dict:
    key = jax.random.key(seed)
    ks = jax.random.split(key, 32)
    f32 = jnp.float32

    def w(k, fan_in, fan_out):
        return jax.random.normal(k, (DEPTH, fan_in, fan_out), f32) * fan_in ** -0.5

    def gain(k, n):
        return 1.0 + 0.02 * jax.random.normal(k, (DEPTH, n), f32)

    def small(k, n):
        return 0.1 * jax.random.normal(k, (DEPTH, n), f32)

    return {
        "x_prompt": jax.random.normal(ks[0], (BATCH, SEQ, D_MODEL), f32),
        "x_sample": jax.random.normal(ks[1], (DEC_BATCH, DEC_SEQ, D_MODEL), f32),
        "ffn1_norm": gain(ks[2], D_MODEL),
        "ffn1_w_gate": w(ks[3], D_MODEL, D_FF),
        "ffn1_w_up": w(ks[4], D_MODEL, D_FF),
        "ffn1_w_down": w(ks[5], D_FF, D_MODEL),
        "mix_norm": gain(ks[6], D_MODEL),
        "w_in": w(ks[7], D_MODEL, IN_W),
        "w_branch_gate": w(ks[8], D_MODEL, 2 * D_MODEL),
        "a_q_norm": gain(ks[9], HEAD_DIM),
        "a_k_norm": gain(ks[10], HEAD_DIM),
        "b_q_norm": gain(ks[11], HEAD_DIM),
        "b_k_norm": gain(ks[12], HEAD_DIM),
        "b_lambda_q1": small(ks[13], HEAD_DIM),
        "b_lambda_k1": small(ks[14], HEAD_DIM),
        "b_lambda_q2": small(ks[15], HEAD_DIM),
        "b_lambda_k2": small(ks[16], HEAD_DIM),
        "b_out_norm": gain(ks[17], B_V_DIM),
        "w_o_a": w(ks[18], A_OUT_W, D_MODEL),
        "w_o_b": w(ks[19], B_V_W, D_MODEL),
        "w_out": w(ks[20], D_MODEL, D_MODEL),
        "ffn2_norm": gain(ks[21], D_MODEL),
        "ffn2_w_gate": w(ks[22], D_MODEL, D_FF),
        "ffn2_w_up": w(ks[23], D_MODEL, D_FF),
        "ffn2_w_down": w(ks[24], D_FF, D_MODEL),
    }


def reference(x_prompt, x_sample, ffn1_norm, ffn1_w_gate, ffn1_w_up, ffn1_w_down, mix_norm, w_in,
              w_branch_gate, a_q_norm, a_k_norm, b_q_norm, b_k_norm, b_lambda_q1, b_lambda_k1,
              b_lambda_q2, b_lambda_k2, b_out_norm, w_o_a, w_o_b, w_out,
              ffn2_norm, ffn2_w_gate, ffn2_w_up, ffn2_w_down):
    y_prompt = run_trunk(x_prompt, ffn1_norm, ffn1_w_gate, ffn1_w_up, ffn1_w_down, mix_norm, w_in,
                         w_branch_gate, a_q_norm, a_k_norm, b_q_norm, b_k_norm, b_lambda_q1,
                         b_lambda_k1, b_lambda_q2, b_lambda_k2, b_out_norm, w_o_a, w_o_b, w_out,
                         ffn2_norm, ffn2_w_gate, ffn2_w_up, ffn2_w_down)
    y_sample = run_trunk(x_sample, ffn1_norm, ffn1_w_gate, ffn1_w_up, ffn1_w_down, mix_norm, w_in,
                         w_branch_gate, a_q_norm, a_k_norm, b_q_norm, b_k_norm, b_lambda_q1,
                         b_lambda_k1, b_lambda_q2, b_lambda_k2, b_out_norm, w_o_a, w_o_b, w_out,
                         ffn2_norm, ffn2_w_gate, ffn2_w_up, ffn2_w_down)
    return (y_prompt, y_sample)
```

```python
import math
import types
from contextlib import ExitStack

import numpy as np
import concourse.bass as bass
import concourse.mybir as mybir
from concourse.bass_utils import run_bass_kernel_spmd

F32 = mybir.dt.float32
BF16 = mybir.dt.bfloat16
I32 = mybir.dt.int32
U8 = mybir.dt.uint8
AF = mybir.ActivationFunctionType
ALU = mybir.AluOpType
AX = mybir.AxisListType

D = 1024
DFF = 2816
NF = DFF // 128
KD = D // 128
INW = 2304
NH = 26
QKW = NH * 64
VW = 640
EPS = 1e-6
SCALE = 0.125
LAM_INIT = 0.8 - 0.6 * math.exp(-0.3 * 0)

ENGS = ("sync", "scalar", "vector", "gpsimd", "tensor")
DMAQ = ("sync", "scalar", "gpsimd")


def _snap(fn):
    if fn is None or getattr(fn, "__closure__", None) is None:
        return fn
    cells = []
    for c in fn.__closure__:
        try:
            cells.append(types.CellType(c.cell_contents))
        except ValueError:
            cells.append(c)
    g = types.FunctionType(fn.__code__, fn.__globals__, fn.__name__, fn.__defaults__, tuple(cells))
    g.__kwdefaults__ = fn.__kwdefaults__
    return g


class Op:
    __slots__ = ("eng", "emit", "deps", "dma", "sigsem", "sigval", "signal")

    def __init__(self, eng, emit, dma):
        self.eng = eng
        self.emit = _snap(emit)
        self.dma = dma
        self.deps = set()
        self.sigsem = None
        self.sigval = 0
        self.signal = False


class Sched:
    def __init__(self, nc, stack, n_dma_sems=12):
        self.nc = nc
        self.ops = {e: [] for e in ENGS}
        self.last_w = {}
        self.readers = {}
        self.csem = {e: stack.enter_context(nc.semaphore("c_" + e)) for e in ENGS if e != "sync"}
        self.dsem = {q: [stack.enter_context(nc.semaphore("d_%s_%d" % (q, i))) for i in range(n_dma_sems)]
                     for q in DMAQ}
        self.dcount = {q: 0 for q in DMAQ}
        self.dlast = {q: {} for q in DMAQ}
        self.since_barrier_dma = []
        self.last_op = {}

    def add(self, eng, emit, reads=(), writes=(), dma=False):
        op = Op(eng, emit, dma)
        deps = op.deps
        for r in reads:
            w = self.last_w.get(r)
            if w is not None:
                deps.add(w)
        for w_ in writes:
            w = self.last_w.get(w_)
            if w is not None:
                deps.add(w)
            rd = self.readers.get(w_)
            if rd:
                deps.update(rd[0].values())
                deps.update(rd[1])
        for r in reads:
            rd = self.readers.get(r)
            if rd is None:
                rd = self.readers[r] = ({}, [])
            if dma:
                rd[1].append(op)
            else:
                rd[0][eng] = op
        for w_ in writes:
            self.last_w[w_] = op
            self.readers[w_] = ({}, [])
        deps.discard(op)
        if dma:
            n = self.dcount[eng]
            sems = self.dsem[eng]
            i = n % len(sems)
            op.sigsem = sems[i]
            op.sigval = 16 * (n // len(sems) + 1)
            prev = self.dlast[eng].get(i)
            if prev is not None:
                deps.add(prev)
            self.dlast[eng][i] = op
            self.dcount[eng] = n + 1
            self.since_barrier_dma.append(op)
        self.ops[eng].append(op)
        if not dma:
            self.last_op[eng] = op
        return op

    def barrier(self):
        deps = set(self.since_barrier_dma)
        for o in self.last_op.values():
            if o is not None and o.emit is not None:
                deps.add(o)
        self.since_barrier_dma = []
        for e in ENGS:
            op = Op(e, None, False)
            op.deps = set(deps)
            self.ops[e].append(op)
        self.last_w = {}
        self.readers = {}

    def finalize(self):
        for e in ENGS:
            for op in self.ops[e]:
                for d in op.deps:
                    if d.dma:
                        continue
                    if d.eng == "tensor" and op.eng == "tensor" and not op.dma:
                        continue
                    d.signal = True
        for e in ENGS:
            if e == "sync":
                continue
            c = 0
            for op in self.ops[e]:
                if op.dma or not op.signal:
                    continue
                assert op.emit is not None
                c += 1
                op.sigsem = self.csem[e]
                op.sigval = c

    def emit_engine(self, ename, eobj):
        waited = {}
        for op in self.ops[ename]:
            needs = {}
            for d in op.deps:
                if (not d.dma) and d.eng == "tensor" and ename == "tensor" and not op.dma:
                    continue
                k = d.sigsem
                if needs.get(k, 0) < d.sigval:
                    needs[k] = d.sigval
            for s, v in needs.items():
                if waited.get(s, 0) < v:
                    eobj.wait_ge(s, v)
                    waited[s] = v
            if op.emit is not None:
                ins = op.emit(eobj)
                if op.dma:
                    ins.then_inc(op.sigsem, 16)
                elif op.signal:
                    ins.then_inc(op.sigsem, 1)

    def run(self, block):
        self.finalize()
        s = self

        @block.sync
        def _(e):
            s.emit_engine("sync", e)

        @block.scalar
        def _(e):
            s.emit_engine("scalar", e)

        @block.vector
        def _(e):
            s.emit_engine("vector", e)

        @block.gpsimd
        def _(e):
            s.emit_engine("gpsimd", e)

        @block.tensor
        def _(e):
            s.emit_engine("tensor", e)


class Arena:
    def __init__(self, t, size):
        self.t = t
        self.size = size
        self.cur = 0

    def alloc(self, shape, dt):
        esz = 4 if dt in (F32, I32) else 2
        n = int(np.prod(shape[1:])) * esz
        off = self.cur
        self.cur = off + (n + 63) // 64 * 64
        assert self.cur <= self.size, ("SBUF arena overflow", self.cur, self.size)
        ap = self.t[:, off:off + n].bitcast(dt)
        if len(shape) == 3:
            ap = ap.rearrange("p (a b) -> p a b", a=shape[1], b=shape[2])
        elif len(shape) == 4:
            ap = ap.rearrange("p (a b c) -> p a b c", a=shape[1], b=shape[2], c=shape[3])
        return ap


def build(TS, TPK, TPQ, debug=False, stop_after=0):
    nc = bass.Bass("TRN2", target_bir_lowering=False)

    def din(name, shape, dt=F32):
        return nc.dram_tensor(name, list(shape), dt, kind="ExternalInput").ap()

    def dout(name, shape, dt=F32):
        return nc.dram_tensor(name, list(shape), dt, kind="ExternalOutput").ap()

    def dscr(name, shape, dt):
        return nc.dram_tensor(name, list(shape), dt, kind="ExternalOutput" if debug else "Internal").ap()

    xs_d = din("xs", [TS, D])
    xp_d = din("xp", [TPK, D])
    tabs_d = din("tab_s", [TS, 80])
    tabp_d = din("tab_p", [TPK, 80])
    W = {}
    for j in (1, 2):
        W["g%d" % j] = din("ffn%d_w_gate" % j, [D, DFF])
        W["u%d" % j] = din("ffn%d_w_up" % j, [D, DFF])
        W["d%d" % j] = din("ffn%d_w_down" % j, [DFF, D])
        W["n%d" % j] = din("ffn%d_norm" % j, [1, D])
    W["nm"] = din("mix_norm", [1, D])
    W["in"] = din("w_in", [D, INW])
    W["bg"] = din("w_branch_gate", [D, 2 * D])
    W["oa"] = din("w_o_a", [512, D])
    W["ob"] = din("w_o_b", [512, D])
    W["out"] = din("w_out", [D, D])
    for nm in ("a_q_norm", "a_k_norm", "b_q_norm", "b_k_norm", "b_lambda_q1", "b_lambda_k1",
               "b_lambda_q2", "b_lambda_k2"):
        W[nm] = din(nm, [1, 64])
    W["b_out_norm"] = din("b_out_norm", [1, 128])
    ys_d = dout("ys", [TS, D])
    yp_d = dout("yp", [TPQ, D])

    wgu_d = {j: dscr("wgu%d" % j, [NF, 128, 2 * KD * 128], BF16) for j in (1, 2)}
    wd_d = {j: dscr("wd%d" % j, [2, NF, 128, 512], BF16) for j in (1, 2)}
    win_d = dscr("win_bf", [128, KD, INW], BF16)
    wc_d = dscr("wc_bf", [8, 128, 24, 128], BF16)
    wout_d = dscr("wout_bf", [128, KD, D], BF16)

    seqs = []
    for nm, xd, tabd, T, TQ, yd in (("s", xs_d, tabs_d, TS, TS, ys_d), ("p", xp_d, tabp_d, TPK, TPQ, yp_d)):
        seqs.append(dict(
            nm=nm, x=xd, tab=tabd, T=T, TQ=TQ, y=yd,
            x1=dscr("x1_" + nm, [TQ, D], F32),
            KTA=dscr("kta_" + nm, [128, T], BF16), KTB=dscr("ktb_" + nm, [4, 128, T], BF16),
            VA=dscr("va_" + nm, [T, 128], BF16), VB=dscr("vb_" + nm, [T, 512], BF16),
            QTA=dscr("qta_" + nm, [4, 128, TQ], BF16), QTB=dscr("qtb_" + nm, [4, 128, TQ], BF16),
            OAT=dscr("oat_" + nm, [512, TQ], BF16), OBT=dscr("obt_" + nm, [512, TQ], BF16)))

    ARENA_BYTES = 207 * 1024
    with ExitStack() as st:
        arena_t = st.enter_context(nc.sbuf_tensor("arena", [128, ARENA_BYTES], U8))
        pp = [st.enter_context(nc.psum_tensor("pp%d" % i, [128, 1024], F32)) for i in range(4)]
        block = st.enter_context(nc.Block())
        S = Sched(nc, st)
        A = Arena(arena_t, ARENA_BYTES)

        def bank(i):
            return pp[i // 2][:, (i % 2) * 512:(i % 2 + 1) * 512]

        def bank_bf(i):
            return bank(i).bitcast(BF16)

        def PS(i):
            return ("ps", i)

        _rr = [0]

        def rr_eng(choices=("vector", "scalar", "gpsimd")):
            _rr[0] += 1
            return choices[_rr[0] % len(choices)]

        def copy_op(eng, out, in_, reads, writes):
            if eng == "scalar":
                S.add("scalar", lambda e: e.copy(out=out, in_=in_), reads=reads, writes=writes)
            else:
                S.add(eng, lambda e: e.tensor_copy(out=out, in_=in_), reads=reads, writes=writes)

        ident = A.alloc([128, 128], BF16)
        ones_bf = A.alloc([128, 128], BF16)
        ones_f = A.alloc([128, 128], F32)
        g1 = A.alloc([128, D], F32)
        gm = A.alloc([128, D], F32)
        g2 = A.alloc([128, D], F32)
        gall = A.alloc([128, NH, 64], F32)
        lamv = A.alloc([128, 4, 64], F32)
        neglam = A.alloc([128, 1], F32)
        gout = A.alloc([128, 1], F32)
        sm1 = A.alloc([128, 8], F32)
        persist_mark = A.cur

        S.add("gpsimd", lambda e: e.memset(ident, 0.0), writes=["ident"])
        S.add("gpsimd", lambda e: e.affine_select(
            out=ident, in_=ident, compare_op=ALU.not_equal, fill=1.0, base=0, pattern=[[-1, 128]],
            channel_multiplier=1), reads=["ident"], writes=["ident"])
        S.add("gpsimd", lambda e: e.memset(ones_bf, 1.0), writes=["ones_bf"])
        S.add("gpsimd", lambda e: e.memset(ones_f, 1.0), writes=["ones_f"])
        for dst, src, key in ((g1, W["n1"], "g1"), (gm, W["nm"], "gm"), (g2, W["n2"], "g2")):
            S.add("sync", lambda e, dst=dst, src=src: e.dma_start(out=dst, in_=src.broadcast_to([128, D])),
                  writes=[key], dma=True)
        for h0, h1, nm in ((0, 8, "a_q_norm"), (8, 10, "a_k_norm"), (10, 18, "b_k_norm"), (18, 26, "b_q_norm")):
            for h in range(h0, h1):
                S.add("sync", lambda e, h=h, nm=nm: e.dma_start(out=gall[:, h, :], in_=W[nm].broadcast_to([128, 64])),
                      writes=[("gall", h)], dma=True)
        gall_keys = [("gall", h) for h in range(NH)]
        for i, nm in enumerate(("b_lambda_q1", "b_lambda_k1", "b_lambda_q2", "b_lambda_k2")):
            S.add("sync", lambda e, i=i, nm=nm: e.dma_start(out=lamv[:, i, :], in_=W[nm].broadcast_to([128, 64])),
                  writes=[("lamv", i)], dma=True)
        S.add("sync", lambda e: e.dma_start(out=gout, in_=W["b_out_norm"].rearrange("o n -> n o")),
              writes=["gout"], dma=True)
        S.add("vector", lambda e: e.tensor_tensor(out=lamv[:, 0, :], in0=lamv[:, 0, :], in1=lamv[:, 1, :], op=ALU.mult),
              reads=[("lamv", 0), ("lamv", 1)], writes=[("lamv", 0)])
        S.add("vector", lambda e: e.tensor_tensor(out=lamv[:, 2, :], in0=lamv[:, 2, :], in1=lamv[:, 3, :], op=ALU.mult),
              reads=[("lamv", 2), ("lamv", 3)], writes=[("lamv", 2)])
        S.add("vector", lambda e: e.tensor_reduce(out=sm1[:, 0:1], in_=lamv[:, 0, :], axis=AX.X, op=ALU.add),
              reads=[("lamv", 0)], writes=["sm1a"])
        S.add("vector", lambda e: e.tensor_reduce(out=sm1[:, 1:2], in_=lamv[:, 2, :], axis=AX.X, op=ALU.add),
              reads=[("lamv", 2)], writes=["sm1b"])
        S.add("scalar", lambda e: e.activation(out=sm1[:, 2:4], in_=sm1[:, 0:2], func=AF.Exp),
              reads=["sm1a", "sm1b"], writes=["sm1e"])
        S.add("vector", lambda e: e.tensor_tensor(out=sm1[:, 4:5], in0=sm1[:, 3:4], in1=sm1[:, 2:3], op=ALU.subtract),
              reads=["sm1e"], writes=["sm1d"])
        S.add("vector", lambda e: e.tensor_scalar(out=neglam, in0=sm1[:, 4:5], scalar1=-LAM_INIT, scalar2=None, op0=ALU.add),
              reads=["sm1d"], writes=["neglam"])
        S.add("vector", lambda e: e.tensor_scalar(out=gout, in0=gout, scalar1=1.0 - LAM_INIT, scalar2=None, op0=ALU.mult),
              reads=["gout"], writes=["gout"])

        stage = [A.alloc([128, 3072], F32) for _ in range(2)]
        stage_bf = [A.alloc([128, 3072], BF16) for _ in range(2)]
        Tbig = A.alloc([128, NF, 2 * KD * 128], BF16)
        cnt = [0]

        def load_cast(src_ap, n, C, dst_bf=None, dst_key=None):
            slot = cnt[0] % 2
            cnt[0] += 1
            sv = stage[slot][:, 0:n * C].rearrange("p (a c) -> p a c", a=n, c=C)
            S.add("sync", lambda e: e.dma_start(out=sv, in_=src_ap.rearrange("(a p) c -> p a c", p=128)),
                  writes=[("stg", slot)], dma=True)
            if dst_bf is None:
                bv = stage_bf[slot][:, 0:n * C].rearrange("p (a c) -> p a c", a=n, c=C)
                copy_op(rr_eng(), bv, sv, [("stg", slot)], [("stb", slot)])
                return bv, ("stb", slot)
            copy_op(rr_eng(), dst_bf, sv[:, 0, :].rearrange("p (f c) -> p f c", f=NF, c=128), [("stg", slot)], [dst_key])
            return dst_bf, dst_key

        for j in (1, 2):
            Tv = Tbig.rearrange("p f (g k c) -> p f g k c", g=2, k=KD, c=128)
            for gi, key in enumerate(("g%d" % j, "u%d" % j)):
                for k in range(KD):
                    load_cast(W[key][k * 128:(k + 1) * 128, :], 1, DFF,
                              dst_bf=Tv[:, :, gi, k, :], dst_key=("Tbig", gi, k))
            for f0 in range(0, NF, 6):
                f1 = min(NF, f0 + 6)
                S.add("gpsimd", lambda e, j=j, f0=f0, f1=f1: e.dma_start(
                    out=wgu_d[j][f0:f1].rearrange("f p x -> p f x"), in_=Tbig[:, f0:f1, :]),
                    reads=[("Tbig", gi_, k_) for gi_ in range(2) for k_ in range(KD)], writes=[("wgu", j, f0)], dma=True)
            for f0 in range(0, NF, 2):
                bv, key = load_cast(W["d%d" % j][f0 * 128:(f0 + 2) * 128, :], 2, D)
                for half in range(2):
                    S.add("gpsimd", lambda e, j=j, f0=f0, half=half, bv=bv: e.dma_start(
                        out=wd_d[j][half, f0:f0 + 2].rearrange("f p x -> p f x"),
                        in_=bv[:, :, half * 512:(half + 1) * 512]),
                        reads=[key], writes=[("wd", j)], dma=True)
        for k in range(KD):
            bv, key = load_cast(W["in"][k * 128:(k + 1) * 128, :], 1, INW)
            S.add("gpsimd", lambda e, k=k, bv=bv: e.dma_start(out=win_d[:, k, :], in_=bv[:, 0, :]),
                  reads=[key], writes=["win_d"], dma=True)
        for k in range(KD):
            bv, key = load_cast(W["bg"][k * 128:(k + 1) * 128, :], 1, 2 * D)
            for ab in range(2):
                S.add("gpsimd", lambda e, k=k, ab=ab, bv=bv: e.dma_start(
                    out=wc_d[:, :, ab * 8 + k, :].rearrange("c p j -> p c j"),
                    in_=bv[:, 0, ab * D:(ab + 1) * D].rearrange("p (c j) -> p c j", c=8, j=128)),
                    reads=[key], writes=["wc_d"], dma=True)
        for oi, key_w in enumerate(("oa", "ob")):
            for k0 in range(0, 4, 2):
                bv, key = load_cast(W[key_w][k0 * 128:(k0 + 2) * 128, :], 2, D)
                for kk in range(2):
                    S.add("gpsimd", lambda e, oi=oi, k=k0 + kk, kk=kk, bv=bv: e.dma_start(
                        out=wc_d[:, :, 16 + oi * 4 + k, :].rearrange("c p j -> p c j"),
                        in_=bv[:, kk, :].rearrange("p (c j) -> p c j", c=8, j=128)),
                        reads=[key], writes=["wc_d"], dma=True)
        for k0 in range(0, KD, 2):
            bv, key = load_cast(W["out"][k0 * 128:(k0 + 2) * 128, :], 2, D)
            S.add("gpsimd", lambda e, k0=k0, bv=bv: e.dma_start(out=wout_d[:, k0:k0 + 2, :], in_=bv),
                  reads=[key], writes=["wout_d"], dma=True)
        S.barrier()

        def rsqrt_ops(eng, v, y, t, vk, yk, tk):
            S.add(eng, lambda e: e.tensor_scalar(out=y.bitcast(I32), in0=v.bitcast(I32), scalar1=1, scalar2=None,
                                                 op0=ALU.arith_shift_right), reads=[vk], writes=[yk])
            S.add(eng, lambda e: e.tensor_scalar(out=y.bitcast(I32), in0=y.bitcast(I32), scalar1=-1,
                                                 scalar2=0x5f3759df, op0=ALU.mult, op1=ALU.add), reads=[yk], writes=[yk])
            for _ in range(3):
                S.add(eng, lambda e: e.tensor_tensor(out=t, in0=y, in1=y, op=ALU.mult), reads=[yk], writes=[tk])
                S.add(eng, lambda e: e.scalar_tensor_tensor(out=t, in0=t, scalar=-0.5, in1=v, op0=ALU.mult, op1=ALU.mult),
                      reads=[tk, vk], writes=[tk])
                S.add(eng, lambda e: e.scalar_tensor_tensor(out=y, in0=t, scalar=1.5, in1=y, op0=ALU.add, op1=ALU.mult),
                      reads=[tk, yk], writes=[yk])

        def rsqrt_steps(eng, v, y, t, vk, yk, tk):
            st_ = []
            st_.append(lambda: S.add(eng, lambda e: e.tensor_scalar(out=y.bitcast(I32), in0=v.bitcast(I32), scalar1=1, scalar2=None,
                                                                    op0=ALU.arith_shift_right), reads=[vk], writes=[yk]))
            st_.append(lambda: S.add(eng, lambda e: e.tensor_scalar(out=y.bitcast(I32), in0=y.bitcast(I32), scalar1=-1,
                                                                    scalar2=0x5f3759df, op0=ALU.mult, op1=ALU.add), reads=[yk], writes=[yk]))
            for _ in range(3):
                st_.append(lambda: S.add(eng, lambda e: e.tensor_tensor(out=t, in0=y, in1=y, op=ALU.mult), reads=[yk], writes=[tk]))
                st_.append(lambda: S.add(eng, lambda e: e.scalar_tensor_tensor(out=t, in0=t, scalar=-0.5, in1=v, op0=ALU.mult, op1=ALU.mult),
                                         reads=[tk, vk], writes=[tk]))
                st_.append(lambda: S.add(eng, lambda e: e.scalar_tensor_tensor(out=y, in0=t, scalar=1.5, in1=y, op0=ALU.add, op1=ALU.mult),
                                         reads=[tk, yk], writes=[yk]))
            return st_

        class Stream:
            def __init__(self, name, slots, plan):
                self.name = name
                self.slots = slots
                self.plan = plan
                self.issued = 0

            def _issue_to(self, n):
                while self.issued < min(n, len(self.plan)):
                    i = self.issued
                    sl = i % len(self.slots)
                    S.add("sync", self.plan[i](self.slots[sl]), writes=[(self.name, sl)], dma=True)
                    self.issued += 1

            def get(self, i):
                self._issue_to(i + len(self.slots))
                sl = i % len(self.slots)
                return self.slots[sl], (self.name, sl)

            def prefetch(self, i):
                self._issue_to(i + len(self.slots))

        def norm_pre(xt, xkeys, gain, gkey, bufs):
            junk, ss4, vv4, rstd4, tt4, hb = bufs
            for t in range(4):
                jo = junk if junk is not None else hb[:, t, :]
                jw = ["junk"] if junk is not None else [("hb", t)]
                S.add("scalar", lambda e, t=t, jo=jo: e.activation(out=jo, in_=xt[:, t, :], func=AF.Square,
                                                                   accum_out=ss4[:, t:t + 1]),
                      reads=[xkeys[t]], writes=jw + ["ss4"])
            S.add("vector", lambda e: e.tensor_scalar(out=vv4, in0=ss4, scalar1=1.0 / D, scalar2=EPS, op0=ALU.mult,
                                                      op1=ALU.add), reads=["ss4"], writes=["vv4"])
            rsqrt_ops("vector", vv4, rstd4, tt4, "vv4", "rstd4", "tt4")
            for t in range(4):
                S.add("vector", lambda e, t=t: e.scalar_tensor_tensor(
                    out=hb[:, t, :], in0=xt[:, t, :], scalar=rstd4[:, t:t + 1], in1=gain, op0=ALU.mult, op1=ALU.mult),
                    reads=[xkeys[t], "rstd4", gkey], writes=[("hb", t)])

        def norm_tr(hTb, hkey, bufs, banks=(0, 1)):
            hb = bufs[5]
            for k in range(KD):
                bi = banks[k % 2]

                def emit(e, k=k, bi=bi):
                    for t in range(4):
                        ins = e.transpose(out=bank_bf(bi)[:, t * 128:(t + 1) * 128],
                                          in_=hb[:, t, k * 128:(k + 1) * 128], identity=ident)
                    return ins
                S.add("tensor", emit, reads=[("hb", t) for t in range(4)] + ["ident"], writes=[PS(bi)])
                copy_op("scalar", hTb[:, k, :], bank_bf(bi)[:, 0:512], [PS(bi)], [(hkey, k)])

        def norm_T(xt, xkeys, gain, gkey, hTb, hkey, bufs):
            norm_pre(xt, xkeys, gain, gkey, bufs)
            norm_tr(hTb, hkey, bufs)

        def ffn_block(xt, xkeys, hTb, hkey, j, wgu_s, wd_s, wbase, AT, sg, hook=None, hook_f=8, hooks=None):
            for f in range(NF):
                wsl, wkey = wgu_s.get(wbase[0] + f)
                gb_, ub_ = (f % 2), 2 + (f % 2)

                def emit_g(e, wsl=wsl, b=gb_):
                    for k in range(KD):
                        ins = e.matmul(bank(b), lhsT=wsl[:, 0, k, :], rhs=hTb[:, k, :], start=(k == 0), stop=(k == KD - 1))
                    return ins

                def emit_u(e, wsl=wsl, b=ub_):
                    for k in range(KD):
                        ins = e.matmul(bank(b), lhsT=wsl[:, 1, k, :], rhs=hTb[:, k, :], start=(k == 0), stop=(k == KD - 1))
                    return ins
                hk = [(hkey, k) for k in range(KD)]
                S.add("tensor", emit_g, reads=hk + [wkey], writes=[PS(gb_)])
                S.add("tensor", emit_u, reads=hk + [wkey], writes=[PS(ub_)])
                sgl = sg[f % 2]
                S.add("scalar", lambda e, sgl=sgl, b=gb_: e.activation(out=sgl, in_=bank(b), func=AF.Silu),
                      reads=[PS(gb_)], writes=[("sg", f % 2)])
                S.add("vector", lambda e, sgl=sgl, b=ub_, f=f: e.tensor_tensor(out=AT[:, f, :], in0=sgl, in1=bank(b), op=ALU.mult),
                      reads=[("sg", f % 2), PS(ub_)], writes=[("AT", f)])
                if hook is not None and f == hook_f:
                    hook()
                if hooks is not None and f in hooks:
                    hooks[f]()
            wbase[0] += NF
            for half in range(2):
                for f0 in range(0, NF, 2):
                    dsl, dkey = wd_s.get(wbase[1] + (half * NF + f0) // 2)

                    def emit_d(e, dsl=dsl, f0=f0):
                        for ff in range(2):
                            f = f0 + ff
                            for t in range(4):
                                ins = e.matmul(bank(4 + t), lhsT=AT[:, f, t * 128:(t + 1) * 128], rhs=dsl[:, ff, :],
                                               start=(f == 0), stop=(f == NF - 1))
                        return ins
                    S.add("tensor", emit_d, reads=[("AT", f0), ("AT", f0 + 1), dkey], writes=[PS(4 + t) for t in range(4)])
                for t in range(4):
                    sl = xt[:, t, half * 512:(half + 1) * 512]
                    S.add("vector", lambda e, sl=sl, t=t: e.scalar_tensor_tensor(
                        out=sl, in0=bank(4 + t), scalar=0.5, in1=sl, op0=ALU.mult, op1=ALU.add),
                        reads=[PS(4 + t), xkeys[t]], writes=[xkeys[t]])
            wbase[1] += NF

        def wgu_plan(j, nblocks):
            plan = []
            for _ in range(nblocks):
                for f in range(NF):
                    plan.append(lambda dst, f=f: (lambda e: e.dma_start(
                        out=dst, in_=wgu_d[j][f].rearrange("p (g k c) -> p g k c", g=2, k=KD, c=128))))
            return plan

        def wd_plan(j, nblocks):
            plan = []
            for _ in range(nblocks):
                for half in range(2):
                    for f0 in range(0, NF, 2):
                        plan.append(lambda dst, half=half, f0=f0: (lambda e: e.dma_start(
                            out=dst, in_=wd_d[j][half, f0:f0 + 2].rearrange("f p x -> p f x"))))
            return plan

        NSTOP = stop_after
        for si, sq in enumerate(seqs):
            T, TQ = sq["T"], sq["TQ"]
            NBLK = T // 512
            NQB = TQ // 512
            NKB = T // 128
            A.cur = persist_mark
            xin = A.alloc([128, 4, D], F32)
            hb = A.alloc([128, 4, D], BF16)
            hTf = A.alloc([128, KD, 512], BF16)
            hTm = A.alloc([128, KD, 512], BF16)
            AT = A.alloc([128, NF, 512], BF16)
            wgu_slots = [A.alloc([128, 2, KD, 128], BF16) for _ in range(4)]
            wd_slots = [A.alloc([128, 2, 512], BF16) for _ in range(3)]
            win_sb = A.alloc([128, KD, INW], BF16)
            xq2 = [A.alloc([128, NH, 64], F32) for _ in range(2)]
            tq = A.alloc([128, NH, 64], F32)
            qkn = A.alloc([128, 4, QKW], BF16)
            qkT = A.alloc([128, 13, 512], BF16)
            vst = A.alloc([128, 4, VW], BF16)
            tabt = [A.alloc([128, 4, 80], F32) for _ in range(3)]
            sg = [A.alloc([128, 512], F32) for _ in range(2)]
            ss4 = A.alloc([128, 4], F32)
            vv4 = A.alloc([128, 4], F32)
            rstd4 = A.alloc([128, 4], F32)
            tt4 = A.alloc([128, 4], F32)
            ssq = A.alloc([128, NH], F32)
            vvq = A.alloc([128, NH], F32)
            rq = A.alloc([128, NH], F32)
            ttq = A.alloc([128, NH], F32)
            rA = [A.alloc([128, 10, 32], F32) for _ in range(4)]
            rB = [A.alloc([128, 16, 8], F32) for _ in range(4)]
            nbufs = (None, ss4, vv4, rstd4, tt4, hb)
            xkeys = [("xin", t) for t in range(4)]

            wgu_s = Stream("wgu", wgu_slots, wgu_plan(1, NBLK))
            wd_s = Stream("wd", wd_slots, wd_plan(1, NBLK))
            wbase = [0, 0]
            own_chunks = [(0, 512), (512, 512), (1024, 512), (1536, 512), (2048, 256)]
            rest_chunks = [(512, 512), (1024, 128), (1664, 512), (2176, 128)]

            def load_x(b, sq=sq, xin=xin):
                S.add("sync", lambda e: e.dma_start(
                    out=xin, in_=sq["x"][b * 512:(b + 1) * 512, :].rearrange("(t p) d -> p t d", p=128)),
                    writes=xkeys, dma=True)

            def load_tab(b, sq=sq, tabt=tabt):
                S.add("sync", lambda e: e.dma_start(
                    out=tabt[b % 3], in_=sq["tab"][b * 512:(b + 1) * 512, :].rearrange("(t p) d -> p t d", p=128)),
                    writes=[("tab", b % 3)], dma=True)

            load_x(0)
            load_tab(0)
            for k0 in range(0, KD, 4):
                S.add("sync", lambda e, k0=k0: e.dma_start(out=win_sb[:, k0:k0 + 4, :], in_=win_d[:, k0:k0 + 4, :]),
                      writes=[("win_sb", k0)], dma=True)
            win_keys = [("win_sb", 0), ("win_sb", 4)]
            bank_ctr = [0]
            t3_pending = [None]

            def run_t3():
                if t3_pending[0] is not None:
                    f = t3_pending[0]
                    t3_pending[0] = None
                    f()

            def proj_mm(b, t, chunks):
                xq = xq2[t % 2]
                xqf = xq.rearrange("p h d -> p (h d)")
                for ci, (c0, ncol) in enumerate(chunks):
                    bi = 2 + bank_ctr[0] % 6
                    bank_ctr[0] += 1

                    def emit_p(e, c0=c0, ncol=ncol, bi=bi):
                        for k in range(KD):
                            ins = e.matmul(bank(bi)[:, 0:ncol], lhsT=hTm[:, k, t * 128:(t + 1) * 128],
                                           rhs=win_sb[:, k, c0:c0 + ncol], start=(k == 0), stop=(k == KD - 1))
                        return ins
                    S.add("tensor", emit_p, reads=[("hTm", k) for k in range(KD)] + win_keys, writes=[PS(bi)])
                    a0, a1 = c0, min(c0 + ncol, QKW)
                    b0, b1 = max(c0, QKW), c0 + ncol
                    if a1 > a0:
                        S.add("scalar", lambda e, bi=bi, a0=a0, a1=a1, c0=c0: e.copy(
                            out=xqf[:, a0:a1], in_=bank(bi)[:, a0 - c0:a1 - c0]),
                            reads=[PS(bi)], writes=[("xq", t % 2, ci)])
                    if b1 > b0:
                        S.add("scalar", lambda e, bi=bi, b0=b0, b1=b1, c0=c0: e.copy(
                            out=vst[:, t, b0 - QKW:b1 - QKW], in_=bank(bi)[:, b0 - c0:b1 - c0]),
                            reads=[PS(bi)], writes=[("vst", t)])

            def qk_chain(b, t, chunks, h0, h1, tsl, tkey):
                xq = xq2[t % 2]
                xqk = [("xq", t % 2, ci) for ci in range(len(chunks))]
                H = h1 - h0
                X = xq[:, h0:h1, :]
                Tq = tq[:, h0:h1, :]
                S.add("scalar", lambda e: e.activation(out=Tq, in_=X, func=AF.Square), reads=xqk, writes=["tq"])
                S.add("vector", lambda e: e.tensor_reduce(out=ssq[:, h0:h1], in_=Tq, axis=AX.X, op=ALU.add),
                      reads=["tq"], writes=["ssq"])
                S.add("vector", lambda e: e.tensor_scalar(out=vvq[:, h0:h1], in0=ssq[:, h0:h1], scalar1=1.0 / 64,
                                                          scalar2=EPS, op0=ALU.mult, op1=ALU.add),
                      reads=["ssq"], writes=["vvq"])
                rsqrt_ops("vector", vvq[:, h0:h1], rq[:, h0:h1], ttq[:, h0:h1], "vvq", "rq", "ttq")
                S.add("gpsimd", lambda e: e.tensor_tensor(out=X, in0=X, in1=gall[:, h0:h1, :], op=ALU.mult),
                      reads=xqk + gall_keys + ["tq"], writes=xqk)
                S.add("vector", lambda e: e.tensor_tensor(
                    out=X, in0=X, in1=rq[:, h0:h1].unsqueeze(2).broadcast_to([128, H, 64]), op=ALU.mult),
                    reads=xqk + ["rq"], writes=xqk)
                qv = qkn[:, t, :].rearrange("p (h d) -> p h d", h=NH, d=64)
                for (r0, r1, m, cofs, rbuf, rkey) in ((h0, 10, 32, 0, rA, "rA"), (10, h1, 8, 64, rB, "rB")):
                    Hr = r1 - r0
                    x1 = xq[:, r0:r1, 0:m]
                    x2 = xq[:, r0:r1, m:2 * m]
                    cs = tsl[:, t, cofs:cofs + m].unsqueeze(1).broadcast_to([128, Hr, m])
                    sn = tsl[:, t, cofs + m:cofs + 2 * m].unsqueeze(1).broadcast_to([128, Hr, m])
                    bA, bB, bC, bD = [rb[:, 0:Hr, :] for rb in rbuf]
                    S.add("vector", lambda e, bA=bA, x1=x1, cs=cs: e.tensor_tensor(out=bA, in0=x1, in1=cs, op=ALU.mult),
                          reads=xqk + [tkey], writes=[(rkey, 0)])
                    S.add("vector", lambda e, bB=bB, x2=x2, sn=sn: e.tensor_tensor(out=bB, in0=x2, in1=sn, op=ALU.mult),
                          reads=xqk + [tkey], writes=[(rkey, 1)])
                    S.add("vector", lambda e, bA=bA, bB=bB, r0=r0, r1=r1, m=m: e.tensor_tensor(
                        out=qv[:, r0:r1, 0:m], in0=bA, in1=bB, op=ALU.subtract),
                        reads=[(rkey, 0), (rkey, 1)], writes=[("qkn", t, rkey, 0)])
                    S.add("gpsimd", lambda e, bC=bC, x2=x2, cs=cs: e.tensor_tensor(out=bC, in0=x2, in1=cs, op=ALU.mult),
                          reads=xqk + [tkey], writes=[(rkey, 2)])
                    S.add("gpsimd", lambda e, bD=bD, x1=x1, sn=sn: e.tensor_tensor(out=bD, in0=x1, in1=sn, op=ALU.mult),
                          reads=xqk + [tkey], writes=[(rkey, 3)])
                    S.add("gpsimd", lambda e, bC=bC, bD=bD, r0=r0, r1=r1, m=m: e.tensor_tensor(
                        out=qv[:, r0:r1, m:2 * m], in0=bC, in1=bD, op=ALU.add),
                        reads=[(rkey, 2), (rkey, 3)], writes=[("qkn", t, rkey, 1)])
                    if m == 8:
                        S.add("gpsimd", lambda e, r0=r0, r1=r1: e.tensor_copy(out=qv[:, r0:r1, 16:64], in_=xq[:, r0:r1, 16:64]),
                              reads=xqk, writes=[("qkn", t, rkey, 2)])

            def make_t3(b, own, sq=sq):
                def t3():
                    qkn_keys = [("qkn", t, rk, i) for t in range(4) for rk, n in (("rA", 2), ("rB", 3)) for i in range(n)]
                    groups = list(range(13)) if own else list(range(4, 9))
                    for r in groups:
                        bi = 4 + r % 2

                        def emit_t(e, r=r, bi=bi):
                            for t in range(4):
                                ins = e.transpose(out=bank_bf(bi)[:, t * 128:(t + 1) * 128],
                                                  in_=qkn[:, t, r * 128:(r + 1) * 128], identity=ident)
                            return ins
                        S.add("tensor", emit_t, reads=qkn_keys + ["ident"], writes=[PS(bi)])
                        copy_op("scalar" if r % 2 == 0 else "vector", qkT[:, r, :], bank_bf(bi)[:, 0:512],
                                [PS(bi)], [("qkT", r)])
                    blk = slice(b * 512, (b + 1) * 512)
                    if own:
                        S.add("gpsimd", lambda e: e.dma_start(
                            out=sq["QTA"][:, :, blk].rearrange("g r t -> r g t"), in_=qkT[:, 0:4, :]),
                            reads=[("qkT", r) for r in range(0, 4)], writes=[("QTA", b)], dma=True)
                        S.add("gpsimd", lambda e: e.dma_start(
                            out=sq["QTB"][:, :, blk].rearrange("g r t -> r g t"), in_=qkT[:, 9:13, :]),
                            reads=[("qkT", r) for r in range(9, 13)], writes=[("QTB", b)], dma=True)
                    S.add("gpsimd", lambda e: e.dma_start(out=sq["KTA"][:, blk], in_=qkT[:, 4, :]),
                          reads=[("qkT", 4)], writes=[("KTA", b)], dma=True)
                    S.add("gpsimd", lambda e: e.dma_start(
                        out=sq["KTB"][:, :, blk].rearrange("g r t -> r g t"), in_=qkT[:, 5:9, :]),
                        reads=[("qkT", r) for r in range(5, 9)], writes=[("KTB", b)], dma=True)
                return t3

            norm_pre(xin, xkeys, g1, "g1", nbufs)
            norm_tr(hTf, "hTf", nbufs)
            deferred = {}
            for b in range(NBLK):
                own = b * 512 < TQ
                if b + 1 < NBLK:
                    load_tab(b + 1)
                ffn_block(xin, xkeys, hTf, "hTf", 1, wgu_s, wd_s, wbase, AT, sg, hooks=deferred)
                deferred = {}
                if own:
                    S.add("gpsimd", lambda e, b=b: e.dma_start(
                        out=sq["x1"][b * 512:(b + 1) * 512, :].rearrange("(t p) d -> p t d", p=128), in_=xin),
                        reads=xkeys, writes=[("x1", b)], dma=True)
                norm_pre(xin, xkeys, gm, "gm", nbufs)
                norm_tr(hTm, "hTm", nbufs)
                if b + 1 < NBLK:
                    load_x(b + 1)
                    norm_pre(xin, xkeys, g1, "g1", nbufs)
                chunks = own_chunks if own else rest_chunks
                h0, h1 = (0, NH) if own else (8, 18)
                tsl = tabt[b % 3]
                tkey = ("tab", b % 3)
                proj_mm(b, 0, chunks)
                qk_chain(b, 0, chunks, h0, h1, tsl, tkey)
                proj_mm(b, 1, chunks)
                qk_chain(b, 1, chunks, h0, h1, tsl, tkey)
                if b + 1 < NBLK:
                    norm_tr(hTf, "hTf", nbufs)
                proj_mm(b, 2, chunks)
                proj_mm(b, 3, chunks)
                blk = slice(b * 512, (b + 1) * 512)
                vkeys = [("vst", t) for t in range(4)]
                S.add("gpsimd", lambda e, blk=blk: e.dma_start(
                    out=sq["VA"][blk, :].rearrange("(t p) c -> p t c", p=128), in_=vst[:, :, 0:128]),
                    reads=vkeys, writes=[("VA", b)], dma=True)
                S.add("gpsimd", lambda e, blk=blk: e.dma_start(
                    out=sq["VB"][blk, :].rearrange("(t p) c -> p t c", p=128), in_=vst[:, :, 128:VW]),
                    reads=vkeys, writes=[("VB", b)], dma=True)

                def mk(t, b=b, chunks=chunks, h0=h0, h1=h1, tsl=tsl, tkey=tkey):
                    return lambda: qk_chain(b, t, chunks, h0, h1, tsl, tkey)
                deferred = {1: mk(2), 5: mk(3), 10: make_t3(b, own)}
            for f in sorted(deferred):
                deferred[f]()
            S.barrier()
            if NSTOP == 1:
                continue

            A.cur = persist_mark
            KTA_sb = A.alloc([128, T], BF16)
            VAaug = A.alloc([128, NKB, 2, 65], BF16)
            KTB_sb = [A.alloc([128, T], BF16) for _ in range(2)]
            VB_sb = [A.alloc([128, NKB, 128], BF16) for _ in range(2)]
            QT = [A.alloc([128, 512], BF16) for _ in range(3)]
            ET = [A.alloc([128, 1024], BF16) for _ in range(4)]
            zacc = A.alloc([128, 512], F32)
            rz1 = A.alloc([128, 512], F32)
            rz2 = A.alloc([128, 512], F32)
            bcs = A.alloc([128, 512], F32)
            t1 = A.alloc([128, 512], F32)
            t2 = A.alloc([128, 512], F32)
            obf = A.alloc([128, 512], F32)
            sqf = A.alloc([128, 512], F32)
            vvb = A.alloc([128, 512], F32)
            rsd = A.alloc([128, 512], F32)
            ttb = A.alloc([128, 512], F32)
            obb = [A.alloc([128, 512], BF16) for _ in range(2)]

            S.add("sync", lambda e: e.dma_start(out=KTA_sb, in_=sq["KTA"]), writes=["KTA_sb"], dma=True)
            S.add("gpsimd", lambda e: e.memset(VAaug[:, :, :, 64:65], 1.0), writes=["VAones"])
            KG = 8
            for kb0 in range(0, NKB, KG):
                kb1 = min(NKB, kb0 + KG)
                for kv in range(2):
                    S.add("sync", lambda e, kb0=kb0, kb1=kb1, kv=kv: e.dma_start(
                        out=VAaug[:, kb0:kb1, kv, 0:64],
                        in_=sq["VA"][kb0 * 128:kb1 * 128, kv * 64:(kv + 1) * 64].rearrange("(k p) d -> p k d", p=128)),
                        writes=[("VAaug", kb0, kv)], dma=True)
            va_keys = [("VAaug", kb0, kv) for kb0 in range(0, NKB, KG) for kv in range(2)] + ["VAones"]

            def load_B(h, sq=sq, KTB_sb=KTB_sb, VB_sb=VB_sb, NKB=NKB):
                sl = h % 2
                S.add("sync", lambda e: e.dma_start(out=KTB_sb[sl], in_=sq["KTB"][h]), writes=[("KTB_sb", sl)], dma=True)
                for kb0 in range(0, NKB, KG):
                    kb1 = min(NKB, kb0 + KG)
                    S.add("sync", lambda e, kb0=kb0, kb1=kb1: e.dma_start(
                        out=VB_sb[sl][:, kb0:kb1, :],
                        in_=sq["VB"][kb0 * 128:kb1 * 128, h * 128:(h + 1) * 128].rearrange("(k p) c -> p k c", p=128)),
                        writes=[("VB_sb", sl, kb0)], dma=True)

            ctr = {"qt": 0, "et": 0, "sp": 0}
            osb = A.alloc([128, 4, 512], F32)
            pending_fin = [None]
            fin_steps = []

            def run_pending():
                if pending_fin[0] is not None:
                    f = pending_fin[0]
                    pending_fin[0] = None
                    r = f()
                    if r:
                        fin_steps.extend(r)

            def flush_steps():
                while fin_steps:
                    fin_steps.pop(0)()

            def attn_unit(qsrc, lhs_lo, lhs_hi, lkeys, pv_emit, pv_reads, pv_writes, post_exp=None):
                qi = ctr["qt"] % 3
                ctr["qt"] += 1
                qs, qkey = QT[qi], ("QT", qi)
                S.add("sync", lambda e: e.dma_start(out=qs, in_=qsrc), writes=[qkey], dma=True)
                pend = None
                for kb in range(NKB):
                    sp = ctr["sp"] % 2
                    ctr["sp"] += 1
                    ksl = slice(kb * 128, (kb + 1) * 128)

                    def emit_qk(e, sp=sp, ksl=ksl):
                        e.matmul(pp[sp][:, 0:512], lhsT=lhs_lo[:, ksl], rhs=qs[0:64, :], start=True, stop=True)
                        return e.matmul(pp[sp][:, 512:1024], lhsT=lhs_hi[:, ksl], rhs=qs[64:128, :], start=True, stop=True)
                    S.add("tensor", emit_qk, reads=lkeys + [qkey], writes=[PS(2 * sp), PS(2 * sp + 1)])
                    ei = ctr["et"] % 4
                    ctr["et"] += 1
                    es, ekey = ET[ei], ("ET", ei)
                    S.add("scalar", lambda e, sp=sp, es=es: e.activation(out=es, in_=pp[sp][:, :], func=AF.Exp, scale=SCALE),
                          reads=[PS(2 * sp), PS(2 * sp + 1)], writes=[ekey])
                    if post_exp is not None:
                        post_exp(kb, es, ekey)
                    if pend is not None:
                        S.add("tensor", pend[0], reads=pend[1], writes=pv_writes)
                    pend = (pv_emit(kb, es), pv_reads + [ekey])
                    if kb == min(3, NKB - 1):
                        run_pending()
                    if kb >= 3 and fin_steps:
                        fin_steps.pop(0)()
                S.add("tensor", pend[0], reads=pend[1], writes=pv_writes)
                flush_steps()

            load_B(0)
            for g in range(4):
                for qb in range(NQB):
                    qsl = slice(qb * 512, (qb + 1) * 512)

                    def pv_emit_A(kb, es):
                        def f(e):
                            e.matmul(bank(4)[0:65, :], lhsT=VAaug[:, kb, 0, :], rhs=es[:, 0:512],
                                     start=(kb == 0), stop=(kb == NKB - 1))
                            return e.matmul(bank(5)[0:65, :], lhsT=VAaug[:, kb, 1, :], rhs=es[:, 512:1024],
                                            start=(kb == 0), stop=(kb == NKB - 1))
                        return f
                    attn_unit(sq["QTA"][g][:, qsl], KTA_sb[0:64, :], KTA_sb[64:128, :], ["KTA_sb"],
                              pv_emit_A, va_keys, [PS(4), PS(5)])
                    for kv in range(2):
                        S.add("vector", lambda e, kv=kv: e.tensor_copy(out=osb[0:65, kv, :], in_=bank(4 + kv)[0:65, :]),
                              reads=[PS(4 + kv)], writes=[("osb", kv)])

                    def fin_A(g=g, qb=qb, qsl=qsl):
                        sp = ctr["sp"] % 2
                        ctr["sp"] += 1
                        S.add("vector", lambda e: e.reciprocal(out=rz1[64:65, :], in_=osb[64:65, 0, :]),
                              reads=[("osb", 0)], writes=["rz1"])
                        S.add("vector", lambda e: e.reciprocal(out=rz2[64:65, :], in_=osb[64:65, 1, :]),
                              reads=[("osb", 1)], writes=["rz2"])

                        def emit_bc(e, sp=sp):
                            e.matmul(pp[sp][0:64, 0:512], lhsT=ones_f[64:65, 0:64], rhs=rz1[64:65, :], start=True, stop=True)
                            return e.matmul(pp[sp][0:64, 512:1024], lhsT=ones_f[64:65, 0:64], rhs=rz2[64:65, :],
                                            start=True, stop=True)
                        S.add("tensor", emit_bc, reads=["rz1", "rz2", "ones_f"], writes=[PS(2 * sp), PS(2 * sp + 1)])
                        for kv in range(2):
                            S.add("vector", lambda e, kv=kv, sp=sp: e.tensor_tensor(
                                out=obb[kv][0:64, :], in0=osb[0:64, kv, :], in1=pp[sp][0:64, kv * 512:(kv + 1) * 512],
                                op=ALU.mult), reads=[("osb", kv), PS(2 * sp + kv)], writes=[("obb", kv)])
                            hq = kv * 4 + g
                            S.add("gpsimd", lambda e, kv=kv, hq=hq: e.dma_start(
                                out=sq["OAT"][hq * 64:(hq + 1) * 64, qsl], in_=obb[kv][0:64, :]),
                                reads=[("obb", kv)], writes=[("OAT", hq, qb)], dma=True)
                    pending_fin[0] = fin_A
            for h in range(4):
                sl = h % 2
                if h + 1 < 4:
                    load_B(h + 1)
                vb_keys = [("VB_sb", sl, kb0) for kb0 in range(0, NKB, KG)]
                for qb in range(NQB):
                    qsl = slice(qb * 512, (qb + 1) * 512)

                    def pv_emit_B(kb, es, sl=sl):
                        def f(e):
                            e.matmul(bank(4), lhsT=VB_sb[sl][:, kb, :], rhs=es[:, 0:512], start=(kb == 0), stop=(kb == NKB - 1))
                            e.matmul(bank(5), lhsT=VB_sb[sl][:, kb, :], rhs=es[:, 512:1024], start=(kb == 0), stop=(kb == NKB - 1))
                            return e.matmul(bank(6), lhsT=ones_bf, rhs=es[:, 0:512], start=(kb == 0), stop=(kb == NKB - 1))
                        return f

                    def post_exp_B(kb, es, ekey):
                        src = es[:, 512:1024]
                        if kb == 0:
                            S.add("vector", lambda e: e.tensor_copy(out=zacc, in_=src), reads=[ekey], writes=["zacc"])
                        else:
                            S.add("vector", lambda e: e.tensor_tensor(out=zacc, in0=zacc, in1=src, op=ALU.add),
                                  reads=[ekey, "zacc"], writes=["zacc"])
                    attn_unit(sq["QTB"][h][:, qsl], KTB_sb[sl][0:64, :], KTB_sb[sl][64:128, :], [("KTB_sb", sl)],
                              pv_emit_B, vb_keys + ["ones_bf"], [PS(4), PS(5), PS(6)], post_exp=post_exp_B)
                    S.add("tensor", lambda e: e.matmul(bank(7), lhsT=ones_f, rhs=zacc, start=True, stop=True),
                          reads=["zacc", "ones_f"], writes=[PS(7)])
                    S.add("scalar", lambda e: e.copy(out=osb[:, 0, :], in_=bank(4)), reads=[PS(4)], writes=[("osb", 0)])
                    S.add("scalar", lambda e: e.copy(out=osb[:, 1, :], in_=bank(5)), reads=[PS(5)], writes=[("osb", 1)])
                    S.add("vector", lambda e: e.reciprocal(out=rz1, in_=bank(6)), reads=[PS(6)], writes=["rz1"])
                    S.add("vector", lambda e: e.reciprocal(out=rz2, in_=bank(7)), reads=[PS(7)], writes=["rz2"])

                    def fin_B(h=h, qb=qb, qsl=qsl):
                        st_ = []
                        st_.append(lambda: S.add("vector", lambda e: e.tensor_tensor(out=t1, in0=osb[:, 0, :], in1=rz1, op=ALU.mult),
                                                 reads=[("osb", 0), "rz1"], writes=["t1"]))
                        st_.append(lambda: S.add("gpsimd", lambda e: e.tensor_tensor(out=t2, in0=osb[:, 1, :], in1=rz2, op=ALU.mult),
                                                 reads=[("osb", 1), "rz2"], writes=["t2"]))
                        st_.append(lambda: S.add("vector", lambda e: e.scalar_tensor_tensor(
                            out=obf, in0=t2, scalar=neglam[:, 0:1], in1=t1, op0=ALU.mult, op1=ALU.add),
                            reads=["t1", "t2", "neglam"], writes=["obf"]))
                        st_.append(lambda: S.add("gpsimd", lambda e: e.tensor_tensor(out=sqf, in0=obf, in1=obf, op=ALU.mult),
                                                 reads=["obf"], writes=["sqf"]))
                        slot = {}

                        def ss_mm():
                            sp = ctr["sp"] % 2
                            ctr["sp"] += 1
                            slot["sp"] = sp
                            S.add("tensor", lambda e: e.matmul(pp[sp][:, 0:512], lhsT=ones_f, rhs=sqf, start=True, stop=True),
                                  reads=["sqf", "ones_f"], writes=[PS(2 * sp), PS(2 * sp + 1)])
                        st_.append(ss_mm)

                        def vv():
                            sp = slot["sp"]
                            S.add("vector", lambda e: e.tensor_scalar(out=vvb, in0=pp[sp][:, 0:512], scalar1=1.0 / 128,
                                                                      scalar2=EPS, op0=ALU.mult, op1=ALU.add),
                                  reads=[PS(2 * sp)], writes=["vvb"])
                        st_.append(vv)
                        st_.extend(rsqrt_steps("vector", vvb, rsd, ttb, "vvb", "rsd", "ttb"))
                        st_.append(lambda: S.add("vector", lambda e: e.scalar_tensor_tensor(
                            out=obb[0], in0=obf, scalar=gout[:, 0:1], in1=rsd, op0=ALU.mult, op1=ALU.mult),
                            reads=["obf", "rsd", "gout"], writes=[("obb", 0)]))
                        st_.append(lambda: S.add("gpsimd", lambda e: e.dma_start(
                            out=sq["OBT"][h * 128:(h + 1) * 128, qsl], in_=obb[0]),
                            reads=[("obb", 0)], writes=[("OBT", h, qb)], dma=True))
                        return st_
                    pending_fin[0] = fin_B
            run_pending()
            flush_steps()
            S.barrier()
            if NSTOP == 2:
                continue

            A.cur = persist_mark
            xin2 = [A.alloc([128, 4, D], F32) for _ in range(2)]
            hb = A.alloc([128, 4, D], BF16)
            hT = A.alloc([128, KD, 512], BF16)
            hTf2 = A.alloc([128, KD, 512], BF16)
            AT = A.alloc([128, NF, 512], BF16)
            wgu_slots = [A.alloc([128, 2, KD, 128], BF16) for _ in range(4)]
            wd_slots = [A.alloc([128, 2, 512], BF16) for _ in range(4)]
            wc_slots = [A.alloc([128, 24, 128], BF16) for _ in range(3)]
            wout_sb = A.alloc([128, KD, D], BF16)
            oaT = A.alloc([128, 4, 512], BF16)
            obT = A.alloc([128, 4, 512], BF16)
            mT = A.alloc([128, KD, 512], BF16)
            tha = [A.alloc([128, 512], F32) for _ in range(2)]
            thb = [A.alloc([128, 512], F32) for _ in range(2)]
            u1 = A.alloc([128, 512], F32)
            u2 = A.alloc([128, 512], F32)
            sg = [A.alloc([128, 512], F32) for _ in range(2)]
            junk = A.alloc([128, D], BF16)
            ss4 = A.alloc([128, 4], F32)
            vv4 = A.alloc([128, 4], F32)
            rstd4 = A.alloc([128, 4], F32)
            tt4 = A.alloc([128, 4], F32)
            nbufs = (junk, ss4, vv4, rstd4, tt4, hb)
            wgu_s = Stream("wgu", wgu_slots, wgu_plan(2, NQB))
            wd_s = Stream("wd", wd_slots, wd_plan(2, NQB))
            wbase = [0, 0]
            wc_plan = []
            for _ in range(NQB):
                for c in range(8):
                    wc_plan.append(lambda dst, c=c: (lambda e: e.dma_start(out=dst, in_=wc_d[c])))
            wc_s = Stream("wc", wc_slots, wc_plan)
            for k0 in range(0, KD, 4):
                S.add("sync", lambda e, k0=k0: e.dma_start(out=wout_sb[:, k0:k0 + 4, :], in_=wout_d[:, k0:k0 + 4, :]),
                      writes=[("wout_sb", k0)], dma=True)
            wout_keys = [("wout_sb", 0), ("wout_sb", 4)]

            def load_x1(b, sq=sq, xin2=xin2):
                S.add("sync", lambda e: e.dma_start(
                    out=xin2[b % 2], in_=sq["x1"][b * 512:(b + 1) * 512, :].rearrange("(t p) d -> p t d", p=128)),
                    writes=[("xin", b % 2, t) for t in range(4)], dma=True)

            def load_o(b, sq=sq, oaT=oaT, obT=obT):
                blk = slice(b * 512, (b + 1) * 512)
                S.add("sync", lambda e: e.dma_start(
                    out=oaT, in_=sq["OAT"][:, blk].rearrange("(c p) t -> p c t", p=128)), writes=["oaT"], dma=True)
                S.add("sync", lambda e: e.dma_start(
                    out=obT, in_=sq["OBT"][:, blk].rearrange("(c p) t -> p c t", p=128)), writes=["obT"], dma=True)

            load_x1(0)
            norm_pre(xin2[0], [("xin", 0, t) for t in range(4)], gm, "gm", nbufs)
            for b in range(NQB):
                blk = slice(b * 512, (b + 1) * 512)
                xt = xin2[b % 2]
                xkeys = [("xin", b % 2, t) for t in range(4)]
                if b + 1 < NQB:
                    load_x1(b + 1)
                if b == 0:
                    load_o(0)
                norm_tr(hT, "hT", nbufs)
                hk = [("hT", k) for k in range(KD)]
                for c in range(8):
                    wsl, wkey = wc_s.get(b * 8 + c)
                    bs = (c % 2) * 4

                    def emit_g(e, wsl=wsl, bs=bs):
                        for k in range(KD):
                            e.matmul(bank(bs), lhsT=wsl[:, k, :], rhs=hT[:, k, :], start=(k == 0), stop=(k == KD - 1))
                        for k in range(KD):
                            e.matmul(bank(bs + 1), lhsT=wsl[:, 8 + k, :], rhs=hT[:, k, :], start=(k == 0), stop=(k == KD - 1))
                        for k in range(4):
                            e.matmul(bank(bs + 2), lhsT=wsl[:, 16 + k, :], rhs=oaT[:, k, :], start=(k == 0), stop=(k == 3))
                        for k in range(4):
                            ins = e.matmul(bank(bs + 3), lhsT=wsl[:, 20 + k, :], rhs=obT[:, k, :], start=(k == 0), stop=(k == 3))
                        return ins
                    S.add("tensor", emit_g, reads=hk + [wkey, "oaT", "obT"], writes=[PS(bs + i) for i in range(4)])
                    ta, tb = tha[c % 2], thb[c % 2]
                    S.add("scalar", lambda e, ta=ta, bs=bs: e.activation(out=ta, in_=bank(bs), func=AF.Tanh, scale=0.5),
                          reads=[PS(bs)], writes=[("tha", c % 2)])
                    S.add("scalar", lambda e, tb=tb, bs=bs: e.activation(out=tb, in_=bank(bs + 1), func=AF.Tanh, scale=0.5),
                          reads=[PS(bs + 1)], writes=[("thb", c % 2)])
                    S.add("vector", lambda e, ta=ta, bs=bs: e.scalar_tensor_tensor(
                        out=u1, in0=ta, scalar=1.0, in1=bank(bs + 2), op0=ALU.add, op1=ALU.mult),
                        reads=[("tha", c % 2), PS(bs + 2)], writes=["u1"])
                    S.add("vector", lambda e, tb=tb, bs=bs: e.scalar_tensor_tensor(
                        out=u2, in0=tb, scalar=1.0, in1=bank(bs + 3), op0=ALU.add, op1=ALU.mult),
                        reads=[("thb", c % 2), PS(bs + 3)], writes=["u2"])
                    S.add("gpsimd", lambda e, c=c: e.tensor_tensor(out=mT[:, c, :], in0=u1, in1=u2, op=ALU.add),
                          reads=["u1", "u2"], writes=[("mT", c)])
                mk = [("mT", c) for c in range(8)]
                for t in range(4):
                    for half in range(2):
                        bi = (t * 2 + half) % 8

                        def emit_o(e, t=t, half=half, bi=bi):
                            for k in range(KD):
                                ins = e.matmul(bank(bi), lhsT=mT[:, k, t * 128:(t + 1) * 128],
                                               rhs=wout_sb[:, k, half * 512:(half + 1) * 512], start=(k == 0), stop=(k == KD - 1))
                            return ins
                        S.add("tensor", emit_o, reads=mk + wout_keys, writes=[PS(bi)])
                        xsl = xt[:, t, half * 512:(half + 1) * 512]
                        S.add("vector", lambda e, xsl=xsl, bi=bi: e.scalar_tensor_tensor(
                            out=xsl, in0=bank(bi), scalar=0.5, in1=xsl, op0=ALU.mult, op1=ALU.add),
                            reads=[PS(bi), xkeys[t]], writes=[xkeys[t]])
                if NSTOP != 3:
                    norm_T(xt, xkeys, g2, "g2", hTf2, "hTf2", nbufs)
                    nxt = None
                    if b + 1 < NQB:
                        def nxt(b=b):
                            load_o(b + 1)
                            wc_s.prefetch((b + 1) * 8)
                            norm_pre(xin2[(b + 1) % 2], [("xin", (b + 1) % 2, t) for t in range(4)], gm, "gm", nbufs)
                    ffn_block(xt, xkeys, hTf2, "hTf2", 2, wgu_s, wd_s, wbase, AT, sg, hook=nxt, hook_f=4)
                elif b + 1 < NQB:
                    load_o(b + 1)
                    norm_pre(xin2[(b + 1) % 2], [("xin", (b + 1) % 2, t) for t in range(4)], gm, "gm", nbufs)
                S.add("gpsimd", lambda e, blk=blk, xt=xt: e.dma_start(
                    out=sq["y"][blk, :].rearrange("(t p) d -> p t d", p=128), in_=xt),
                    reads=xkeys, writes=[("y", b)], dma=True)
            S.barrier()
        S.run(block)
    return nc


def _rot_table(pos):
    pos = np.asarray(pos, dtype=np.int64)
    row = (pos // 64).astype(np.float32)
    col = (pos % 64).astype(np.float32)
    invA = (np.float32(10000.0) ** (-np.arange(0, 32, 2, dtype=np.float32) / np.float32(32))).astype(np.float32)
    angA = np.concatenate([row[:, None] * invA[None, :], col[:, None] * invA[None, :]], axis=-1).astype(np.float32)
    invP = (np.float32(500000.0) ** (-np.arange(0, 16, 2, dtype=np.float32) / np.float32(16))).astype(np.float32)
    angP = (pos.astype(np.float32)[:, None] * invP[None, :]).astype(np.float32)
    tab = np.concatenate([np.cos(angA.astype(np.float64)), np.sin(angA.astype(np.float64)),
                          np.cos(angP.astype(np.float64)), np.sin(angP.astype(np.float64))], axis=-1)
    return np.ascontiguousarray(tab.astype(np.float32))


def _win_perm():
    qa = [(kv * 4 + g) * 64 + d for g in range(4) for kv in range(2) for d in range(64)]
    ka = list(range(512, 640))
    va = list(range(640, 768))
    qb = list(range(768, 1280))
    kb = list(range(1280, 1792))
    vb = list(range(1792, 2304))
    return np.array(qa + ka + kb + qb + va + vb, dtype=np.int64)


def make_in_maps(inputs, TS, TPK, TPQ, n_cores=8):
    f = lambda a: np.ascontiguousarray(np.asarray(a, dtype=np.float32))
    xp_all = f(inputs["x_prompt"])
    xs_all = f(inputs["x_sample"])
    nq = TPK // TPQ
    common = {}
    for nm in ("ffn1_norm", "mix_norm", "ffn2_norm", "a_q_norm", "a_k_norm", "b_q_norm", "b_k_norm",
               "b_lambda_q1", "b_lambda_k1", "b_lambda_q2", "b_lambda_k2", "b_out_norm"):
        common[nm] = f(inputs[nm]).reshape(1, -1)
    for nm in ("ffn1_w_gate", "ffn1_w_up", "ffn1_w_down", "ffn2_w_gate", "ffn2_w_up", "ffn2_w_down",
               "w_branch_gate", "w_o_a", "w_o_b", "w_out"):
        a = f(inputs[nm])
        common[nm] = np.ascontiguousarray(a.reshape(a.shape[-2], a.shape[-1]))
    win = f(inputs["w_in"]).reshape(D, INW)
    common["w_in"] = np.ascontiguousarray(win[:, _win_perm()])
    tab_s = _rot_table(np.arange(TS))
    in_maps = []
    for c in range(n_cores):
        pi, q = (c // nq) % xp_all.shape[0], c % nq
        pos_p = np.roll(np.arange(TPK), -q * TPQ)
        m = dict(common)
        m["xs"] = np.ascontiguousarray(xs_all[c])
        m["xp"] = np.ascontiguousarray(xp_all[pi][pos_p])
        m["tab_s"] = tab_s
        m["tab_p"] = _rot_table(pos_p)
        in_maps.append(m)
    return in_maps


_NC_CACHE = {}


def kernel(**inputs):
    TS, TPK, TPQ = 4096, 8192, 2048
    key = (TS, TPK, TPQ)
    if key not in _NC_CACHE:
        _NC_CACHE[key] = build(TS, TPK, TPQ)
    nc = _NC_CACHE[key]
    in_maps = make_in_maps(inputs, TS, TPK, TPQ)
    res = run_bass_kernel_spmd(nc, in_maps, core_ids=list(range(8)))
    y_prompt = np.empty((2, TPK, D), dtype=np.float32)
    y_sample = np.empty((8, TS, D), dtype=np.float32)
    for c in range(8):
        r = res.results[c]
        y_sample[c] = r["ys"]
        pi, q = c // 4, c % 4
        y_prompt[pi, q * TPQ:(q + 1) * TPQ] = r["yp"]
    return (y_prompt, y_sample)
```

```python
import math
import types
from contextlib import ExitStack

import numpy as np
import concourse.bass as bass
import concourse.mybir as mybir
from concourse.bass_utils import run_bass_kernel_spmd

F32 = mybir.dt.float32
BF16 = mybir.dt.bfloat16
I32 = mybir.dt.int32
U8 = mybir.dt.uint8
AF = mybir.ActivationFunctionType
ALU = mybir.AluOpType
AX = mybir.AxisListType

D = 1024
DFF = 2816
NF = DFF // 128
KD = D // 128
INW = 2304
NH = 26
QKW = NH * 64
VW = 640
EPS = 1e-6
SCALE = 0.125
LAM_INIT = 0.8 - 0.6 * math.exp(-0.3 * 0)

ENGS = ("sync", "scalar", "vector", "gpsimd", "tensor")
DMAQ = ("sync", "scalar", "gpsimd")


def _snap(fn):
    if fn is None or getattr(fn, "__closure__", None) is None:
        return fn
    cells = []
    for c in fn.__closure__:
        try:
            cells.append(types.CellType(c.cell_contents))
        except ValueError:
            cells.append(c)
    g = types.FunctionType(fn.__code__, fn.__globals__, fn.__name__, fn.__defaults__, tuple(cells))
    g.__kwdefaults__ = fn.__kwdefaults__
    return g


class Op:
    __slots__ = ("eng", "emit", "deps", "dma", "sigsem", "sigval", "signal")

    def __init__(self, eng, emit, dma):
        self.eng = eng
        self.emit = _snap(emit)
        self.dma = dma
        self.deps = set()
        self.sigsem = None
        self.sigval = 0
        self.signal = False


class Sched:
    def __init__(self, nc, stack, n_dma_sems=12):
        self.nc = nc
        self.ops = {e: [] for e in ENGS}
        self.last_w = {}
        self.readers = {}
        self.csem = {e: stack.enter_context(nc.semaphore("c_" + e)) for e in ENGS if e != "sync"}
        self.dsem = {q: [stack.enter_context(nc.semaphore("d_%s_%d" % (q, i))) for i in range(n_dma_sems)]
                     for q in DMAQ}
        self.dcount = {q: 0 for q in DMAQ}
        self.dlast = {q: {} for q in DMAQ}
        self.since_barrier_dma = []
        self.last_op = {}

    def add(self, eng, emit, reads=(), writes=(), dma=False):
        op = Op(eng, emit, dma)
        deps = op.deps
        for r in reads:
            w = self.last_w.get(r)
            if w is not None:
                deps.add(w)
        for w_ in writes:
            w = self.last_w.get(w_)
            if w is not None:
                deps.add(w)
            rd = self.readers.get(w_)
            if rd:
                deps.update(rd[0].values())
                deps.update(rd[1])
        for r in reads:
            rd = self.readers.get(r)
            if rd is None:
                rd = self.readers[r] = ({}, [])
            if dma:
                rd[1].append(op)
            else:
                rd[0][eng] = op
        for w_ in writes:
            self.last_w[w_] = op
            self.readers[w_] = ({}, [])
        deps.discard(op)
        if dma:
            n = self.dcount[eng]
            sems = self.dsem[eng]
            i = n % len(sems)
            op.sigsem = sems[i]
            op.sigval = 16 * (n // len(sems) + 1)
            prev = self.dlast[eng].get(i)
            if prev is not None:
                deps.add(prev)
            self.dlast[eng][i] = op
            self.dcount[eng] = n + 1
            self.since_barrier_dma.append(op)
        self.ops[eng].append(op)
        if not dma:
            self.last_op[eng] = op
        return op

    def barrier(self):
        deps = set(self.since_barrier_dma)
        for o in self.last_op.values():
            if o is not None and o.emit is not None:
                deps.add(o)
        self.since_barrier_dma = []
        for e in ENGS:
            op = Op(e, None, False)
            op.deps = set(deps)
            self.ops[e].append(op)
        self.last_w = {}
        self.readers = {}

    def finalize(self):
        for e in ENGS:
            for op in self.ops[e]:
                for d in op.deps:
                    if d.dma:
                        continue
                    if d.eng == "tensor" and op.eng == "tensor" and not op.dma:
                        continue
                    d.signal = True
        for e in ENGS:
            if e == "sync":
                continue
            c = 0
            for op in self.ops[e]:
                if op.dma or not op.signal:
                    continue
                assert op.emit is not None
                c += 1
                op.sigsem = self.csem[e]
                op.sigval = c

    def emit_engine(self, ename, eobj):
        waited = {}
        for op in self.ops[ename]:
            needs = {}
            for d in op.deps:
                if (not d.dma) and d.eng == "tensor" and ename == "tensor" and not op.dma:
                    continue
                k = d.sigsem
                if needs.get(k, 0) < d.sigval:
                    needs[k] = d.sigval
            for s, v in needs.items():
                if waited.get(s, 0) < v:
                    eobj.wait_ge(s, v)
                    waited[s] = v
            if op.emit is not None:
                ins = op.emit(eobj)
                if op.dma:
                    ins.then_inc(op.sigsem, 16)
                elif op.signal:
                    ins.then_inc(op.sigsem, 1)

    def run(self, block):
        self.finalize()
        s = self

        @block.sync
        def _(e):
            s.emit_engine("sync", e)

        @block.scalar
        def _(e):
            s.emit_engine("scalar", e)

        @block.vector
        def _(e):
            s.emit_engine("vector", e)

        @block.gpsimd
        def _(e):
            s.emit_engine("gpsimd", e)

        @block.tensor
        def _(e):
            s.emit_engine("tensor", e)


class Arena:
    def __init__(self, t, size):
        self.t = t
        self.size = size
        self.cur = 0

    def alloc(self, shape, dt):
        esz = 4 if dt in (F32, I32) else 2
        n = int(np.prod(shape[1:])) * esz
        off = self.cur
        self.cur = off + (n + 63) // 64 * 64
        assert self.cur <= self.size, ("SBUF arena overflow", self.cur, self.size)
        ap = self.t[:, off:off + n].bitcast(dt)
        if len(shape) == 3:
            ap = ap.rearrange("p (a b) -> p a b", a=shape[1], b=shape[2])
        elif len(shape) == 4:
            ap = ap.rearrange("p (a b c) -> p a b c", a=shape[1], b=shape[2], c=shape[3])
        return ap


def build(TS, TPK, TPQ, debug=False, stop_after=0):
    nc = bass.Bass("TRN2", target_bir_lowering=False)

    def din(name, shape, dt=F32):
        return nc.dram_tensor(name, list(shape), dt, kind="ExternalInput").ap()

    def dout(name, shape, dt=F32):
        return nc.dram_tensor(name, list(shape), dt, kind="ExternalOutput").ap()

    def dscr(name, shape, dt):
        return nc.dram_tensor(name, list(shape), dt, kind="ExternalOutput" if debug else "Internal").ap()

    xs_d = din("xs", [TS, D])
    xp_d = din("xp", [TPK, D])
    tabs_d = din("tab_s", [TS, 80])
    tabp_d = din("tab_p", [TPK, 80])
    W = {}
    for j in (1, 2):
        W["g%d" % j] = din("ffn%d_w_gate" % j, [D, DFF])
        W["u%d" % j] = din("ffn%d_w_up" % j, [D, DFF])
        W["d%d" % j] = din("ffn%d_w_down" % j, [DFF, D])
        W["n%d" % j] = din("ffn%d_norm" % j, [1, D])
    W["nm"] = din("mix_norm", [1, D])
    W["in"] = din("w_in", [D, INW])
    W["bg"] = din("w_branch_gate", [D, 2 * D])
    W["oa"] = din("w_o_a", [512, D])
    W["ob"] = din("w_o_b", [512, D])
    W["out"] = din("w_out", [D, D])
    for nm in ("a_q_norm", "a_k_norm", "b_q_norm", "b_k_norm", "b_lambda_q1", "b_lambda_k1",
               "b_lambda_q2", "b_lambda_k2"):
        W[nm] = din(nm, [1, 64])
    W["b_out_norm"] = din("b_out_norm", [1, 128])
    ys_d = dout("ys", [TS, D])
    yp_d = dout("yp", [TPQ, D])

    wgu_d = {j: dscr("wgu%d" % j, [NF, 128, 2 * KD * 128], BF16) for j in (1, 2)}
    wd_d = {j: dscr("wd%d" % j, [2, NF, 128, 512], BF16) for j in (1, 2)}
    win_d = dscr("win_bf", [128, KD, INW], BF16)
    wc_d = dscr("wc_bf", [8, 128, 24, 128], BF16)
    wout_d = dscr("wout_bf", [128, KD, D], BF16)

    seqs = []
    for nm, xd, tabd, T, TQ, yd in (("s", xs_d, tabs_d, TS, TS, ys_d), ("p", xp_d, tabp_d, TPK, TPQ, yp_d)):
        seqs.append(dict(
            nm=nm, x=xd, tab=tabd, T=T, TQ=TQ, y=yd,
            x1=dscr("x1_" + nm, [TQ, D], F32),
            KTA=dscr("kta_" + nm, [128, T], BF16), KTB=dscr("ktb_" + nm, [4, 128, T], BF16),
            VA=dscr("va_" + nm, [T, 128], BF16), VB=dscr("vb_" + nm, [T, 512], BF16),
            QTA=dscr("qta_" + nm, [4, 128, TQ], BF16), QTB=dscr("qtb_" + nm, [4, 128, TQ], BF16),
            OAT=dscr("oat_" + nm, [512, TQ], BF16), OBT=dscr("obt_" + nm, [512, TQ], BF16)))

    ARENA_BYTES = 207 * 1024
    with ExitStack() as st:
        arena_t = st.enter_context(nc.sbuf_tensor("arena", [128, ARENA_BYTES], U8))
        pp = [st.enter_context(nc.psum_tensor("pp%d" % i, [128, 1024], F32)) for i in range(4)]
        block = st.enter_context(nc.Block())
        S = Sched(nc, st)
        A = Arena(arena_t, ARENA_BYTES)

        def bank(i):
            return pp[i // 2][:, (i % 2) * 512:(i % 2 + 1) * 512]

        def bank_bf(i):
            return bank(i).bitcast(BF16)

        def PS(i):
            return ("ps", i)

        _rr = [0]

        def rr_eng(choices=("vector", "scalar", "gpsimd")):
            _rr[0] += 1
            return choices[_rr[0] % len(choices)]

        def copy_op(eng, out, in_, reads, writes):
            if eng == "scalar":
                S.add("scalar", lambda e: e.copy(out=out, in_=in_), reads=reads, writes=writes)
            else:
                S.add(eng, lambda e: e.tensor_copy(out=out, in_=in_), reads=reads, writes=writes)

        ident = A.alloc([128, 128], BF16)
        ones_bf = A.alloc([128, 128], BF16)
        ones_f = A.alloc([128, 128], F32)
        g1 = A.alloc([128, D], F32)
        gm = A.alloc([128, D], F32)
        g2 = A.alloc([128, D], F32)
        gall = A.alloc([128, NH, 64], F32)
        lamv = A.alloc([128, 4, 64], F32)
        neglam = A.alloc([128, 1], F32)
        gout = A.alloc([128, 1], F32)
        sm1 = A.alloc([128, 8], F32)
        persist_mark = A.cur

        S.add("gpsimd", lambda e: e.memset(ident, 0.0), writes=["ident"])
        S.add("gpsimd", lambda e: e.affine_select(
            out=ident, in_=ident, compare_op=ALU.not_equal, fill=1.0, base=0, pattern=[[-1, 128]],
            channel_multiplier=1), reads=["ident"], writes=["ident"])
        S.add("gpsimd", lambda e: e.memset(ones_bf, 1.0), writes=["ones_bf"])
        S.add("gpsimd", lambda e: e.memset(ones_f, 1.0), writes=["ones_f"])
        for dst, src, key in ((g1, W["n1"], "g1"), (gm, W["nm"], "gm"), (g2, W["n2"], "g2")):
            S.add("sync", lambda e, dst=dst, src=src: e.dma_start(out=dst, in_=src.broadcast_to([128, D])),
                  writes=[key], dma=True)
        for h0, h1, nm in ((0, 8, "a_q_norm"), (8, 10, "a_k_norm"), (10, 18, "b_k_norm"), (18, 26, "b_q_norm")):
            for h in range(h0, h1):
                S.add("sync", lambda e, h=h, nm=nm: e.dma_start(out=gall[:, h, :], in_=W[nm].broadcast_to([128, 64])),
                      writes=[("gall", h)], dma=True)
        gall_keys = [("gall", h) for h in range(NH)]
        for i, nm in enumerate(("b_lambda_q1", "b_lambda_k1", "b_lambda_q2", "b_lambda_k2")):
            S.add("sync", lambda e, i=i, nm=nm: e.dma_start(out=lamv[:, i, :], in_=W[nm].broadcast_to([128, 64])),
                  writes=[("lamv", i)], dma=True)
        S.add("sync", lambda e: e.dma_start(out=gout, in_=W["b_out_norm"].rearrange("o n -> n o")),
              writes=["gout"], dma=True)
        S.add("vector", lambda e: e.tensor_tensor(out=lamv[:, 0, :], in0=lamv[:, 0, :], in1=lamv[:, 1, :], op=ALU.mult),
              reads=[("lamv", 0), ("lamv", 1)], writes=[("lamv", 0)])
        S.add("vector", lambda e: e.tensor_tensor(out=lamv[:, 2, :], in0=lamv[:, 2, :], in1=lamv[:, 3, :], op=ALU.mult),
              reads=[("lamv", 2), ("lamv", 3)], writes=[("lamv", 2)])
        S.add("vector", lambda e: e.tensor_reduce(out=sm1[:, 0:1], in_=lamv[:, 0, :], axis=AX.X, op=ALU.add),
              reads=[("lamv", 0)], writes=["sm1a"])
        S.add("vector", lambda e: e.tensor_reduce(out=sm1[:, 1:2], in_=lamv[:, 2, :], axis=AX.X, op=ALU.add),
              reads=[("lamv", 2)], writes=["sm1b"])
        S.add("scalar", lambda e: e.activation(out=sm1[:, 2:4], in_=sm1[:, 0:2], func=AF.Exp),
              reads=["sm1a", "sm1b"], writes=["sm1e"])
        S.add("vector", lambda e: e.tensor_tensor(out=sm1[:, 4:5], in0=sm1[:, 3:4], in1=sm1[:, 2:3], op=ALU.subtract),
              reads=["sm1e"], writes=["sm1d"])
        S.add("vector", lambda e: e.tensor_scalar(out=neglam, in0=sm1[:, 4:5], scalar1=-LAM_INIT, scalar2=None, op0=ALU.add),
              reads=["sm1d"], writes=["neglam"])
        S.add("vector", lambda e: e.tensor_scalar(out=gout, in0=gout, scalar1=1.0 - LAM_INIT, scalar2=None, op0=ALU.mult),
              reads=["gout"], writes=["gout"])

        def make_conv(stage, stage_bf, Tg, engines):
            cnt = [0]
            rr = [0]

            def eng():
                rr[0] += 1
                return engines[rr[0] % len(engines)]

            def load_cast(src_ap, n, C, dst_bf=None, dst_key=None):
                slot = cnt[0] % 2
                cnt[0] += 1
                sv = stage[slot][:, 0:n * C].rearrange("p (a c) -> p a c", a=n, c=C)
                S.add("sync", lambda e: e.dma_start(out=sv, in_=src_ap.rearrange("(a p) c -> p a c", p=128)),
                      writes=[("stg", slot)], dma=True)
                if dst_bf is None:
                    bv = stage_bf[slot][:, 0:n * C].rearrange("p (a c) -> p a c", a=n, c=C)
                    copy_op(eng(), bv, sv, [("stg", slot)], [("stb", slot)])
                    return bv, ("stb", slot)
                copy_op(eng(), dst_bf, sv[:, 0, :].rearrange("p (f c) -> p f c", f=NF, c=128), [("stg", slot)], [dst_key])
                return dst_bf, dst_key

            def steps_ffn(j):
                st_ = []
                Tv = Tg.rearrange("p f (k c) -> p f k c", k=KD, c=128)
                for gi, key in enumerate(("g%d" % j, "u%d" % j)):
                    for k in range(KD):
                        st_.append(lambda gi=gi, key=key, k=k: load_cast(
                            W[key][k * 128:(k + 1) * 128, :], 1, DFF, dst_bf=Tv[:, :, k, :], dst_key=("Tg", k)))

                    def store_g(gi=gi):
                        for f0 in range(0, NF, 6):
                            f1 = min(NF, f0 + 6)
                            S.add("gpsimd", lambda e, f0=f0, f1=f1: e.dma_start(
                                out=wgu_d[j][f0:f1, :, gi * KD * 128:(gi + 1) * KD * 128].rearrange("f p x -> p f x"),
                                in_=Tg[:, f0:f1, :]),
                                reads=[("Tg", k_) for k_ in range(KD)], writes=[("wgu", j, gi, f0)], dma=True)
                    st_.append(store_g)
                for f0 in range(0, NF, 2):
                    def wd_step(f0=f0):
                        bv, key = load_cast(W["d%d" % j][f0 * 128:(f0 + 2) * 128, :], 2, D)
                        for half in range(2):
                            S.add("gpsimd", lambda e, half=half: e.dma_start(
                                out=wd_d[j][half, f0:f0 + 2].rearrange("f p x -> p f x"),
                                in_=bv[:, :, half * 512:(half + 1) * 512]),
                                reads=[key], writes=[("wd", j, f0, half)], dma=True)
                    st_.append(wd_step)
                return st_

            def steps_win():
                st_ = []
                for k in range(KD):
                    def f(k=k):
                        bv, key = load_cast(W["in"][k * 128:(k + 1) * 128, :], 1, INW)
                        S.add("gpsimd", lambda e: e.dma_start(out=win_d[:, k, :], in_=bv[:, 0, :]),
                              reads=[key], writes=[("win_d", k)], dma=True)
                    st_.append(f)
                return st_

            def steps_mix():
                st_ = []
                for k in range(KD):
                    def f(k=k):
                        bv, key = load_cast(W["bg"][k * 128:(k + 1) * 128, :], 1, 2 * D)
                        for ab in range(2):
                            S.add("gpsimd", lambda e, ab=ab: e.dma_start(
                                out=wc_d[:, :, ab * 8 + k, :].rearrange("c p j -> p c j"),
                                in_=bv[:, 0, ab * D:(ab + 1) * D].rearrange("p (c j) -> p c j", c=8, j=128)),
                                reads=[key], writes=[("wc_d", ab, k)], dma=True)
                    st_.append(f)
                for oi, key_w in enumerate(("oa", "ob")):
                    for k0 in range(0, 4, 2):
                        def f(oi=oi, key_w=key_w, k0=k0):
                            bv, key = load_cast(W[key_w][k0 * 128:(k0 + 2) * 128, :], 2, D)
                            for kk in range(2):
                                S.add("gpsimd", lambda e, kk=kk: e.dma_start(
                                    out=wc_d[:, :, 16 + oi * 4 + k0 + kk, :].rearrange("c p j -> p c j"),
                                    in_=bv[:, kk, :].rearrange("p (c j) -> p c j", c=8, j=128)),
                                    reads=[key], writes=[("wc_d", 2 + oi, k0 + kk)], dma=True)
                        st_.append(f)
                for k0 in range(0, KD, 2):
                    def f(k0=k0):
                        bv, key = load_cast(W["out"][k0 * 128:(k0 + 2) * 128, :], 2, D)
                        S.add("gpsimd", lambda e: e.dma_start(out=wout_d[:, k0:k0 + 2, :], in_=bv),
                              reads=[key], writes=[("wout_d", k0)], dma=True)
                    st_.append(f)
                return st_
            return steps_ffn, steps_win, steps_mix

        _stage = [A.alloc([128, 3072], F32) for _ in range(2)]
        _stage_bf = [A.alloc([128, 3072], BF16) for _ in range(2)]
        _Tg = A.alloc([128, NF, KD * 128], BF16)
        p_ffn, p_win, p_mix = make_conv(_stage, _stage_bf, _Tg, ("vector", "scalar", "gpsimd"))
        for stp in p_ffn(1) + p_win():
            stp()
        S.barrier()
        bg_steps = []

        def rsqrt_ops(eng, v, y, t, vk, yk, tk):
            S.add(eng, lambda e: e.tensor_scalar(out=y.bitcast(I32), in0=v.bitcast(I32), scalar1=1, scalar2=None,
                                                 op0=ALU.arith_shift_right), reads=[vk], writes=[yk])
            S.add(eng, lambda e: e.tensor_scalar(out=y.bitcast(I32), in0=y.bitcast(I32), scalar1=-1,
                                                 scalar2=0x5f3759df, op0=ALU.mult, op1=ALU.add), reads=[yk], writes=[yk])
            for _ in range(3):
                S.add(eng, lambda e: e.tensor_tensor(out=t, in0=y, in1=y, op=ALU.mult), reads=[yk], writes=[tk])
                S.add(eng, lambda e: e.scalar_tensor_tensor(out=t, in0=t, scalar=-0.5, in1=v, op0=ALU.mult, op1=ALU.mult),
                      reads=[tk, vk], writes=[tk])
                S.add(eng, lambda e: e.scalar_tensor_tensor(out=y, in0=t, scalar=1.5, in1=y, op0=ALU.add, op1=ALU.mult),
                      reads=[tk, yk], writes=[yk])

        def rsqrt_steps(eng, v, y, t, vk, yk, tk):
            st_ = []
            st_.append(lambda: S.add(eng, lambda e: e.tensor_scalar(out=y.bitcast(I32), in0=v.bitcast(I32), scalar1=1, scalar2=None,
                                                                    op0=ALU.arith_shift_right), reads=[vk], writes=[yk]))
            st_.append(lambda: S.add(eng, lambda e: e.tensor_scalar(out=y.bitcast(I32), in0=y.bitcast(I32), scalar1=-1,
                                                                    scalar2=0x5f3759df, op0=ALU.mult, op1=ALU.add), reads=[yk], writes=[yk]))
            for _ in range(3):
                st_.append(lambda: S.add(eng, lambda e: e.tensor_tensor(out=t, in0=y, in1=y, op=ALU.mult), reads=[yk], writes=[tk]))
                st_.append(lambda: S.add(eng, lambda e: e.scalar_tensor_tensor(out=t, in0=t, scalar=-0.5, in1=v, op0=ALU.mult, op1=ALU.mult),
                                         reads=[tk, vk], writes=[tk]))
                st_.append(lambda: S.add(eng, lambda e: e.scalar_tensor_tensor(out=y, in0=t, scalar=1.5, in1=y, op0=ALU.add, op1=ALU.mult),
                                         reads=[tk, yk], writes=[yk]))
            return st_

        class Stream:
            def __init__(self, name, slots, plan):
                self.name = name
                self.slots = slots
                self.plan = plan
                self.issued = 0

            def _issue_to(self, n):
                while self.issued < min(n, len(self.plan)):
                    i = self.issued
                    sl = i % len(self.slots)
                    S.add("sync", self.plan[i](self.slots[sl]), writes=[(self.name, sl)], dma=True)
                    self.issued += 1

            def get(self, i):
                self._issue_to(i + len(self.slots))
                sl = i % len(self.slots)
                return self.slots[sl], (self.name, sl)

            def prefetch(self, i):
                self._issue_to(i + len(self.slots))

        def norm_pre(xt, xkeys, gain, gkey, bufs):
            junk, ss4, vv4, rstd4, tt4, hb = bufs
            for t in range(4):
                jo = junk if junk is not None else hb[:, t, :]
                jw = ["junk"] if junk is not None else [("hb", t)]
                S.add("scalar", lambda e, t=t, jo=jo: e.activation(out=jo, in_=xt[:, t, :], func=AF.Square,
                                                                   accum_out=ss4[:, t:t + 1]),
                      reads=[xkeys[t]], writes=jw + ["ss4"])
            S.add("vector", lambda e: e.tensor_scalar(out=vv4, in0=ss4, scalar1=1.0 / D, scalar2=EPS, op0=ALU.mult,
                                                      op1=ALU.add), reads=["ss4"], writes=["vv4"])
            rsqrt_ops("vector", vv4, rstd4, tt4, "vv4", "rstd4", "tt4")
            for t in range(4):
                S.add("vector", lambda e, t=t: e.scalar_tensor_tensor(
                    out=hb[:, t, :], in0=xt[:, t, :], scalar=rstd4[:, t:t + 1], in1=gain, op0=ALU.mult, op1=ALU.mult),
                    reads=[xkeys[t], "rstd4", gkey], writes=[("hb", t)])

        def norm_tr(hTb, hkey, bufs, banks=(0, 1)):
            hb = bufs[5]
            for k in range(KD):
                bi = banks[k % 2]

                def emit(e, k=k, bi=bi):
                    for t in range(4):
                        ins = e.transpose(out=bank_bf(bi)[:, t * 128:(t + 1) * 128],
                                          in_=hb[:, t, k * 128:(k + 1) * 128], identity=ident)
                    return ins
                S.add("tensor", emit, reads=[("hb", t) for t in range(4)] + ["ident"], writes=[PS(bi)])
                copy_op("scalar", hTb[:, k, :], bank_bf(bi)[:, 0:512], [PS(bi)], [(hkey, k)])

        def norm_T(xt, xkeys, gain, gkey, hTb, hkey, bufs):
            norm_pre(xt, xkeys, gain, gkey, bufs)
            norm_tr(hTb, hkey, bufs)

        def ffn_block(xt, xkeys, hTb, hkey, j, wgu_s, wd_s, wbase, AT, sg, hook=None, hook_f=8, hooks=None):
            for f in range(NF):
                wsl, wkey = wgu_s.get(wbase[0] + f)
                gb_, ub_ = (f % 2), 2 + (f % 2)

                def emit_g(e, wsl=wsl, b=gb_):
                    for k in range(KD):
                        ins = e.matmul(bank(b), lhsT=wsl[:, 0, k, :], rhs=hTb[:, k, :], start=(k == 0), stop=(k == KD - 1))
                    return ins

                def emit_u(e, wsl=wsl, b=ub_):
                    for k in range(KD):
                        ins = e.matmul(bank(b), lhsT=wsl[:, 1, k, :], rhs=hTb[:, k, :], start=(k == 0), stop=(k == KD - 1))
                    return ins
                hk = [(hkey, k) for k in range(KD)]
                S.add("tensor", emit_g, reads=hk + [wkey], writes=[PS(gb_)])
                S.add("tensor", emit_u, reads=hk + [wkey], writes=[PS(ub_)])
                sgl = sg[f % 2]
                S.add("scalar", lambda e, sgl=sgl, b=gb_: e.activation(out=sgl, in_=bank(b), func=AF.Silu),
                      reads=[PS(gb_)], writes=[("sg", f % 2)])
                S.add("vector", lambda e, sgl=sgl, b=ub_, f=f: e.tensor_tensor(out=AT[:, f, :], in0=sgl, in1=bank(b), op=ALU.mult),
                      reads=[("sg", f % 2), PS(ub_)], writes=[("AT", f)])
                if hook is not None and f == hook_f:
                    hook()
                if hooks is not None and f in hooks:
                    hooks[f]()
            wbase[0] += NF
            for half in range(2):
                for f0 in range(0, NF, 2):
                    dsl, dkey = wd_s.get(wbase[1] + (half * NF + f0) // 2)

                    def emit_d(e, dsl=dsl, f0=f0):
                        for ff in range(2):
                            f = f0 + ff
                            for t in range(4):
                                ins = e.matmul(bank(4 + t), lhsT=AT[:, f, t * 128:(t + 1) * 128], rhs=dsl[:, ff, :],
                                               start=(f == 0), stop=(f == NF - 1))
                        return ins
                    S.add("tensor", emit_d, reads=[("AT", f0), ("AT", f0 + 1), dkey], writes=[PS(4 + t) for t in range(4)])
                for t in range(4):
                    sl = xt[:, t, half * 512:(half + 1) * 512]
                    S.add("vector", lambda e, sl=sl, t=t: e.scalar_tensor_tensor(
                        out=sl, in0=bank(4 + t), scalar=0.5, in1=sl, op0=ALU.mult, op1=ALU.add),
                        reads=[PS(4 + t), xkeys[t]], writes=[xkeys[t]])
            wbase[1] += NF

        def wgu_plan(j, nblocks):
            plan = []
            for _ in range(nblocks):
                for f in range(NF):
                    plan.append(lambda dst, f=f: (lambda e: e.dma_start(
                        out=dst, in_=wgu_d[j][f].rearrange("p (g k c) -> p g k c", g=2, k=KD, c=128))))
            return plan

        def wd_plan(j, nblocks):
            plan = []
            for _ in range(nblocks):
                for half in range(2):
                    for f0 in range(0, NF, 2):
                        plan.append(lambda dst, half=half, f0=f0: (lambda e: e.dma_start(
                            out=dst, in_=wd_d[j][half, f0:f0 + 2].rearrange("f p x -> p f x"))))
            return plan

        NSTOP = stop_after
        for si, sq in enumerate(seqs):
            T, TQ = sq["T"], sq["TQ"]
            NBLK = T // 512
            NQB = TQ // 512
            NKB = T // 128
            A.cur = persist_mark
            xin = A.alloc([128, 4, D], F32)
            hb = A.alloc([128, 4, D], BF16)
            hTf = A.alloc([128, KD, 512], BF16)
            hTm = A.alloc([128, KD, 512], BF16)
            AT = A.alloc([128, NF, 512], BF16)
            wgu_slots = [A.alloc([128, 2, KD, 128], BF16) for _ in range(4)]
            wd_slots = [A.alloc([128, 2, 512], BF16) for _ in range(3)]
            win_sb = A.alloc([128, KD, INW], BF16)
            xq2 = [A.alloc([128, NH, 64], F32) for _ in range(2)]
            tq = A.alloc([128, NH, 64], F32)
            qkn = A.alloc([128, 4, QKW], BF16)
            qkT = A.alloc([128, 13, 512], BF16)
            vst = A.alloc([128, 4, VW], BF16)
            tabt = [A.alloc([128, 4, 80], F32) for _ in range(3)]
            sg = [A.alloc([128, 512], F32) for _ in range(2)]
            ss4 = A.alloc([128, 4], F32)
            vv4 = A.alloc([128, 4], F32)
            rstd4 = A.alloc([128, 4], F32)
            tt4 = A.alloc([128, 4], F32)
            ssq = A.alloc([128, NH], F32)
            vvq = A.alloc([128, NH], F32)
            rq = A.alloc([128, NH], F32)
            ttq = A.alloc([128, NH], F32)
            rA = [A.alloc([128, 10, 32], F32) for _ in range(4)]
            rB = [A.alloc([128, 16, 8], F32) for _ in range(4)]
            nbufs = (None, ss4, vv4, rstd4, tt4, hb)
            xkeys = [("xin", t) for t in range(4)]

            wgu_s = Stream("wgu", wgu_slots, wgu_plan(1, NBLK))
            wd_s = Stream("wd", wd_slots, wd_plan(1, NBLK))
            wbase = [0, 0]
            own_chunks = [(0, 512), (512, 512), (1024, 512), (1536, 512), (2048, 256)]
            rest_chunks = [(512, 512), (1024, 128), (1664, 512), (2176, 128)]

            def load_x(b, sq=sq, xin=xin):
                S.add("sync", lambda e: e.dma_start(
                    out=xin, in_=sq["x"][b * 512:(b + 1) * 512, :].rearrange("(t p) d -> p t d", p=128)),
                    writes=xkeys, dma=True)

            def load_tab(b, sq=sq, tabt=tabt):
                S.add("sync", lambda e: e.dma_start(
                    out=tabt[b % 3], in_=sq["tab"][b * 512:(b + 1) * 512, :].rearrange("(t p) d -> p t d", p=128)),
                    writes=[("tab", b % 3)], dma=True)

            load_x(0)
            load_tab(0)
            for k0 in range(0, KD, 4):
                S.add("sync", lambda e, k0=k0: e.dma_start(out=win_sb[:, k0:k0 + 4, :], in_=win_d[:, k0:k0 + 4, :]),
                      writes=[("win_sb", k0)], dma=True)
            win_keys = [("win_sb", 0), ("win_sb", 4)]
            bank_ctr = [0]
            t3_pending = [None]

            def run_t3():
                if t3_pending[0] is not None:
                    f = t3_pending[0]
                    t3_pending[0] = None
                    f()

            def proj_mm(b, t, chunks):
                xq = xq2[t % 2]
                xqf = xq.rearrange("p h d -> p (h d)")
                for ci, (c0, ncol) in enumerate(chunks):
                    bi = 2 + bank_ctr[0] % 6
                    bank_ctr[0] += 1

                    def emit_p(e, c0=c0, ncol=ncol, bi=bi):
                        for k in range(KD):
                            ins = e.matmul(bank(bi)[:, 0:ncol], lhsT=hTm[:, k, t * 128:(t + 1) * 128],
                                           rhs=win_sb[:, k, c0:c0 + ncol], start=(k == 0), stop=(k == KD - 1))
                        return ins
                    S.add("tensor", emit_p, reads=[("hTm", k) for k in range(KD)] + win_keys, writes=[PS(bi)])
                    a0, a1 = c0, min(c0 + ncol, QKW)
                    b0, b1 = max(c0, QKW), c0 + ncol
                    if a1 > a0:
                        S.add("scalar", lambda e, bi=bi, a0=a0, a1=a1, c0=c0: e.copy(
                            out=xqf[:, a0:a1], in_=bank(bi)[:, a0 - c0:a1 - c0]),
                            reads=[PS(bi)], writes=[("xq", t % 2, ci)])
                    if b1 > b0:
                        S.add("scalar", lambda e, bi=bi, b0=b0, b1=b1, c0=c0: e.copy(
                            out=vst[:, t, b0 - QKW:b1 - QKW], in_=bank(bi)[:, b0 - c0:b1 - c0]),
                            reads=[PS(bi)], writes=[("vst", t)])

            def qk_chain(b, t, chunks, h0, h1, tsl, tkey):
                xq = xq2[t % 2]
                xqk = [("xq", t % 2, ci) for ci in range(len(chunks))]
                H = h1 - h0
                X = xq[:, h0:h1, :]
                Tq = tq[:, h0:h1, :]
                S.add("scalar", lambda e: e.activation(out=Tq, in_=X, func=AF.Square), reads=xqk, writes=["tq"])
                S.add("vector", lambda e: e.tensor_reduce(out=ssq[:, h0:h1], in_=Tq, axis=AX.X, op=ALU.add),
                      reads=["tq"], writes=["ssq"])
                S.add("vector", lambda e: e.tensor_scalar(out=vvq[:, h0:h1], in0=ssq[:, h0:h1], scalar1=1.0 / 64,
                                                          scalar2=EPS, op0=ALU.mult, op1=ALU.add),
                      reads=["ssq"], writes=["vvq"])
                rsqrt_ops("vector", vvq[:, h0:h1], rq[:, h0:h1], ttq[:, h0:h1], "vvq", "rq", "ttq")
                S.add("gpsimd", lambda e: e.tensor_tensor(out=X, in0=X, in1=gall[:, h0:h1, :], op=ALU.mult),
                      reads=xqk + gall_keys + ["tq"], writes=xqk)
                S.add("vector", lambda e: e.tensor_tensor(
                    out=X, in0=X, in1=rq[:, h0:h1].unsqueeze(2).broadcast_to([128, H, 64]), op=ALU.mult),
                    reads=xqk + ["rq"], writes=xqk)
                qv = qkn[:, t, :].rearrange("p (h d) -> p h d", h=NH, d=64)
                for (r0, r1, m, cofs, rbuf, rkey) in ((h0, 10, 32, 0, rA, "rA"), (10, h1, 8, 64, rB, "rB")):
                    Hr = r1 - r0
                    x1 = xq[:, r0:r1, 0:m]
                    x2 = xq[:, r0:r1, m:2 * m]
                    cs = tsl[:, t, cofs:cofs + m].unsqueeze(1).broadcast_to([128, Hr, m])
                    sn = tsl[:, t, cofs + m:cofs + 2 * m].unsqueeze(1).broadcast_to([128, Hr, m])
                    bA, bB, bC, bD = [rb[:, 0:Hr, :] for rb in rbuf]
                    S.add("vector", lambda e, bA=bA, x1=x1, cs=cs: e.tensor_tensor(out=bA, in0=x1, in1=cs, op=ALU.mult),
                          reads=xqk + [tkey], writes=[(rkey, 0)])
                    S.add("vector", lambda e, bB=bB, x2=x2, sn=sn: e.tensor_tensor(out=bB, in0=x2, in1=sn, op=ALU.mult),
                          reads=xqk + [tkey], writes=[(rkey, 1)])
                    S.add("vector", lambda e, bA=bA, bB=bB, r0=r0, r1=r1, m=m: e.tensor_tensor(
                        out=qv[:, r0:r1, 0:m], in0=bA, in1=bB, op=ALU.subtract),
                        reads=[(rkey, 0), (rkey, 1)], writes=[("qkn", t, rkey, 0)])
                    S.add("gpsimd", lambda e, bC=bC, x2=x2, cs=cs: e.tensor_tensor(out=bC, in0=x2, in1=cs, op=ALU.mult),
                          reads=xqk + [tkey], writes=[(rkey, 2)])
                    S.add("gpsimd", lambda e, bD=bD, x1=x1, sn=sn: e.tensor_tensor(out=bD, in0=x1, in1=sn, op=ALU.mult),
                          reads=xqk + [tkey], writes=[(rkey, 3)])
                    S.add("gpsimd", lambda e, bC=bC, bD=bD, r0=r0, r1=r1, m=m: e.tensor_tensor(
                        out=qv[:, r0:r1, m:2 * m], in0=bC, in1=bD, op=ALU.add),
                        reads=[(rkey, 2), (rkey, 3)], writes=[("qkn", t, rkey, 1)])
                    if m == 8:
                        S.add("gpsimd", lambda e, r0=r0, r1=r1: e.tensor_copy(out=qv[:, r0:r1, 16:64], in_=xq[:, r0:r1, 16:64]),
                              reads=xqk, writes=[("qkn", t, rkey, 2)])

            def make_t3(b, own, sq=sq):
                def t3():
                    qkn_keys = [("qkn", t, rk, i) for t in range(4) for rk, n in (("rA", 2), ("rB", 3)) for i in range(n)]
                    groups = list(range(13)) if own else list(range(4, 9))
                    for r in groups:
                        bi = 4 + r % 2

                        def emit_t(e, r=r, bi=bi):
                            for t in range(4):
                                ins = e.transpose(out=bank_bf(bi)[:, t * 128:(t + 1) * 128],
                                                  in_=qkn[:, t, r * 128:(r + 1) * 128], identity=ident)
                            return ins
                        S.add("tensor", emit_t, reads=qkn_keys + ["ident"], writes=[PS(bi)])
                        copy_op("scalar" if r % 2 == 0 else "vector", qkT[:, r, :], bank_bf(bi)[:, 0:512],
                                [PS(bi)], [("qkT", r)])
                    blk = slice(b * 512, (b + 1) * 512)
                    if own:
                        S.add("gpsimd", lambda e: e.dma_start(
                            out=sq["QTA"][:, :, blk].rearrange("g r t -> r g t"), in_=qkT[:, 0:4, :]),
                            reads=[("qkT", r) for r in range(0, 4)], writes=[("QTA", b)], dma=True)
                        S.add("gpsimd", lambda e: e.dma_start(
                            out=sq["QTB"][:, :, blk].rearrange("g r t -> r g t"), in_=qkT[:, 9:13, :]),
                            reads=[("qkT", r) for r in range(9, 13)], writes=[("QTB", b)], dma=True)
                    S.add("gpsimd", lambda e: e.dma_start(out=sq["KTA"][:, blk], in_=qkT[:, 4, :]),
                          reads=[("qkT", 4)], writes=[("KTA", b)], dma=True)
                    S.add("gpsimd", lambda e: e.dma_start(
                        out=sq["KTB"][:, :, blk].rearrange("g r t -> r g t"), in_=qkT[:, 5:9, :]),
                        reads=[("qkT", r) for r in range(5, 9)], writes=[("KTB", b)], dma=True)
                return t3

            norm_pre(xin, xkeys, g1, "g1", nbufs)
            norm_tr(hTf, "hTf", nbufs)
            deferred = {}
            for b in range(NBLK):
                own = b * 512 < TQ
                if b + 1 < NBLK:
                    load_tab(b + 1)
                ffn_block(xin, xkeys, hTf, "hTf", 1, wgu_s, wd_s, wbase, AT, sg, hooks=deferred)
                deferred = {}
                if own:
                    S.add("gpsimd", lambda e, b=b: e.dma_start(
                        out=sq["x1"][b * 512:(b + 1) * 512, :].rearrange("(t p) d -> p t d", p=128), in_=xin),
                        reads=xkeys, writes=[("x1", b)], dma=True)
                norm_pre(xin, xkeys, gm, "gm", nbufs)
                norm_tr(hTm, "hTm", nbufs)
                if b + 1 < NBLK:
                    load_x(b + 1)
                    norm_pre(xin, xkeys, g1, "g1", nbufs)
                chunks = own_chunks if own else rest_chunks
                h0, h1 = (0, NH) if own else (8, 18)
                tsl = tabt[b % 3]
                tkey = ("tab", b % 3)
                proj_mm(b, 0, chunks)
                qk_chain(b, 0, chunks, h0, h1, tsl, tkey)
                proj_mm(b, 1, chunks)
                qk_chain(b, 1, chunks, h0, h1, tsl, tkey)
                if b + 1 < NBLK:
                    norm_tr(hTf, "hTf", nbufs)
                proj_mm(b, 2, chunks)
                proj_mm(b, 3, chunks)
                blk = slice(b * 512, (b + 1) * 512)
                vkeys = [("vst", t) for t in range(4)]
                S.add("gpsimd", lambda e, blk=blk: e.dma_start(
                    out=sq["VA"][blk, :].rearrange("(t p) c -> p t c", p=128), in_=vst[:, :, 0:128]),
                    reads=vkeys, writes=[("VA", b)], dma=True)
                S.add("gpsimd", lambda e, blk=blk: e.dma_start(
                    out=sq["VB"][blk, :].rearrange("(t p) c -> p t c", p=128), in_=vst[:, :, 128:VW]),
                    reads=vkeys, writes=[("VB", b)], dma=True)

                def mk(t, b=b, chunks=chunks, h0=h0, h1=h1, tsl=tsl, tkey=tkey):
                    return lambda: qk_chain(b, t, chunks, h0, h1, tsl, tkey)
                deferred = {1: mk(2), 5: mk(3), 10: make_t3(b, own)}
            for f in sorted(deferred):
                deferred[f]()
            S.barrier()
            if NSTOP == 1:
                continue

            A.cur = persist_mark
            KTA_sb = A.alloc([128, T], BF16)
            VAaug = A.alloc([128, NKB, 2, 65], BF16)
            KTB_sb = [A.alloc([128, T], BF16) for _ in range(2)]
            VB_sb = [A.alloc([128, NKB, 128], BF16) for _ in range(2)]
            QT = [A.alloc([128, 512], BF16) for _ in range(3)]
            ET = [A.alloc([128, 1024], BF16) for _ in range(4)]
            zacc = A.alloc([128, 512], F32)
            rz1 = A.alloc([128, 512], F32)
            rz2 = A.alloc([128, 512], F32)
            bcs = A.alloc([128, 512], F32)
            t1 = A.alloc([128, 512], F32)
            t2 = A.alloc([128, 512], F32)
            obf = A.alloc([128, 512], F32)
            sqf = A.alloc([128, 512], F32)
            vvb = A.alloc([128, 512], F32)
            rsd = A.alloc([128, 512], F32)
            ttb = A.alloc([128, 512], F32)
            obb = [A.alloc([128, 512], BF16) for _ in range(2)]

            S.add("sync", lambda e: e.dma_start(out=KTA_sb, in_=sq["KTA"]), writes=["KTA_sb"], dma=True)
            S.add("gpsimd", lambda e: e.memset(VAaug[:, :, :, 64:65], 1.0), writes=["VAones"])
            KG = 8
            for kb0 in range(0, NKB, KG):
                kb1 = min(NKB, kb0 + KG)
                for kv in range(2):
                    S.add("sync", lambda e, kb0=kb0, kb1=kb1, kv=kv: e.dma_start(
                        out=VAaug[:, kb0:kb1, kv, 0:64],
                        in_=sq["VA"][kb0 * 128:kb1 * 128, kv * 64:(kv + 1) * 64].rearrange("(k p) d -> p k d", p=128)),
                        writes=[("VAaug", kb0, kv)], dma=True)
            va_keys = [("VAaug", kb0, kv) for kb0 in range(0, NKB, KG) for kv in range(2)] + ["VAones"]

            def load_B(h, sq=sq, KTB_sb=KTB_sb, VB_sb=VB_sb, NKB=NKB):
                sl = h % 2
                S.add("sync", lambda e: e.dma_start(out=KTB_sb[sl], in_=sq["KTB"][h]), writes=[("KTB_sb", sl)], dma=True)
                for kb0 in range(0, NKB, KG):
                    kb1 = min(NKB, kb0 + KG)
                    S.add("sync", lambda e, kb0=kb0, kb1=kb1: e.dma_start(
                        out=VB_sb[sl][:, kb0:kb1, :],
                        in_=sq["VB"][kb0 * 128:kb1 * 128, h * 128:(h + 1) * 128].rearrange("(k p) c -> p k c", p=128)),
                        writes=[("VB_sb", sl, kb0)], dma=True)

            ctr = {"qt": 0, "et": 0, "sp": 0, "kb": 0}
            osb = A.alloc([128, 4, 512], F32)
            if si == 0:
                b_stage = [A.alloc([128, 3072], F32) for _ in range(2)]
                b_stage_bf = [A.alloc([128, 3072], BF16) for _ in range(2)]
                b_Tg = A.alloc([128, NF, KD * 128], BF16)
                b_ffn, _b_win, b_mix = make_conv(b_stage, b_stage_bf, b_Tg, ("gpsimd",))
                bg_steps.extend(b_ffn(2) + b_mix())
            pending_fin = [None]
            fin_steps = []

            def run_pending():
                if pending_fin[0] is not None:
                    f = pending_fin[0]
                    pending_fin[0] = None
                    r = f()
                    if r:
                        fin_steps.extend(r)

            def flush_steps():
                while fin_steps:
                    fin_steps.pop(0)()

            def attn_unit(qsrc, lhs_lo, lhs_hi, lkeys, pv_emit, pv_reads, pv_writes, post_exp=None):
                qi = ctr["qt"] % 3
                ctr["qt"] += 1
                qs, qkey = QT[qi], ("QT", qi)
                S.add("sync", lambda e: e.dma_start(out=qs, in_=qsrc), writes=[qkey], dma=True)
                pend = None
                for kb in range(NKB):
                    sp = ctr["sp"] % 2
                    ctr["sp"] += 1
                    ksl = slice(kb * 128, (kb + 1) * 128)

                    def emit_qk(e, sp=sp, ksl=ksl):
                        e.matmul(pp[sp][:, 0:512], lhsT=lhs_lo[:, ksl], rhs=qs[0:64, :], start=True, stop=True)
                        return e.matmul(pp[sp][:, 512:1024], lhsT=lhs_hi[:, ksl], rhs=qs[64:128, :], start=True, stop=True)
                    S.add("tensor", emit_qk, reads=lkeys + [qkey], writes=[PS(2 * sp), PS(2 * sp + 1)])
                    ei = ctr["et"] % 4
                    ctr["et"] += 1
                    es, ekey = ET[ei], ("ET", ei)
                    S.add("scalar", lambda e, sp=sp, es=es: e.activation(out=es, in_=pp[sp][:, :], func=AF.Exp, scale=SCALE),
                          reads=[PS(2 * sp), PS(2 * sp + 1)], writes=[ekey])
                    if post_exp is not None:
                        post_exp(kb, es, ekey)
                    if pend is not None:
                        S.add("tensor", pend[0], reads=pend[1], writes=pv_writes)
                    pend = (pv_emit(kb, es), pv_reads + [ekey])
                    if kb == min(3, NKB - 1):
                        run_pending()
                    if kb >= 3 and fin_steps:
                        fin_steps.pop(0)()
                    ctr["kb"] += 1
                    if bg_steps and ctr["kb"] % 16 == 8:
                        bg_steps.pop(0)()
                S.add("tensor", pend[0], reads=pend[1], writes=pv_writes)
                flush_steps()

            load_B(0)
            for g in range(4):
                for qb in range(NQB):
                    qsl = slice(qb * 512, (qb + 1) * 512)

                    def pv_emit_A(kb, es):
                        def f(e):
                            e.matmul(bank(4)[0:65, :], lhsT=VAaug[:, kb, 0, :], rhs=es[:, 0:512],
                                     start=(kb == 0), stop=(kb == NKB - 1))
                            return e.matmul(bank(5)[0:65, :], lhsT=VAaug[:, kb, 1, :], rhs=es[:, 512:1024],
                                            start=(kb == 0), stop=(kb == NKB - 1))
                        return f
                    attn_unit(sq["QTA"][g][:, qsl], KTA_sb[0:64, :], KTA_sb[64:128, :], ["KTA_sb"],
                              pv_emit_A, va_keys, [PS(4), PS(5)])
                    for kv in range(2):
                        S.add("vector", lambda e, kv=kv: e.tensor_copy(out=osb[0:65, kv, :], in_=bank(4 + kv)[0:65, :]),
                              reads=[PS(4 + kv)], writes=[("osb", kv)])

                    def fin_A(g=g, qb=qb, qsl=qsl):
                        sp = ctr["sp"] % 2
                        ctr["sp"] += 1
                        S.add("vector", lambda e: e.reciprocal(out=rz1[64:65, :], in_=osb[64:65, 0, :]),
                              reads=[("osb", 0)], writes=["rz1"])
                        S.add("vector", lambda e: e.reciprocal(out=rz2[64:65, :], in_=osb[64:65, 1, :]),
                              reads=[("osb", 1)], writes=["rz2"])

                        def emit_bc(e, sp=sp):
                            e.matmul(pp[sp][0:64, 0:512], lhsT=ones_f[64:65, 0:64], rhs=rz1[64:65, :], start=True, stop=True)
                            return e.matmul(pp[sp][0:64, 512:1024], lhsT=ones_f[64:65, 0:64], rhs=rz2[64:65, :],
                                            start=True, stop=True)
                        S.add("tensor", emit_bc, reads=["rz1", "rz2", "ones_f"], writes=[PS(2 * sp), PS(2 * sp + 1)])
                        for kv in range(2):
                            S.add("vector", lambda e, kv=kv, sp=sp: e.tensor_tensor(
                                out=obb[kv][0:64, :], in0=osb[0:64, kv, :], in1=pp[sp][0:64, kv * 512:(kv + 1) * 512],
                                op=ALU.mult), reads=[("osb", kv), PS(2 * sp + kv)], writes=[("obb", kv)])
                            hq = kv * 4 + g
                            S.add("gpsimd", lambda e, kv=kv, hq=hq: e.dma_start(
                                out=sq["OAT"][hq * 64:(hq + 1) * 64, qsl], in_=obb[kv][0:64, :]),
                                reads=[("obb", kv)], writes=[("OAT", hq, qb)], dma=True)
                    pending_fin[0] = fin_A
            for h in range(4):
                sl = h % 2
                if h + 1 < 4:
                    load_B(h + 1)
                vb_keys = [("VB_sb", sl, kb0) for kb0 in range(0, NKB, KG)]
                for qb in range(NQB):
                    qsl = slice(qb * 512, (qb + 1) * 512)

                    def pv_emit_B(kb, es, sl=sl):
                        def f(e):
                            e.matmul(bank(4), lhsT=VB_sb[sl][:, kb, :], rhs=es[:, 0:512], start=(kb == 0), stop=(kb == NKB - 1))
                            e.matmul(bank(5), lhsT=VB_sb[sl][:, kb, :], rhs=es[:, 512:1024], start=(kb == 0), stop=(kb == NKB - 1))
                            return e.matmul(bank(6), lhsT=ones_bf, rhs=es[:, 0:512], start=(kb == 0), stop=(kb == NKB - 1))
                        return f

                    def post_exp_B(kb, es, ekey):
                        src = es[:, 512:1024]
                        if kb == 0:
                            S.add("vector", lambda e: e.tensor_copy(out=zacc, in_=src), reads=[ekey], writes=["zacc"])
                        else:
                            S.add("vector", lambda e: e.tensor_tensor(out=zacc, in0=zacc, in1=src, op=ALU.add),
                                  reads=[ekey, "zacc"], writes=["zacc"])
                    attn_unit(sq["QTB"][h][:, qsl], KTB_sb[sl][0:64, :], KTB_sb[sl][64:128, :], [("KTB_sb", sl)],
                              pv_emit_B, vb_keys + ["ones_bf"], [PS(4), PS(5), PS(6)], post_exp=post_exp_B)
                    S.add("tensor", lambda e: e.matmul(bank(7), lhsT=ones_f, rhs=zacc, start=True, stop=True),
                          reads=["zacc", "ones_f"], writes=[PS(7)])
                    S.add("scalar", lambda e: e.copy(out=osb[:, 0, :], in_=bank(4)), reads=[PS(4)], writes=[("osb", 0)])
                    S.add("scalar", lambda e: e.copy(out=osb[:, 1, :], in_=bank(5)), reads=[PS(5)], writes=[("osb", 1)])
                    S.add("vector", lambda e: e.reciprocal(out=rz1, in_=bank(6)), reads=[PS(6)], writes=["rz1"])
                    S.add("vector", lambda e: e.reciprocal(out=rz2, in_=bank(7)), reads=[PS(7)], writes=["rz2"])

                    def fin_B(h=h, qb=qb, qsl=qsl):
                        st_ = []
                        st_.append(lambda: S.add("vector", lambda e: e.tensor_tensor(out=t1, in0=osb[:, 0, :], in1=rz1, op=ALU.mult),
                                                 reads=[("osb", 0), "rz1"], writes=["t1"]))
                        st_.append(lambda: S.add("gpsimd", lambda e: e.tensor_tensor(out=t2, in0=osb[:, 1, :], in1=rz2, op=ALU.mult),
                                                 reads=[("osb", 1), "rz2"], writes=["t2"]))
                        st_.append(lambda: S.add("vector", lambda e: e.scalar_tensor_tensor(
                            out=obf, in0=t2, scalar=neglam[:, 0:1], in1=t1, op0=ALU.mult, op1=ALU.add),
                            reads=["t1", "t2", "neglam"], writes=["obf"]))
                        st_.append(lambda: S.add("gpsimd", lambda e: e.tensor_tensor(out=sqf, in0=obf, in1=obf, op=ALU.mult),
                                                 reads=["obf"], writes=["sqf"]))
                        slot = {}

                        def ss_mm():
                            sp = ctr["sp"] % 2
                            ctr["sp"] += 1
                            slot["sp"] = sp
                            S.add("tensor", lambda e: e.matmul(pp[sp][:, 0:512], lhsT=ones_f, rhs=sqf, start=True, stop=True),
                                  reads=["sqf", "ones_f"], writes=[PS(2 * sp), PS(2 * sp + 1)])
                        st_.append(ss_mm)

                        def vv():
                            sp = slot["sp"]
                            S.add("vector", lambda e: e.tensor_scalar(out=vvb, in0=pp[sp][:, 0:512], scalar1=1.0 / 128,
                                                                      scalar2=EPS, op0=ALU.mult, op1=ALU.add),
                                  reads=[PS(2 * sp)], writes=["vvb"])
                        st_.append(vv)
                        st_.extend(rsqrt_steps("vector", vvb, rsd, ttb, "vvb", "rsd", "ttb"))
                        st_.append(lambda: S.add("vector", lambda e: e.scalar_tensor_tensor(
                            out=obb[0], in0=obf, scalar=gout[:, 0:1], in1=rsd, op0=ALU.mult, op1=ALU.mult),
                            reads=["obf", "rsd", "gout"], writes=[("obb", 0)]))
                        st_.append(lambda: S.add("gpsimd", lambda e: e.dma_start(
                            out=sq["OBT"][h * 128:(h + 1) * 128, qsl], in_=obb[0]),
                            reads=[("obb", 0)], writes=[("OBT", h, qb)], dma=True))
                        return st_
                    pending_fin[0] = fin_B
            run_pending()
            flush_steps()
            while bg_steps:
                bg_steps.pop(0)()
            S.barrier()
            if NSTOP == 2:
                continue

            A.cur = persist_mark
            xin2 = [A.alloc([128, 4, D], F32) for _ in range(2)]
            hb = A.alloc([128, 4, D], BF16)
            hT = A.alloc([128, KD, 512], BF16)
            hTf2 = A.alloc([128, KD, 512], BF16)
            AT = A.alloc([128, NF, 512], BF16)
            wgu_slots = [A.alloc([128, 2, KD, 128], BF16) for _ in range(4)]
            wd_slots = [A.alloc([128, 2, 512], BF16) for _ in range(4)]
            wc_slots = [A.alloc([128, 24, 128], BF16) for _ in range(3)]
            wout_sb = A.alloc([128, KD, D], BF16)
            oaT = A.alloc([128, 4, 512], BF16)
            obT = A.alloc([128, 4, 512], BF16)
            mT = A.alloc([128, KD, 512], BF16)
            tha = [A.alloc([128, 512], F32) for _ in range(2)]
            thb = [A.alloc([128, 512], F32) for _ in range(2)]
            u1 = A.alloc([128, 512], F32)
            u2 = A.alloc([128, 512], F32)
            sg = [A.alloc([128, 512], F32) for _ in range(2)]
            junk = A.alloc([128, D], BF16)
            ss4 = A.alloc([128, 4], F32)
            vv4 = A.alloc([128, 4], F32)
            rstd4 = A.alloc([128, 4], F32)
            tt4 = A.alloc([128, 4], F32)
            nbufs = (junk, ss4, vv4, rstd4, tt4, hb)
            wgu_s = Stream("wgu", wgu_slots, wgu_plan(2, NQB))
            wd_s = Stream("wd", wd_slots, wd_plan(2, NQB))
            wbase = [0, 0]
            wc_plan = []
            for _ in range(NQB):
                for c in range(8):
                    wc_plan.append(lambda dst, c=c: (lambda e: e.dma_start(out=dst, in_=wc_d[c])))
            wc_s = Stream("wc", wc_slots, wc_plan)
            for k0 in range(0, KD, 4):
                S.add("sync", lambda e, k0=k0: e.dma_start(out=wout_sb[:, k0:k0 + 4, :], in_=wout_d[:, k0:k0 + 4, :]),
                      writes=[("wout_sb", k0)], dma=True)
            wout_keys = [("wout_sb", 0), ("wout_sb", 4)]

            def load_x1(b, sq=sq, xin2=xin2):
                S.add("sync", lambda e: e.dma_start(
                    out=xin2[b % 2], in_=sq["x1"][b * 512:(b + 1) * 512, :].rearrange("(t p) d -> p t d", p=128)),
                    writes=[("xin", b % 2, t) for t in range(4)], dma=True)

            def load_o(b, sq=sq, oaT=oaT, obT=obT):
                blk = slice(b * 512, (b + 1) * 512)
                S.add("sync", lambda e: e.dma_start(
                    out=oaT, in_=sq["OAT"][:, blk].rearrange("(c p) t -> p c t", p=128)), writes=["oaT"], dma=True)
                S.add("sync", lambda e: e.dma_start(
                    out=obT, in_=sq["OBT"][:, blk].rearrange("(c p) t -> p c t", p=128)), writes=["obT"], dma=True)

            load_x1(0)
            norm_pre(xin2[0], [("xin", 0, t) for t in range(4)], gm, "gm", nbufs)
            for b in range(NQB):
                blk = slice(b * 512, (b + 1) * 512)
                xt = xin2[b % 2]
                xkeys = [("xin", b % 2, t) for t in range(4)]
                if b + 1 < NQB:
                    load_x1(b + 1)
                if b == 0:
                    load_o(0)
                norm_tr(hT, "hT", nbufs)
                hk = [("hT", k) for k in range(KD)]
                for c in range(8):
                    wsl, wkey = wc_s.get(b * 8 + c)
                    bs = (c % 2) * 4

                    def emit_g(e, wsl=wsl, bs=bs):
                        for k in range(KD):
                            e.matmul(bank(bs), lhsT=wsl[:, k, :], rhs=hT[:, k, :], start=(k == 0), stop=(k == KD - 1))
                        for k in range(KD):
                            e.matmul(bank(bs + 1), lhsT=wsl[:, 8 + k, :], rhs=hT[:, k, :], start=(k == 0), stop=(k == KD - 1))
                        for k in range(4):
                            e.matmul(bank(bs + 2), lhsT=wsl[:, 16 + k, :], rhs=oaT[:, k, :], start=(k == 0), stop=(k == 3))
                        for k in range(4):
                            ins = e.matmul(bank(bs + 3), lhsT=wsl[:, 20 + k, :], rhs=obT[:, k, :], start=(k == 0), stop=(k == 3))
                        return ins
                    S.add("tensor", emit_g, reads=hk + [wkey, "oaT", "obT"], writes=[PS(bs + i) for i in range(4)])
                    ta, tb = tha[c % 2], thb[c % 2]
                    S.add("scalar", lambda e, ta=ta, bs=bs: e.activation(out=ta, in_=bank(bs), func=AF.Tanh, scale=0.5),
                          reads=[PS(bs)], writes=[("tha", c % 2)])
                    S.add("scalar", lambda e, tb=tb, bs=bs: e.activation(out=tb, in_=bank(bs + 1), func=AF.Tanh, scale=0.5),
                          reads=[PS(bs + 1)], writes=[("thb", c % 2)])
                    S.add("vector", lambda e, ta=ta, bs=bs: e.scalar_tensor_tensor(
                        out=u1, in0=ta, scalar=1.0, in1=bank(bs + 2), op0=ALU.add, op1=ALU.mult),
                        reads=[("tha", c % 2), PS(bs + 2)], writes=["u1"])
                    S.add("vector", lambda e, tb=tb, bs=bs: e.scalar_tensor_tensor(
                        out=u2, in0=tb, scalar=1.0, in1=bank(bs + 3), op0=ALU.add, op1=ALU.mult),
                        reads=[("thb", c % 2), PS(bs + 3)], writes=["u2"])
                    S.add("gpsimd", lambda e, c=c: e.tensor_tensor(out=mT[:, c, :], in0=u1, in1=u2, op=ALU.add),
                          reads=["u1", "u2"], writes=[("mT", c)])
                mk = [("mT", c) for c in range(8)]
                for t in range(4):
                    for half in range(2):
                        bi = (t * 2 + half) % 8

                        def emit_o(e, t=t, half=half, bi=bi):
                            for k in range(KD):
                                ins = e.matmul(bank(bi), lhsT=mT[:, k, t * 128:(t + 1) * 128],
                                               rhs=wout_sb[:, k, half * 512:(half + 1) * 512], start=(k == 0), stop=(k == KD - 1))
                            return ins
                        S.add("tensor", emit_o, reads=mk + wout_keys, writes=[PS(bi)])
                        xsl = xt[:, t, half * 512:(half + 1) * 512]
                        S.add("vector", lambda e, xsl=xsl, bi=bi: e.scalar_tensor_tensor(
                            out=xsl, in0=bank(bi), scalar=0.5, in1=xsl, op0=ALU.mult, op1=ALU.add),
                            reads=[PS(bi), xkeys[t]], writes=[xkeys[t]])
                if NSTOP != 3:
                    norm_T(xt, xkeys, g2, "g2", hTf2, "hTf2", nbufs)
                    nxt = None
                    if b + 1 < NQB:
                        def nxt(b=b):
                            load_o(b + 1)
                            wc_s.prefetch((b + 1) * 8)
                            norm_pre(xin2[(b + 1) % 2], [("xin", (b + 1) % 2, t) for t in range(4)], gm, "gm", nbufs)
                    ffn_block(xt, xkeys, hTf2, "hTf2", 2, wgu_s, wd_s, wbase, AT, sg, hook=nxt, hook_f=4)
                elif b + 1 < NQB:
                    load_o(b + 1)
                    norm_pre(xin2[(b + 1) % 2], [("xin", (b + 1) % 2, t) for t in range(4)], gm, "gm", nbufs)
                S.add("gpsimd", lambda e, blk=blk, xt=xt: e.dma_start(
                    out=sq["y"][blk, :].rearrange("(t p) d -> p t d", p=128), in_=xt),
                    reads=xkeys, writes=[("y", b)], dma=True)
            S.barrier()
        S.run(block)
    return nc


def _rot_table(pos):
    pos = np.asarray(pos, dtype=np.int64)
    row = (pos // 64).astype(np.float32)
    col = (pos % 64).astype(np.float32)
    invA = (np.float32(10000.0) ** (-np.arange(0, 32, 2, dtype=np.float32) / np.float32(32))).astype(np.float32)
    angA = np.concatenate([row[:, None] * invA[None, :], col[:, None] * invA[None, :]], axis=-1).astype(np.float32)
    invP = (np.float32(500000.0) ** (-np.arange(0, 16, 2, dtype=np.float32) / np.float32(16))).astype(np.float32)
    angP = (pos.astype(np.float32)[:, None] * invP[None, :]).astype(np.float32)
    tab = np.concatenate([np.cos(angA.astype(np.float64)), np.sin(angA.astype(np.float64)),
                          np.cos(angP.astype(np.float64)), np.sin(angP.astype(np.float64))], axis=-1)
    return np.ascontiguousarray(tab.astype(np.float32))


def _win_perm():
    qa = [(kv * 4 + g) * 64 + d for g in range(4) for kv in range(2) for d in range(64)]
    ka = list(range(512, 640))
    va = list(range(640, 768))
    qb = list(range(768, 1280))
    kb = list(range(1280, 1792))
    vb = list(range(1792, 2304))
    return np.array(qa + ka + kb + qb + va + vb, dtype=np.int64)


def make_in_maps(inputs, TS, TPK, TPQ, n_cores=8):
    f = lambda a: np.ascontiguousarray(np.asarray(a, dtype=np.float32))
    xp_all = f(inputs["x_prompt"])
    xs_all = f(inputs["x_sample"])
    nq = TPK // TPQ
    common = {}
    for nm in ("ffn1_norm", "mix_norm", "ffn2_norm", "a_q_norm", "a_k_norm", "b_q_norm", "b_k_norm",
               "b_lambda_q1", "b_lambda_k1", "b_lambda_q2", "b_lambda_k2", "b_out_norm"):
        common[nm] = f(inputs[nm]).reshape(1, -1)
    for nm in ("ffn1_w_gate", "ffn1_w_up", "ffn1_w_down", "ffn2_w_gate", "ffn2_w_up", "ffn2_w_down",
               "w_branch_gate", "w_o_a", "w_o_b", "w_out"):
        a = f(inputs[nm])
        common[nm] = np.ascontiguousarray(a.reshape(a.shape[-2], a.shape[-1]))
    win = f(inputs["w_in"]).reshape(D, INW)
    common["w_in"] = np.ascontiguousarray(win[:, _win_perm()])
    tab_s = _rot_table(np.arange(TS))
    in_maps = []
    for c in range(n_cores):
        pi, q = (c // nq) % xp_all.shape[0], c % nq
        pos_p = np.roll(np.arange(TPK), -q * TPQ)
        m = dict(common)
        m["xs"] = np.ascontiguousarray(xs_all[c])
        m["xp"] = np.ascontiguousarray(xp_all[pi][pos_p])
        m["tab_s"] = tab_s
        m["tab_p"] = _rot_table(pos_p)
        in_maps.append(m)
    return in_maps


_NC_CACHE = {}


def kernel(**inputs):
    TS, TPK, TPQ = 4096, 8192, 2048
    key = (TS, TPK, TPQ)
    if key not in _NC_CACHE:
        _NC_CACHE[key] = build(TS, TPK, TPQ)
    nc = _NC_CACHE[key]
    in_maps = make_in_maps(inputs, TS, TPK, TPQ)
    res = run_bass_kernel_spmd(nc, in_maps, core_ids=list(range(8)))
    y_prompt = np.empty((2, TPK, D), dtype=np.float32)
    y_sample = np.empty((8, TS, D), dtype=np.float32)
    for c in range(8):
        r = res.results[c]
        y_sample[c] = r["ys"]
        pi, q = c // 4, c % 4
        y_prompt[pi, q * TPQ:(q + 1) * TPQ] = r["yp"]
    return (y_prompt, y_sample)
```

```python
import math
import types
from contextlib import ExitStack

import numpy as np
import concourse.bass as bass
import concourse.mybir as mybir
from concourse.bass_utils import run_bass_kernel_spmd

F32 = mybir.dt.float32
BF16 = mybir.dt.bfloat16
I32 = mybir.dt.int32
U8 = mybir.dt.uint8
AF = mybir.ActivationFunctionType
ALU = mybir.AluOpType
AX = mybir.AxisListType

D = 1024
DFF = 2816
NF = DFF // 128
KD = D // 128
INW = 2304
NH = 26
QKW = NH * 64
VW = 640
EPS = 1e-6
SCALE = 0.125
LAM_INIT = 0.8 - 0.6 * math.exp(-0.3 * 0)

ENGS = ("sync", "scalar", "vector", "gpsimd", "tensor")
DMAQ = ("sync", "scalar", "gpsimd")


def _snap(fn):
    if fn is None or getattr(fn, "__closure__", None) is None:
        return fn
    cells = []
    for c in fn.__closure__:
        try:
            cells.append(types.CellType(c.cell_contents))
        except ValueError:
            cells.append(c)
    g = types.FunctionType(fn.__code__, fn.__globals__, fn.__name__, fn.__defaults__, tuple(cells))
    g.__kwdefaults__ = fn.__kwdefaults__
    return g


class Op:
    __slots__ = ("eng", "emit", "deps", "dma", "sigsem", "sigval", "signal")

    def __init__(self, eng, emit, dma):
        self.eng = eng
        self.emit = _snap(emit)
        self.dma = dma
        self.deps = set()
        self.sigsem = None
        self.sigval = 0
        self.signal = False


class Sched:
    def __init__(self, nc, stack, n_dma_sems=12):
        self.nc = nc
        self.ops = {e: [] for e in ENGS}
        self.last_w = {}
        self.readers = {}
        self.csem = {e: stack.enter_context(nc.semaphore("c_" + e)) for e in ENGS if e != "sync"}
        self.dsem = {q: [stack.enter_context(nc.semaphore("d_%s_%d" % (q, i))) for i in range(n_dma_sems)]
                     for q in DMAQ}
        self.dcount = {q: 0 for q in DMAQ}
        self.dlast = {q: {} for q in DMAQ}
        self.since_barrier_dma = []
        self.last_op = {}

    def add(self, eng, emit, reads=(), writes=(), dma=False):
        op = Op(eng, emit, dma)
        deps = op.deps
        for r in reads:
            w = self.last_w.get(r)
            if w is not None:
                deps.add(w)
        for w_ in writes:
            w = self.last_w.get(w_)
            if w is not None:
                deps.add(w)
            rd = self.readers.get(w_)
            if rd:
                deps.update(rd[0].values())
                deps.update(rd[1])
        for r in reads:
            rd = self.readers.get(r)
            if rd is None:
                rd = self.readers[r] = ({}, [])
            if dma:
                rd[1].append(op)
            else:
                rd[0][eng] = op
        for w_ in writes:
            self.last_w[w_] = op
            self.readers[w_] = ({}, [])
        deps.discard(op)
        if dma:
            n = self.dcount[eng]
            sems = self.dsem[eng]
            i = n % len(sems)
            op.sigsem = sems[i]
            op.sigval = 16 * (n // len(sems) + 1)
            prev = self.dlast[eng].get(i)
            if prev is not None:
                deps.add(prev)
            self.dlast[eng][i] = op
            self.dcount[eng] = n + 1
            self.since_barrier_dma.append(op)
        self.ops[eng].append(op)
        if not dma:
            self.last_op[eng] = op
        return op

    def barrier(self):
        deps = set(self.since_barrier_dma)
        for o in self.last_op.values():
            if o is not None and o.emit is not None:
                deps.add(o)
        self.since_barrier_dma = []
        for e in ENGS:
            op = Op(e, None, False)
            op.deps = set(deps)
            self.ops[e].append(op)
        self.last_w = {}
        self.readers = {}

    def finalize(self):
        for e in ENGS:
            for op in self.ops[e]:
                for d in op.deps:
                    if d.dma:
                        continue
                    if d.eng == "tensor" and op.eng == "tensor" and not op.dma:
                        continue
                    d.signal = True
        for e in ENGS:
            if e == "sync":
                continue
            c = 0
            for op in self.ops[e]:
                if op.dma or not op.signal:
                    continue
                assert op.emit is not None
                c += 1
                op.sigsem = self.csem[e]
                op.sigval = c

    def emit_engine(self, ename, eobj):
        waited = {}
        for op in self.ops[ename]:
            needs = {}
            for d in op.deps:
                if (not d.dma) and d.eng == "tensor" and ename == "tensor" and not op.dma:
                    continue
                k = d.sigsem
                if needs.get(k, 0) < d.sigval:
                    needs[k] = d.sigval
            for s, v in needs.items():
                if waited.get(s, 0) < v:
                    eobj.wait_ge(s, v)
                    waited[s] = v
            if op.emit is not None:
                ins = op.emit(eobj)
                if op.dma:
                    ins.then_inc(op.sigsem, 16)
                elif op.signal:
                    ins.then_inc(op.sigsem, 1)

    def run(self, block):
        self.finalize()
        s = self

        @block.sync
        def _(e):
            s.emit_engine("sync", e)

        @block.scalar
        def _(e):
            s.emit_engine("scalar", e)

        @block.vector
        def _(e):
            s.emit_engine("vector", e)

        @block.gpsimd
        def _(e):
            s.emit_engine("gpsimd", e)

        @block.tensor
        def _(e):
            s.emit_engine("tensor", e)


class Arena:
    def __init__(self, t, size):
        self.t = t
        self.size = size
        self.cur = 0

    def alloc(self, shape, dt):
        esz = 4 if dt in (F32, I32) else 2
        n = int(np.prod(shape[1:])) * esz
        off = self.cur
        self.cur = off + (n + 63) // 64 * 64
        assert self.cur <= self.size, ("SBUF arena overflow", self.cur, self.size)
        ap = self.t[:, off:off + n].bitcast(dt)
        if len(shape) == 3:
            ap = ap.rearrange("p (a b) -> p a b", a=shape[1], b=shape[2])
        elif len(shape) == 4:
            ap = ap.rearrange("p (a b c) -> p a b c", a=shape[1], b=shape[2], c=shape[3])
        return ap


def build(TS, TPK, TPQ, debug=False, stop_after=0):
    nc = bass.Bass("TRN2", target_bir_lowering=False)

    def din(name, shape, dt=F32):
        return nc.dram_tensor(name, list(shape), dt, kind="ExternalInput").ap()

    def dout(name, shape, dt=F32):
        return nc.dram_tensor(name, list(shape), dt, kind="ExternalOutput").ap()

    def dscr(name, shape, dt):
        return nc.dram_tensor(name, list(shape), dt, kind="ExternalOutput" if debug else "Internal").ap()

    xs_d = din("xs", [TS, D])
    xp_d = din("xp", [TPK, D])
    tabs_d = din("tab_s", [TS, 80])
    tabp_d = din("tab_p", [TPK, 80])
    W = {}
    for j in (1, 2):
        W["g%d" % j] = din("ffn%d_w_gate" % j, [D, DFF])
        W["u%d" % j] = din("ffn%d_w_up" % j, [D, DFF])
        W["d%d" % j] = din("ffn%d_w_down" % j, [DFF, D])
        W["n%d" % j] = din("ffn%d_norm" % j, [1, D])
    W["nm"] = din("mix_norm", [1, D])
    W["in"] = din("w_in", [D, INW])
    W["bg"] = din("w_branch_gate", [D, 2 * D])
    W["oa"] = din("w_o_a", [512, D])
    W["ob"] = din("w_o_b", [512, D])
    W["out"] = din("w_out", [D, D])
    for nm in ("a_q_norm", "a_k_norm", "b_q_norm", "b_k_norm", "b_lambda_q1", "b_lambda_k1",
               "b_lambda_q2", "b_lambda_k2"):
        W[nm] = din(nm, [1, 64])
    W["b_out_norm"] = din("b_out_norm", [1, 128])
    ys_d = dout("ys", [TS, D])
    yp_d = dout("yp", [TPQ, D])

    wgu_d = {j: dscr("wgu%d" % j, [NF, 128, 2 * KD * 128], BF16) for j in (1, 2)}
    wd_d = {j: dscr("wd%d" % j, [2, NF, 128, 512], BF16) for j in (1, 2)}
    win_d = dscr("win_bf", [128, KD, INW], BF16)
    wc_d = dscr("wc_bf", [8, 128, 24, 128], BF16)
    wout_d = dscr("wout_bf", [128, KD, D], BF16)

    seqs = []
    for nm, xd, tabd, T, TQ, yd in (("s", xs_d, tabs_d, TS, TS, ys_d), ("p", xp_d, tabp_d, TPK, TPQ, yp_d)):
        seqs.append(dict(
            nm=nm, x=xd, tab=tabd, T=T, TQ=TQ, y=yd,
            x1=dscr("x1_" + nm, [TQ, D], F32),
            KTA=dscr("kta_" + nm, [128, T], BF16), KTB=dscr("ktb_" + nm, [4, 128, T], BF16),
            VA=dscr("va_" + nm, [T, 128], BF16), VB=dscr("vb_" + nm, [T, 512], BF16),
            QTA=dscr("qta_" + nm, [4, 128, TQ], BF16), QTB=dscr("qtb_" + nm, [4, 128, TQ], BF16),
            OAT=dscr("oat_" + nm, [512, TQ], BF16), OBT=dscr("obt_" + nm, [512, TQ], BF16)))

    ARENA_BYTES = 207 * 1024
    with ExitStack() as st:
        arena_t = st.enter_context(nc.sbuf_tensor("arena", [128, ARENA_BYTES], U8))
        pp = [st.enter_context(nc.psum_tensor("pp%d" % i, [128, 1024], F32)) for i in range(4)]
        block = st.enter_context(nc.Block())
        S = Sched(nc, st)
        A = Arena(arena_t, ARENA_BYTES)

        def bank(i):
            return pp[i // 2][:, (i % 2) * 512:(i % 2 + 1) * 512]

        def bank_bf(i):
            return bank(i).bitcast(BF16)

        def PS(i):
            return ("ps", i)

        _rr = [0]

        def rr_eng(choices=("vector", "scalar", "gpsimd")):
            _rr[0] += 1
            return choices[_rr[0] % len(choices)]

        def copy_op(eng, out, in_, reads, writes):
            if eng == "scalar":
                S.add("scalar", lambda e: e.copy(out=out, in_=in_), reads=reads, writes=writes)
            else:
                S.add(eng, lambda e: e.tensor_copy(out=out, in_=in_), reads=reads, writes=writes)

        ident = A.alloc([128, 128], BF16)
        ones_bf = A.alloc([128, 128], BF16)
        ones_f = A.alloc([128, 128], F32)
        g1 = A.alloc([128, D], F32)
        gm = A.alloc([128, D], F32)
        g2 = A.alloc([128, D], F32)
        gall = A.alloc([128, NH, 64], F32)
        lamv = A.alloc([128, 4, 64], F32)
        neglam = A.alloc([128, 1], F32)
        gout = A.alloc([128, 1], F32)
        sm1 = A.alloc([128, 8], F32)
        persist_mark = A.cur

        S.add("gpsimd", lambda e: e.memset(ident, 0.0), writes=["ident"])
        S.add("gpsimd", lambda e: e.affine_select(
            out=ident, in_=ident, compare_op=ALU.not_equal, fill=1.0, base=0, pattern=[[-1, 128]],
            channel_multiplier=1), reads=["ident"], writes=["ident"])
        S.add("gpsimd", lambda e: e.memset(ones_bf, 1.0), writes=["ones_bf"])
        S.add("gpsimd", lambda e: e.memset(ones_f, 1.0), writes=["ones_f"])
        for dst, src, key in ((g1, W["n1"], "g1"), (gm, W["nm"], "gm"), (g2, W["n2"], "g2")):
            S.add("sync", lambda e, dst=dst, src=src: e.dma_start(out=dst, in_=src.broadcast_to([128, D])),
                  writes=[key], dma=True)
        for h0, h1, nm in ((0, 8, "a_q_norm"), (8, 10, "a_k_norm"), (10, 18, "b_k_norm"), (18, 26, "b_q_norm")):
            for h in range(h0, h1):
                S.add("sync", lambda e, h=h, nm=nm: e.dma_start(out=gall[:, h, :], in_=W[nm].broadcast_to([128, 64])),
                      writes=[("gall", h)], dma=True)
        gall_keys = [("gall", h) for h in range(NH)]
        for i, nm in enumerate(("b_lambda_q1", "b_lambda_k1", "b_lambda_q2", "b_lambda_k2")):
            S.add("sync", lambda e, i=i, nm=nm: e.dma_start(out=lamv[:, i, :], in_=W[nm].broadcast_to([128, 64])),
                  writes=[("lamv", i)], dma=True)
        S.add("sync", lambda e: e.dma_start(out=gout, in_=W["b_out_norm"].rearrange("o n -> n o")),
              writes=["gout"], dma=True)
        S.add("vector", lambda e: e.tensor_tensor(out=lamv[:, 0, :], in0=lamv[:, 0, :], in1=lamv[:, 1, :], op=ALU.mult),
              reads=[("lamv", 0), ("lamv", 1)], writes=[("lamv", 0)])
        S.add("vector", lambda e: e.tensor_tensor(out=lamv[:, 2, :], in0=lamv[:, 2, :], in1=lamv[:, 3, :], op=ALU.mult),
              reads=[("lamv", 2), ("lamv", 3)], writes=[("lamv", 2)])
        S.add("vector", lambda e: e.tensor_reduce(out=sm1[:, 0:1], in_=lamv[:, 0, :], axis=AX.X, op=ALU.add),
              reads=[("lamv", 0)], writes=["sm1a"])
        S.add("vector", lambda e: e.tensor_reduce(out=sm1[:, 1:2], in_=lamv[:, 2, :], axis=AX.X, op=ALU.add),
              reads=[("lamv", 2)], writes=["sm1b"])
        S.add("scalar", lambda e: e.activation(out=sm1[:, 2:4], in_=sm1[:, 0:2], func=AF.Exp),
              reads=["sm1a", "sm1b"], writes=["sm1e"])
        S.add("vector", lambda e: e.tensor_tensor(out=sm1[:, 4:5], in0=sm1[:, 3:4], in1=sm1[:, 2:3], op=ALU.subtract),
              reads=["sm1e"], writes=["sm1d"])
        S.add("vector", lambda e: e.tensor_scalar(out=neglam, in0=sm1[:, 4:5], scalar1=-LAM_INIT, scalar2=None, op0=ALU.add),
              reads=["sm1d"], writes=["neglam"])
        S.add("vector", lambda e: e.tensor_scalar(out=gout, in0=gout, scalar1=1.0 - LAM_INIT, scalar2=None, op0=ALU.mult),
              reads=["gout"], writes=["gout"])

        def make_conv(stage, stage_bf, Tg, engines):
            cnt = [0]
            rr = [0]

            def eng():
                rr[0] += 1
                return engines[rr[0] % len(engines)]

            def load_cast(src_ap, n, C, dst_bf=None, dst_key=None):
                slot = cnt[0] % 2
                cnt[0] += 1
                sv = stage[slot][:, 0:n * C].rearrange("p (a c) -> p a c", a=n, c=C)
                S.add("sync", lambda e: e.dma_start(out=sv, in_=src_ap.rearrange("(a p) c -> p a c", p=128)),
                      writes=[("stg", slot)], dma=True)
                if dst_bf is None:
                    bv = stage_bf[slot][:, 0:n * C].rearrange("p (a c) -> p a c", a=n, c=C)
                    copy_op(eng(), bv, sv, [("stg", slot)], [("stb", slot)])
                    return bv, ("stb", slot)
                copy_op(eng(), dst_bf, sv[:, 0, :].rearrange("p (f c) -> p f c", f=NF, c=128), [("stg", slot)], [dst_key])
                return dst_bf, dst_key

            def steps_ffn(j):
                st_ = []
                Tv = Tg.rearrange("p f (k c) -> p f k c", k=KD, c=128)
                for gi, key in enumerate(("g%d" % j, "u%d" % j)):
                    for k in range(KD):
                        st_.append(lambda gi=gi, key=key, k=k: load_cast(
                            W[key][k * 128:(k + 1) * 128, :], 1, DFF, dst_bf=Tv[:, :, k, :], dst_key=("Tg", k)))

                    def store_g(gi=gi):
                        for f0 in range(0, NF, 6):
                            f1 = min(NF, f0 + 6)
                            S.add("gpsimd", lambda e, f0=f0, f1=f1: e.dma_start(
                                out=wgu_d[j][f0:f1, :, gi * KD * 128:(gi + 1) * KD * 128].rearrange("f p x -> p f x"),
                                in_=Tg[:, f0:f1, :]),
                                reads=[("Tg", k_) for k_ in range(KD)], writes=[("wgu", j, gi, f0)], dma=True)
                    st_.append(store_g)
                for f0 in range(0, NF, 2):
                    def wd_step(f0=f0):
                        bv, key = load_cast(W["d%d" % j][f0 * 128:(f0 + 2) * 128, :], 2, D)
                        for half in range(2):
                            S.add("gpsimd", lambda e, half=half: e.dma_start(
                                out=wd_d[j][half, f0:f0 + 2].rearrange("f p x -> p f x"),
                                in_=bv[:, :, half * 512:(half + 1) * 512]),
                                reads=[key], writes=[("wd", j, f0, half)], dma=True)
                    st_.append(wd_step)
                return st_

            def steps_win():
                st_ = []
                for k in range(KD):
                    def f(k=k):
                        bv, key = load_cast(W["in"][k * 128:(k + 1) * 128, :], 1, INW)
                        S.add("gpsimd", lambda e: e.dma_start(out=win_d[:, k, :], in_=bv[:, 0, :]),
                              reads=[key], writes=[("win_d", k)], dma=True)
                    st_.append(f)
                return st_

            def steps_mix():
                st_ = []
                for k in range(KD):
                    def f(k=k):
                        bv, key = load_cast(W["bg"][k * 128:(k + 1) * 128, :], 1, 2 * D)
                        for ab in range(2):
                            S.add("gpsimd", lambda e, ab=ab: e.dma_start(
                                out=wc_d[:, :, ab * 8 + k, :].rearrange("c p j -> p c j"),
                                in_=bv[:, 0, ab * D:(ab + 1) * D].rearrange("p (c j) -> p c j", c=8, j=128)),
                                reads=[key], writes=[("wc_d", ab, k)], dma=True)
                    st_.append(f)
                for oi, key_w in enumerate(("oa", "ob")):
                    for k0 in range(0, 4, 2):
                        def f(oi=oi, key_w=key_w, k0=k0):
                            bv, key = load_cast(W[key_w][k0 * 128:(k0 + 2) * 128, :], 2, D)
                            for kk in range(2):
                                S.add("gpsimd", lambda e, kk=kk: e.dma_start(
                                    out=wc_d[:, :, 16 + oi * 4 + k0 + kk, :].rearrange("c p j -> p c j"),
                                    in_=bv[:, kk, :].rearrange("p (c j) -> p c j", c=8, j=128)),
                                    reads=[key], writes=[("wc_d", 2 + oi, k0 + kk)], dma=True)
                        st_.append(f)
                for k0 in range(0, KD, 2):
                    def f(k0=k0):
                        bv, key = load_cast(W["out"][k0 * 128:(k0 + 2) * 128, :], 2, D)
                        S.add("gpsimd", lambda e: e.dma_start(out=wout_d[:, k0:k0 + 2, :], in_=bv),
                              reads=[key], writes=[("wout_d", k0)], dma=True)
                    st_.append(f)
                return st_
            return steps_ffn, steps_win, steps_mix

        _stage = [A.alloc([128, 3072], F32) for _ in range(2)]
        _stage_bf = [A.alloc([128, 3072], BF16) for _ in range(2)]
        _Tg = A.alloc([128, NF, KD * 128], BF16)
        p_ffn, p_win, p_mix = make_conv(_stage, _stage_bf, _Tg, ("vector", "scalar", "gpsimd"))
        for stp in p_ffn(1) + p_win():
            stp()
        S.barrier()
        bg_steps = []

        def rsqrt_ops(eng, v, y, t, vk, yk, tk):
            S.add(eng, lambda e: e.tensor_scalar(out=y.bitcast(I32), in0=v.bitcast(I32), scalar1=1, scalar2=None,
                                                 op0=ALU.arith_shift_right), reads=[vk], writes=[yk])
            S.add(eng, lambda e: e.tensor_scalar(out=y.bitcast(I32), in0=y.bitcast(I32), scalar1=-1,
                                                 scalar2=0x5f3759df, op0=ALU.mult, op1=ALU.add), reads=[yk], writes=[yk])
            for _ in range(3):
                S.add(eng, lambda e: e.tensor_tensor(out=t, in0=y, in1=y, op=ALU.mult), reads=[yk], writes=[tk])
                S.add(eng, lambda e: e.scalar_tensor_tensor(out=t, in0=t, scalar=-0.5, in1=v, op0=ALU.mult, op1=ALU.mult),
                      reads=[tk, vk], writes=[tk])
                S.add(eng, lambda e: e.scalar_tensor_tensor(out=y, in0=t, scalar=1.5, in1=y, op0=ALU.add, op1=ALU.mult),
                      reads=[tk, yk], writes=[yk])

        def rsqrt_steps(eng, v, y, t, vk, yk, tk):
            st_ = []
            st_.append(lambda: S.add(eng, lambda e: e.tensor_scalar(out=y.bitcast(I32), in0=v.bitcast(I32), scalar1=1, scalar2=None,
                                                                    op0=ALU.arith_shift_right), reads=[vk], writes=[yk]))
            st_.append(lambda: S.add(eng, lambda e: e.tensor_scalar(out=y.bitcast(I32), in0=y.bitcast(I32), scalar1=-1,
                                                                    scalar2=0x5f3759df, op0=ALU.mult, op1=ALU.add), reads=[yk], writes=[yk]))
            for _ in range(3):
                st_.append(lambda: S.add(eng, lambda e: e.tensor_tensor(out=t, in0=y, in1=y, op=ALU.mult), reads=[yk], writes=[tk]))
                st_.append(lambda: S.add(eng, lambda e: e.scalar_tensor_tensor(out=t, in0=t, scalar=-0.5, in1=v, op0=ALU.mult, op1=ALU.mult),
                                         reads=[tk, vk], writes=[tk]))
                st_.append(lambda: S.add(eng, lambda e: e.scalar_tensor_tensor(out=y, in0=t, scalar=1.5, in1=y, op0=ALU.add, op1=ALU.mult),
                                         reads=[tk, yk], writes=[yk]))
            return st_

        class Stream:
            def __init__(self, name, slots, plan):
                self.name = name
                self.slots = slots
                self.plan = plan
                self.issued = 0

            def _issue_to(self, n):
                while self.issued < min(n, len(self.plan)):
                    i = self.issued
                    sl = i % len(self.slots)
                    S.add("sync", self.plan[i](self.slots[sl]), writes=[(self.name, sl)], dma=True)
                    self.issued += 1

            def get(self, i):
                self._issue_to(i + len(self.slots))
                sl = i % len(self.slots)
                return self.slots[sl], (self.name, sl)

            def prefetch(self, i):
                self._issue_to(i + len(self.slots))

        def norm_pre(xt, xkeys, gain, gkey, bufs):
            junk, ss4, vv4, rstd4, tt4, hb = bufs
            for t in range(4):
                jo = junk if junk is not None else hb[:, t, :]
                jw = ["junk"] if junk is not None else [("hb", t)]
                S.add("scalar", lambda e, t=t, jo=jo: e.activation(out=jo, in_=xt[:, t, :], func=AF.Square,
                                                                   accum_out=ss4[:, t:t + 1]),
                      reads=[xkeys[t]], writes=jw + ["ss4"])
            S.add("vector", lambda e: e.tensor_scalar(out=vv4, in0=ss4, scalar1=1.0 / D, scalar2=EPS, op0=ALU.mult,
                                                      op1=ALU.add), reads=["ss4"], writes=["vv4"])
            rsqrt_ops("vector", vv4, rstd4, tt4, "vv4", "rstd4", "tt4")
            for t in range(4):
                S.add("vector", lambda e, t=t: e.scalar_tensor_tensor(
                    out=hb[:, t, :], in0=xt[:, t, :], scalar=rstd4[:, t:t + 1], in1=gain, op0=ALU.mult, op1=ALU.mult),
                    reads=[xkeys[t], "rstd4", gkey], writes=[("hb", t)])

        def norm_tr(hTb, hkey, bufs, banks=(0, 1)):
            hb = bufs[5]
            for k in range(KD):
                bi = banks[k % 2]

                def emit(e, k=k, bi=bi):
                    for t in range(4):
                        ins = e.transpose(out=bank_bf(bi)[:, t * 128:(t + 1) * 128],
                                          in_=hb[:, t, k * 128:(k + 1) * 128], identity=ident)
                    return ins
                S.add("tensor", emit, reads=[("hb", t) for t in range(4)] + ["ident"], writes=[PS(bi)])
                copy_op("scalar", hTb[:, k, :], bank_bf(bi)[:, 0:512], [PS(bi)], [(hkey, k)])

        def norm_T(xt, xkeys, gain, gkey, hTb, hkey, bufs):
            norm_pre(xt, xkeys, gain, gkey, bufs)
            norm_tr(hTb, hkey, bufs)

        def ffn_block(xt, xkeys, hTb, hkey, j, wgu_s, wd_s, wbase, AT, sg, hook=None, hook_f=8, hooks=None):
            for f in range(NF):
                wsl, wkey = wgu_s.get(wbase[0] + f)
                gb_, ub_ = (f % 2), 2 + (f % 2)

                def emit_g(e, wsl=wsl, b=gb_):
                    for k in range(KD):
                        ins = e.matmul(bank(b), lhsT=wsl[:, 0, k, :], rhs=hTb[:, k, :], start=(k == 0), stop=(k == KD - 1))
                    return ins

                def emit_u(e, wsl=wsl, b=ub_):
                    for k in range(KD):
                        ins = e.matmul(bank(b), lhsT=wsl[:, 1, k, :], rhs=hTb[:, k, :], start=(k == 0), stop=(k == KD - 1))
                    return ins
                hk = [(hkey, k) for k in range(KD)]
                S.add("tensor", emit_g, reads=hk + [wkey], writes=[PS(gb_)])
                S.add("tensor", emit_u, reads=hk + [wkey], writes=[PS(ub_)])
                sgl = sg[f % 2]
                S.add("scalar", lambda e, sgl=sgl, b=gb_: e.activation(out=sgl, in_=bank(b), func=AF.Silu),
                      reads=[PS(gb_)], writes=[("sg", f % 2)])
                S.add("vector", lambda e, sgl=sgl, b=ub_, f=f: e.tensor_tensor(out=AT[:, f, :], in0=sgl, in1=bank(b), op=ALU.mult),
                      reads=[("sg", f % 2), PS(ub_)], writes=[("AT", f)])
                if hook is not None and f == hook_f:
                    hook()
                if hooks is not None and f in hooks:
                    hooks[f]()
            wbase[0] += NF
            for half in range(2):
                yb0 = 4 if half == 0 else 0
                for f0 in range(0, NF, 2):
                    dsl, dkey = wd_s.get(wbase[1] + (half * NF + f0) // 2)

                    def emit_d(e, dsl=dsl, f0=f0, yb0=yb0):
                        for ff in range(2):
                            f = f0 + ff
                            for t in range(4):
                                ins = e.matmul(bank(yb0 + t), lhsT=AT[:, f, t * 128:(t + 1) * 128], rhs=dsl[:, ff, :],
                                               start=(f == 0), stop=(f == NF - 1))
                        return ins
                    S.add("tensor", emit_d, reads=[("AT", f0), ("AT", f0 + 1), dkey], writes=[PS(yb0 + t) for t in range(4)])
                for t in range(4):
                    sl = xt[:, t, half * 512:(half + 1) * 512]
                    S.add("vector", lambda e, sl=sl, t=t, yb0=yb0: e.scalar_tensor_tensor(
                        out=sl, in0=bank(yb0 + t), scalar=0.5, in1=sl, op0=ALU.mult, op1=ALU.add),
                        reads=[PS(yb0 + t), xkeys[t]], writes=[xkeys[t]])
            wbase[1] += NF

        def wgu_plan(j, nblocks):
            plan = []
            for _ in range(nblocks):
                for f in range(NF):
                    plan.append(lambda dst, f=f: (lambda e: e.dma_start(
                        out=dst, in_=wgu_d[j][f].rearrange("p (g k c) -> p g k c", g=2, k=KD, c=128))))
            return plan

        def wd_plan(j, nblocks):
            plan = []
            for _ in range(nblocks):
                for half in range(2):
                    for f0 in range(0, NF, 2):
                        plan.append(lambda dst, half=half, f0=f0: (lambda e: e.dma_start(
                            out=dst, in_=wd_d[j][half, f0:f0 + 2].rearrange("f p x -> p f x"))))
            return plan

        NSTOP = stop_after
        for si, sq in enumerate(seqs):
            T, TQ = sq["T"], sq["TQ"]
            NBLK = T // 512
            NQB = TQ // 512
            NKB = T // 128
            A.cur = persist_mark
            xin = A.alloc([128, 4, D], F32)
            hb = A.alloc([128, 4, D], BF16)
            hTf = A.alloc([128, KD, 512], BF16)
            hTm = A.alloc([128, KD, 512], BF16)
            AT = A.alloc([128, NF, 512], BF16)
            wgu_slots = [A.alloc([128, 2, KD, 128], BF16) for _ in range(4)]
            wd_slots = [A.alloc([128, 2, 512], BF16) for _ in range(3)]
            win_sb = A.alloc([128, KD, INW], BF16)
            xq2 = [A.alloc([128, NH, 64], F32) for _ in range(2)]
            tq = A.alloc([128, NH, 64], F32)
            qkn = A.alloc([128, 4, QKW], BF16)
            qkT = A.alloc([128, 13, 512], BF16)
            vst = A.alloc([128, 4, VW], BF16)
            tabt = [A.alloc([128, 4, 80], F32) for _ in range(3)]
            sg = [A.alloc([128, 512], F32) for _ in range(2)]
            ss4 = A.alloc([128, 4], F32)
            vv4 = A.alloc([128, 4], F32)
            rstd4 = A.alloc([128, 4], F32)
            tt4 = A.alloc([128, 4], F32)
            ssq = A.alloc([128, NH], F32)
            vvq = A.alloc([128, NH], F32)
            rq = A.alloc([128, NH], F32)
            ttq = A.alloc([128, NH], F32)
            rA = [A.alloc([128, 10, 32], F32) for _ in range(4)]
            rB = [A.alloc([128, 16, 8], F32) for _ in range(4)]
            nbufs = (None, ss4, vv4, rstd4, tt4, hb)
            xkeys = [("xin", t) for t in range(4)]

            wgu_s = Stream("wgu", wgu_slots, wgu_plan(1, NBLK))
            wd_s = Stream("wd", wd_slots, wd_plan(1, NBLK))
            wbase = [0, 0]
            own_chunks = [(0, 512), (512, 512), (1024, 512), (1536, 512), (2048, 256)]
            rest_chunks = [(512, 512), (1024, 128), (1664, 512), (2176, 128)]

            def load_x(b, sq=sq, xin=xin):
                S.add("sync", lambda e: e.dma_start(
                    out=xin, in_=sq["x"][b * 512:(b + 1) * 512, :].rearrange("(t p) d -> p t d", p=128)),
                    writes=xkeys, dma=True)

            def load_tab(b, sq=sq, tabt=tabt):
                S.add("sync", lambda e: e.dma_start(
                    out=tabt[b % 3], in_=sq["tab"][b * 512:(b + 1) * 512, :].rearrange("(t p) d -> p t d", p=128)),
                    writes=[("tab", b % 3)], dma=True)

            load_x(0)
            load_tab(0)
            for k0 in range(0, KD, 4):
                S.add("sync", lambda e, k0=k0: e.dma_start(out=win_sb[:, k0:k0 + 4, :], in_=win_d[:, k0:k0 + 4, :]),
                      writes=[("win_sb", k0)], dma=True)
            win_keys = [("win_sb", 0), ("win_sb", 4)]
            bank_ctr = [0]
            t3_pending = [None]

            def run_t3():
                if t3_pending[0] is not None:
                    f = t3_pending[0]
                    t3_pending[0] = None
                    f()

            def proj_mm(b, t, chunks):
                xq = xq2[t % 2]
                xqf = xq.rearrange("p h d -> p (h d)")
                for ci, (c0, ncol) in enumerate(chunks):
                    bi = 2 + bank_ctr[0] % 6
                    bank_ctr[0] += 1

                    def emit_p(e, c0=c0, ncol=ncol, bi=bi):
                        for k in range(KD):
                            ins = e.matmul(bank(bi)[:, 0:ncol], lhsT=hTm[:, k, t * 128:(t + 1) * 128],
                                           rhs=win_sb[:, k, c0:c0 + ncol], start=(k == 0), stop=(k == KD - 1))
                        return ins
                    S.add("tensor", emit_p, reads=[("hTm", k) for k in range(KD)] + win_keys, writes=[PS(bi)])
                    a0, a1 = c0, min(c0 + ncol, QKW)
                    b0, b1 = max(c0, QKW), c0 + ncol
                    if a1 > a0:
                        S.add("scalar", lambda e, bi=bi, a0=a0, a1=a1, c0=c0: e.copy(
                            out=xqf[:, a0:a1], in_=bank(bi)[:, a0 - c0:a1 - c0]),
                            reads=[PS(bi)], writes=[("xq", t % 2, ci)])
                    if b1 > b0:
                        S.add("scalar", lambda e, bi=bi, b0=b0, b1=b1, c0=c0: e.copy(
                            out=vst[:, t, b0 - QKW:b1 - QKW], in_=bank(bi)[:, b0 - c0:b1 - c0]),
                            reads=[PS(bi)], writes=[("vst", t)])

            def qk_chain(b, t, chunks, h0, h1, tsl, tkey):
                xq = xq2[t % 2]
                xqk = [("xq", t % 2, ci) for ci in range(len(chunks))]
                H = h1 - h0
                X = xq[:, h0:h1, :]
                Tq = tq[:, h0:h1, :]
                S.add("scalar", lambda e: e.activation(out=Tq, in_=X, func=AF.Square), reads=xqk, writes=["tq"])
                S.add("vector", lambda e: e.tensor_reduce(out=ssq[:, h0:h1], in_=Tq, axis=AX.X, op=ALU.add),
                      reads=["tq"], writes=["ssq"])
                S.add("vector", lambda e: e.tensor_scalar(out=vvq[:, h0:h1], in0=ssq[:, h0:h1], scalar1=1.0 / 64,
                                                          scalar2=EPS, op0=ALU.mult, op1=ALU.add),
                      reads=["ssq"], writes=["vvq"])
                rsqrt_ops("vector", vvq[:, h0:h1], rq[:, h0:h1], ttq[:, h0:h1], "vvq", "rq", "ttq")
                S.add("gpsimd", lambda e: e.tensor_tensor(out=X, in0=X, in1=gall[:, h0:h1, :], op=ALU.mult),
                      reads=xqk + gall_keys + ["tq"], writes=xqk)
                S.add("vector", lambda e: e.tensor_tensor(
                    out=X, in0=X, in1=rq[:, h0:h1].unsqueeze(2).broadcast_to([128, H, 64]), op=ALU.mult),
                    reads=xqk + ["rq"], writes=xqk)
                qv = qkn[:, t, :].rearrange("p (h d) -> p h d", h=NH, d=64)
                for (r0, r1, m, cofs, rbuf, rkey) in ((h0, 10, 32, 0, rA, "rA"), (10, h1, 8, 64, rB, "rB")):
                    Hr = r1 - r0
                    x1 = xq[:, r0:r1, 0:m]
                    x2 = xq[:, r0:r1, m:2 * m]
                    cs = tsl[:, t, cofs:cofs + m].unsqueeze(1).broadcast_to([128, Hr, m])
                    sn = tsl[:, t, cofs + m:cofs + 2 * m].unsqueeze(1).broadcast_to([128, Hr, m])
                    bA, bB, bC, bD = [rb[:, 0:Hr, :] for rb in rbuf]
                    S.add("vector", lambda e, bA=bA, x1=x1, cs=cs: e.tensor_tensor(out=bA, in0=x1, in1=cs, op=ALU.mult),
                          reads=xqk + [tkey], writes=[(rkey, 0)])
                    S.add("vector", lambda e, bB=bB, x2=x2, sn=sn: e.tensor_tensor(out=bB, in0=x2, in1=sn, op=ALU.mult),
                          reads=xqk + [tkey], writes=[(rkey, 1)])
                    S.add("vector", lambda e, bA=bA, bB=bB, r0=r0, r1=r1, m=m: e.tensor_tensor(
                        out=qv[:, r0:r1, 0:m], in0=bA, in1=bB, op=ALU.subtract),
                        reads=[(rkey, 0), (rkey, 1)], writes=[("qkn", t, rkey, 0)])
                    S.add("gpsimd", lambda e, bC=bC, x2=x2, cs=cs: e.tensor_tensor(out=bC, in0=x2, in1=cs, op=ALU.mult),
                          reads=xqk + [tkey], writes=[(rkey, 2)])
                    S.add("gpsimd", lambda e, bD=bD, x1=x1, sn=sn: e.tensor_tensor(out=bD, in0=x1, in1=sn, op=ALU.mult),
                          reads=xqk + [tkey], writes=[(rkey, 3)])
                    S.add("gpsimd", lambda e, bC=bC, bD=bD, r0=r0, r1=r1, m=m: e.tensor_tensor(
                        out=qv[:, r0:r1, m:2 * m], in0=bC, in1=bD, op=ALU.add),
                        reads=[(rkey, 2), (rkey, 3)], writes=[("qkn", t, rkey, 1)])
                    if m == 8:
                        S.add("gpsimd", lambda e, r0=r0, r1=r1: e.tensor_copy(out=qv[:, r0:r1, 16:64], in_=xq[:, r0:r1, 16:64]),
                              reads=xqk, writes=[("qkn", t, rkey, 2)])

            def make_t3(b, own, sq=sq):
                def t3():
                    qkn_keys = [("qkn", t, rk, i) for t in range(4) for rk, n in (("rA", 2), ("rB", 3)) for i in range(n)]
                    groups = list(range(13)) if own else list(range(4, 9))
                    for r in groups:
                        bi = 4 + r % 2

                        def emit_t(e, r=r, bi=bi):
                            for t in range(4):
                                ins = e.transpose(out=bank_bf(bi)[:, t * 128:(t + 1) * 128],
                                                  in_=qkn[:, t, r * 128:(r + 1) * 128], identity=ident)
                            return ins
                        S.add("tensor", emit_t, reads=qkn_keys + ["ident"], writes=[PS(bi)])
                        copy_op("scalar" if r % 2 == 0 else "vector", qkT[:, r, :], bank_bf(bi)[:, 0:512],
                                [PS(bi)], [("qkT", r)])
                    blk = slice(b * 512, (b + 1) * 512)
                    if own:
                        S.add("gpsimd", lambda e: e.dma_start(
                            out=sq["QTA"][:, :, blk].rearrange("g r t -> r g t"), in_=qkT[:, 0:4, :]),
                            reads=[("qkT", r) for r in range(0, 4)], writes=[("QTA", b)], dma=True)
                        S.add("gpsimd", lambda e: e.dma_start(
                            out=sq["QTB"][:, :, blk].rearrange("g r t -> r g t"), in_=qkT[:, 9:13, :]),
                            reads=[("qkT", r) for r in range(9, 13)], writes=[("QTB", b)], dma=True)
                    S.add("gpsimd", lambda e: e.dma_start(out=sq["KTA"][:, blk], in_=qkT[:, 4, :]),
                          reads=[("qkT", 4)], writes=[("KTA", b)], dma=True)
                    S.add("gpsimd", lambda e: e.dma_start(
                        out=sq["KTB"][:, :, blk].rearrange("g r t -> r g t"), in_=qkT[:, 5:9, :]),
                        reads=[("qkT", r) for r in range(5, 9)], writes=[("KTB", b)], dma=True)
                return t3

            norm_pre(xin, xkeys, g1, "g1", nbufs)
            norm_tr(hTf, "hTf", nbufs)
            deferred = {}
            for b in range(NBLK):
                own = b * 512 < TQ
                if b + 1 < NBLK:
                    load_tab(b + 1)
                ffn_block(xin, xkeys, hTf, "hTf", 1, wgu_s, wd_s, wbase, AT, sg, hooks=deferred)
                deferred = {}
                if own:
                    S.add("gpsimd", lambda e, b=b: e.dma_start(
                        out=sq["x1"][b * 512:(b + 1) * 512, :].rearrange("(t p) d -> p t d", p=128), in_=xin),
                        reads=xkeys, writes=[("x1", b)], dma=True)
                norm_pre(xin, xkeys, gm, "gm", nbufs)
                norm_tr(hTm, "hTm", nbufs)
                if b + 1 < NBLK:
                    load_x(b + 1)
                    norm_pre(xin, xkeys, g1, "g1", nbufs)
                chunks = own_chunks if own else rest_chunks
                h0, h1 = (0, NH) if own else (8, 18)
                tsl = tabt[b % 3]
                tkey = ("tab", b % 3)
                proj_mm(b, 0, chunks)
                qk_chain(b, 0, chunks, h0, h1, tsl, tkey)
                proj_mm(b, 1, chunks)
                qk_chain(b, 1, chunks, h0, h1, tsl, tkey)
                if b + 1 < NBLK:
                    norm_tr(hTf, "hTf", nbufs)
                proj_mm(b, 2, chunks)
                proj_mm(b, 3, chunks)
                blk = slice(b * 512, (b + 1) * 512)
                vkeys = [("vst", t) for t in range(4)]
                S.add("gpsimd", lambda e, blk=blk: e.dma_start(
                    out=sq["VA"][blk, :].rearrange("(t p) c -> p t c", p=128), in_=vst[:, :, 0:128]),
                    reads=vkeys, writes=[("VA", b)], dma=True)
                S.add("gpsimd", lambda e, blk=blk: e.dma_start(
                    out=sq["VB"][blk, :].rearrange("(t p) c -> p t c", p=128), in_=vst[:, :, 128:VW]),
                    reads=vkeys, writes=[("VB", b)], dma=True)

                def mk(t, b=b, chunks=chunks, h0=h0, h1=h1, tsl=tsl, tkey=tkey):
                    return lambda: qk_chain(b, t, chunks, h0, h1, tsl, tkey)
                deferred = {1: mk(2), 5: mk(3), 10: make_t3(b, own)}
            for f in sorted(deferred):
                deferred[f]()
            S.barrier()
            if NSTOP == 1:
                continue

            A.cur = persist_mark
            KTA_sb = A.alloc([128, T], BF16)
            VAaug = A.alloc([128, NKB, 2, 65], BF16)
            KTB_sb = [A.alloc([128, T], BF16) for _ in range(2)]
            VB_sb = [A.alloc([128, NKB, 128], BF16) for _ in range(2)]
            QT = [A.alloc([128, 512], BF16) for _ in range(3)]
            ET = [A.alloc([128, 1024], BF16) for _ in range(4)]
            zacc = A.alloc([128, 512], F32)
            rz1 = A.alloc([128, 512], F32)
            rz2 = A.alloc([128, 512], F32)
            bcs = A.alloc([128, 512], F32)
            t1 = A.alloc([128, 512], F32)
            t2 = A.alloc([128, 512], F32)
            obf = A.alloc([128, 512], F32)
            sqf = A.alloc([128, 512], F32)
            vvb = A.alloc([128, 512], F32)
            rsd = A.alloc([128, 512], F32)
            ttb = A.alloc([128, 512], F32)
            obb = [A.alloc([128, 512], BF16) for _ in range(2)]

            S.add("sync", lambda e: e.dma_start(out=KTA_sb, in_=sq["KTA"]), writes=["KTA_sb"], dma=True)
            S.add("gpsimd", lambda e: e.memset(VAaug[:, :, :, 64:65], 1.0), writes=["VAones"])
            KG = 8
            for kb0 in range(0, NKB, KG):
                kb1 = min(NKB, kb0 + KG)
                for kv in range(2):
                    S.add("sync", lambda e, kb0=kb0, kb1=kb1, kv=kv: e.dma_start(
                        out=VAaug[:, kb0:kb1, kv, 0:64],
                        in_=sq["VA"][kb0 * 128:kb1 * 128, kv * 64:(kv + 1) * 64].rearrange("(k p) d -> p k d", p=128)),
                        writes=[("VAaug", kb0, kv)], dma=True)
            va_keys = [("VAaug", kb0, kv) for kb0 in range(0, NKB, KG) for kv in range(2)] + ["VAones"]

            def load_B(h, sq=sq, KTB_sb=KTB_sb, VB_sb=VB_sb, NKB=NKB):
                sl = h % 2
                S.add("sync", lambda e: e.dma_start(out=KTB_sb[sl], in_=sq["KTB"][h]), writes=[("KTB_sb", sl)], dma=True)
                for kb0 in range(0, NKB, KG):
                    kb1 = min(NKB, kb0 + KG)
                    S.add("sync", lambda e, kb0=kb0, kb1=kb1: e.dma_start(
                        out=VB_sb[sl][:, kb0:kb1, :],
                        in_=sq["VB"][kb0 * 128:kb1 * 128, h * 128:(h + 1) * 128].rearrange("(k p) c -> p k c", p=128)),
                        writes=[("VB_sb", sl, kb0)], dma=True)

            ctr = {"qt": 0, "et": 0, "sp": 0, "kb": 0}
            osb = A.alloc([128, 4, 512], F32)
            if si == 0:
                b_stage = [A.alloc([128, 3072], F32) for _ in range(2)]
                b_stage_bf = [A.alloc([128, 3072], BF16) for _ in range(2)]
                b_Tg = A.alloc([128, NF, KD * 128], BF16)
                b_ffn, _b_win, b_mix = make_conv(b_stage, b_stage_bf, b_Tg, ("gpsimd",))
                bg_steps.extend(b_ffn(2) + b_mix())
            pending_fin = [None]
            fin_steps = []

            def run_pending():
                if pending_fin[0] is not None:
                    f = pending_fin[0]
                    pending_fin[0] = None
                    r = f()
                    if r:
                        fin_steps.extend(r)

            def flush_steps():
                while fin_steps:
                    fin_steps.pop(0)()

            ring = {"pairs": [0, 1, 3]}

            def ring_take():
                p = ring["pairs"][ctr["sp"] % len(ring["pairs"])]
                ctr["sp"] += 1
                return p

            def attn_unit(qsrc, lhs_lo, lhs_hi, lkeys, pv_emit, pv_reads, pv_writes, post_exp=None):
                qi = ctr["qt"] % 3
                ctr["qt"] += 1
                qs, qkey = QT[qi], ("QT", qi)
                S.add("sync", lambda e: e.dma_start(out=qs, in_=qsrc), writes=[qkey], dma=True)
                pend = []
                lag = len(ring["pairs"]) - 1
                for kb in range(NKB):
                    sp = ring_take()
                    ksl = slice(kb * 128, (kb + 1) * 128)

                    def emit_qk(e, sp=sp, ksl=ksl):
                        e.matmul(pp[sp][:, 0:512], lhsT=lhs_lo[:, ksl], rhs=qs[0:64, :], start=True, stop=True)
                        return e.matmul(pp[sp][:, 512:1024], lhsT=lhs_hi[:, ksl], rhs=qs[64:128, :], start=True, stop=True)
                    S.add("tensor", emit_qk, reads=lkeys + [qkey], writes=[PS(2 * sp), PS(2 * sp + 1)])
                    ei = ctr["et"] % 4
                    ctr["et"] += 1
                    es, ekey = ET[ei], ("ET", ei)
                    S.add("scalar", lambda e, sp=sp, es=es: e.activation(out=es, in_=pp[sp][:, :], func=AF.Exp, scale=SCALE),
                          reads=[PS(2 * sp), PS(2 * sp + 1)], writes=[ekey])
                    if post_exp is not None:
                        post_exp(kb, es, ekey)
                    pend.append((pv_emit(kb, es), pv_reads + [ekey]))
                    if len(pend) > lag:
                        p0 = pend.pop(0)
                        S.add("tensor", p0[0], reads=p0[1], writes=pv_writes)
                    if kb == min(3, NKB - 1):
                        run_pending()
                    if kb >= 3 and fin_steps:
                        fin_steps.pop(0)()
                    ctr["kb"] += 1
                    if bg_steps and ctr["kb"] % 16 == 8:
                        bg_steps.pop(0)()
                for p0 in pend:
                    S.add("tensor", p0[0], reads=p0[1], writes=pv_writes)
                flush_steps()

            load_B(0)
            for g in range(4):
                for qb in range(NQB):
                    qsl = slice(qb * 512, (qb + 1) * 512)

                    def pv_emit_A(kb, es):
                        def f(e):
                            e.matmul(bank(4)[0:65, :], lhsT=VAaug[:, kb, 0, :], rhs=es[:, 0:512],
                                     start=(kb == 0), stop=(kb == NKB - 1))
                            return e.matmul(bank(5)[0:65, :], lhsT=VAaug[:, kb, 1, :], rhs=es[:, 512:1024],
                                            start=(kb == 0), stop=(kb == NKB - 1))
                        return f
                    attn_unit(sq["QTA"][g][:, qsl], KTA_sb[0:64, :], KTA_sb[64:128, :], ["KTA_sb"],
                              pv_emit_A, va_keys, [PS(4), PS(5)])
                    for kv in range(2):
                        S.add("vector", lambda e, kv=kv: e.tensor_copy(out=osb[0:65, kv, :], in_=bank(4 + kv)[0:65, :]),
                              reads=[PS(4 + kv)], writes=[("osb", kv)])

                    def fin_A(g=g, qb=qb, qsl=qsl):
                        sp = ring_take()
                        S.add("vector", lambda e: e.reciprocal(out=rz1[64:65, :], in_=osb[64:65, 0, :]),
                              reads=[("osb", 0)], writes=["rz1"])
                        S.add("vector", lambda e: e.reciprocal(out=rz2[64:65, :], in_=osb[64:65, 1, :]),
                              reads=[("osb", 1)], writes=["rz2"])

                        def emit_bc(e, sp=sp):
                            e.matmul(pp[sp][0:64, 0:512], lhsT=ones_f[64:65, 0:64], rhs=rz1[64:65, :], start=True, stop=True)
                            return e.matmul(pp[sp][0:64, 512:1024], lhsT=ones_f[64:65, 0:64], rhs=rz2[64:65, :],
                                            start=True, stop=True)
                        S.add("tensor", emit_bc, reads=["rz1", "rz2", "ones_f"], writes=[PS(2 * sp), PS(2 * sp + 1)])
                        for kv in range(2):
                            S.add("vector", lambda e, kv=kv, sp=sp: e.tensor_tensor(
                                out=obb[kv][0:64, :], in0=osb[0:64, kv, :], in1=pp[sp][0:64, kv * 512:(kv + 1) * 512],
                                op=ALU.mult), reads=[("osb", kv), PS(2 * sp + kv)], writes=[("obb", kv)])
                            hq = kv * 4 + g
                            S.add("gpsimd", lambda e, kv=kv, hq=hq: e.dma_start(
                                out=sq["OAT"][hq * 64:(hq + 1) * 64, qsl], in_=obb[kv][0:64, :]),
                                reads=[("obb", kv)], writes=[("OAT", hq, qb)], dma=True)
                    pending_fin[0] = fin_A
            ring["pairs"] = [0, 1]
            for h in range(4):
                sl = h % 2
                if h + 1 < 4:
                    load_B(h + 1)
                vb_keys = [("VB_sb", sl, kb0) for kb0 in range(0, NKB, KG)]
                for qb in range(NQB):
                    qsl = slice(qb * 512, (qb + 1) * 512)

                    def pv_emit_B(kb, es, sl=sl):
                        def f(e):
                            e.matmul(bank(4), lhsT=VB_sb[sl][:, kb, :], rhs=es[:, 0:512], start=(kb == 0), stop=(kb == NKB - 1))
                            e.matmul(bank(5), lhsT=VB_sb[sl][:, kb, :], rhs=es[:, 512:1024], start=(kb == 0), stop=(kb == NKB - 1))
                            return e.matmul(bank(6), lhsT=ones_bf, rhs=es[:, 0:512], start=(kb == 0), stop=(kb == NKB - 1))
                        return f

                    def post_exp_B(kb, es, ekey):
                        src = es[:, 512:1024]
                        if kb == 0:
                            S.add("vector", lambda e: e.tensor_copy(out=zacc, in_=src), reads=[ekey], writes=["zacc"])
                        else:
                            S.add("vector", lambda e: e.tensor_tensor(out=zacc, in0=zacc, in1=src, op=ALU.add),
                                  reads=[ekey, "zacc"], writes=["zacc"])
                    attn_unit(sq["QTB"][h][:, qsl], KTB_sb[sl][0:64, :], KTB_sb[sl][64:128, :], [("KTB_sb", sl)],
                              pv_emit_B, vb_keys + ["ones_bf"], [PS(4), PS(5), PS(6)], post_exp=post_exp_B)
                    S.add("tensor", lambda e: e.matmul(bank(7), lhsT=ones_f, rhs=zacc, start=True, stop=True),
                          reads=["zacc", "ones_f"], writes=[PS(7)])
                    S.add("scalar", lambda e: e.copy(out=osb[:, 0, :], in_=bank(4)), reads=[PS(4)], writes=[("osb", 0)])
                    S.add("scalar", lambda e: e.copy(out=osb[:, 1, :], in_=bank(5)), reads=[PS(5)], writes=[("osb", 1)])
                    S.add("vector", lambda e: e.reciprocal(out=rz1, in_=bank(6)), reads=[PS(6)], writes=["rz1"])
                    S.add("vector", lambda e: e.reciprocal(out=rz2, in_=bank(7)), reads=[PS(7)], writes=["rz2"])

                    def fin_B(h=h, qb=qb, qsl=qsl):
                        st_ = []
                        st_.append(lambda: S.add("vector", lambda e: e.tensor_tensor(out=t1, in0=osb[:, 0, :], in1=rz1, op=ALU.mult),
                                                 reads=[("osb", 0), "rz1"], writes=["t1"]))
                        st_.append(lambda: S.add("gpsimd", lambda e: e.tensor_tensor(out=t2, in0=osb[:, 1, :], in1=rz2, op=ALU.mult),
                                                 reads=[("osb", 1), "rz2"], writes=["t2"]))
                        st_.append(lambda: S.add("vector", lambda e: e.scalar_tensor_tensor(
                            out=obf, in0=t2, scalar=neglam[:, 0:1], in1=t1, op0=ALU.mult, op1=ALU.add),
                            reads=["t1", "t2", "neglam"], writes=["obf"]))
                        st_.append(lambda: S.add("gpsimd", lambda e: e.tensor_tensor(out=sqf, in0=obf, in1=obf, op=ALU.mult),
                                                 reads=["obf"], writes=["sqf"]))
                        slot = {}

                        def ss_mm():
                            sp = ring_take()
                            slot["sp"] = sp
                            S.add("tensor", lambda e: e.matmul(pp[sp][:, 0:512], lhsT=ones_f, rhs=sqf, start=True, stop=True),
                                  reads=["sqf", "ones_f"], writes=[PS(2 * sp), PS(2 * sp + 1)])
                        st_.append(ss_mm)

                        def vv():
                            sp = slot["sp"]
                            S.add("vector", lambda e: e.tensor_scalar(out=vvb, in0=pp[sp][:, 0:512], scalar1=1.0 / 128,
                                                                      scalar2=EPS, op0=ALU.mult, op1=ALU.add),
                                  reads=[PS(2 * sp)], writes=["vvb"])
                        st_.append(vv)
                        st_.extend(rsqrt_steps("vector", vvb, rsd, ttb, "vvb", "rsd", "ttb"))
                        st_.append(lambda: S.add("vector", lambda e: e.scalar_tensor_tensor(
                            out=obb[0], in0=obf, scalar=gout[:, 0:1], in1=rsd, op0=ALU.mult, op1=ALU.mult),
                            reads=["obf", "rsd", "gout"], writes=[("obb", 0)]))
                        st_.append(lambda: S.add("gpsimd", lambda e: e.dma_start(
                            out=sq["OBT"][h * 128:(h + 1) * 128, qsl], in_=obb[0]),
                            reads=[("obb", 0)], writes=[("OBT", h, qb)], dma=True))
                        return st_
                    pending_fin[0] = fin_B
            run_pending()
            flush_steps()
            while bg_steps:
                bg_steps.pop(0)()
            S.barrier()
            if NSTOP == 2:
                continue

            A.cur = persist_mark
            xin2 = [A.alloc([128, 4, D], F32) for _ in range(2)]
            hb = A.alloc([128, 4, D], BF16)
            hT = A.alloc([128, KD, 512], BF16)
            hTf2 = A.alloc([128, KD, 512], BF16)
            AT = A.alloc([128, NF, 512], BF16)
            wgu_slots = [A.alloc([128, 2, KD, 128], BF16) for _ in range(4)]
            wd_slots = [A.alloc([128, 2, 512], BF16) for _ in range(4)]
            wc_slots = [A.alloc([128, 24, 128], BF16) for _ in range(3)]
            wout_sb = A.alloc([128, KD, D], BF16)
            oaT = A.alloc([128, 4, 512], BF16)
            obT = A.alloc([128, 4, 512], BF16)
            mT = A.alloc([128, KD, 512], BF16)
            tha = [A.alloc([128, 512], F32) for _ in range(2)]
            thb = [A.alloc([128, 512], F32) for _ in range(2)]
            u1 = A.alloc([128, 512], F32)
            u2 = A.alloc([128, 512], F32)
            sg = [A.alloc([128, 512], F32) for _ in range(2)]
            junk = A.alloc([128, D], BF16)
            ss4 = A.alloc([128, 4], F32)
            vv4 = A.alloc([128, 4], F32)
            rstd4 = A.alloc([128, 4], F32)
            tt4 = A.alloc([128, 4], F32)
            nbufs = (junk, ss4, vv4, rstd4, tt4, hb)
            wgu_s = Stream("wgu", wgu_slots, wgu_plan(2, NQB))
            wd_s = Stream("wd", wd_slots, wd_plan(2, NQB))
            wbase = [0, 0]
            wc_plan = []
            for _ in range(NQB):
                for c in range(8):
                    wc_plan.append(lambda dst, c=c: (lambda e: e.dma_start(out=dst, in_=wc_d[c])))
            wc_s = Stream("wc", wc_slots, wc_plan)
            for k0 in range(0, KD, 4):
                S.add("sync", lambda e, k0=k0: e.dma_start(out=wout_sb[:, k0:k0 + 4, :], in_=wout_d[:, k0:k0 + 4, :]),
                      writes=[("wout_sb", k0)], dma=True)
            wout_keys = [("wout_sb", 0), ("wout_sb", 4)]

            def load_x1(b, sq=sq, xin2=xin2):
                S.add("sync", lambda e: e.dma_start(
                    out=xin2[b % 2], in_=sq["x1"][b * 512:(b + 1) * 512, :].rearrange("(t p) d -> p t d", p=128)),
                    writes=[("xin", b % 2, t) for t in range(4)], dma=True)

            def load_o(b, sq=sq, oaT=oaT, obT=obT):
                blk = slice(b * 512, (b + 1) * 512)
                S.add("sync", lambda e: e.dma_start(
                    out=oaT, in_=sq["OAT"][:, blk].rearrange("(c p) t -> p c t", p=128)), writes=["oaT"], dma=True)
                S.add("sync", lambda e: e.dma_start(
                    out=obT, in_=sq["OBT"][:, blk].rearrange("(c p) t -> p c t", p=128)), writes=["obT"], dma=True)

            load_x1(0)
            norm_pre(xin2[0], [("xin", 0, t) for t in range(4)], gm, "gm", nbufs)
            for b in range(NQB):
                blk = slice(b * 512, (b + 1) * 512)
                xt = xin2[b % 2]
                xkeys = [("xin", b % 2, t) for t in range(4)]
                if b + 1 < NQB:
                    load_x1(b + 1)
                if b == 0:
                    load_o(0)
                norm_tr(hT, "hT", nbufs)
                hk = [("hT", k) for k in range(KD)]
                for c in range(8):
                    wsl, wkey = wc_s.get(b * 8 + c)
                    bs = (c % 2) * 4

                    def emit_g(e, wsl=wsl, bs=bs):
                        for k in range(KD):
                            e.matmul(bank(bs), lhsT=wsl[:, k, :], rhs=hT[:, k, :], start=(k == 0), stop=(k == KD - 1))
                        for k in range(KD):
                            e.matmul(bank(bs + 1), lhsT=wsl[:, 8 + k, :], rhs=hT[:, k, :], start=(k == 0), stop=(k == KD - 1))
                        for k in range(4):
                            e.matmul(bank(bs + 2), lhsT=wsl[:, 16 + k, :], rhs=oaT[:, k, :], start=(k == 0), stop=(k == 3))
                        for k in range(4):
                            ins = e.matmul(bank(bs + 3), lhsT=wsl[:, 20 + k, :], rhs=obT[:, k, :], start=(k == 0), stop=(k == 3))
                        return ins
                    S.add("tensor", emit_g, reads=hk + [wkey, "oaT", "obT"], writes=[PS(bs + i) for i in range(4)])
                    ta, tb = tha[c % 2], thb[c % 2]
                    S.add("scalar", lambda e, ta=ta, bs=bs: e.activation(out=ta, in_=bank(bs), func=AF.Tanh, scale=0.5),
                          reads=[PS(bs)], writes=[("tha", c % 2)])
                    S.add("scalar", lambda e, tb=tb, bs=bs: e.activation(out=tb, in_=bank(bs + 1), func=AF.Tanh, scale=0.5),
                          reads=[PS(bs + 1)], writes=[("thb", c % 2)])
                    S.add("vector", lambda e, ta=ta, bs=bs: e.scalar_tensor_tensor(
                        out=u1, in0=ta, scalar=1.0, in1=bank(bs + 2), op0=ALU.add, op1=ALU.mult),
                        reads=[("tha", c % 2), PS(bs + 2)], writes=["u1"])
                    S.add("vector", lambda e, tb=tb, bs=bs: e.scalar_tensor_tensor(
                        out=u2, in0=tb, scalar=1.0, in1=bank(bs + 3), op0=ALU.add, op1=ALU.mult),
                        reads=[("thb", c % 2), PS(bs + 3)], writes=["u2"])
                    S.add("gpsimd", lambda e, c=c: e.tensor_tensor(out=mT[:, c, :], in0=u1, in1=u2, op=ALU.add),
                          reads=["u1", "u2"], writes=[("mT", c)])
                mk = [("mT", c) for c in range(8)]
                for t in range(4):
                    for half in range(2):
                        bi = (t * 2 + half) % 8

                        def emit_o(e, t=t, half=half, bi=bi):
                            for k in range(KD):
                                ins = e.matmul(bank(bi), lhsT=mT[:, k, t * 128:(t + 1) * 128],
                                               rhs=wout_sb[:, k, half * 512:(half + 1) * 512], start=(k == 0), stop=(k == KD - 1))
                            return ins
                        S.add("tensor", emit_o, reads=mk + wout_keys, writes=[PS(bi)])
                        xsl = xt[:, t, half * 512:(half + 1) * 512]
                        S.add("vector", lambda e, xsl=xsl, bi=bi: e.scalar_tensor_tensor(
                            out=xsl, in0=bank(bi), scalar=0.5, in1=xsl, op0=ALU.mult, op1=ALU.add),
                            reads=[PS(bi), xkeys[t]], writes=[xkeys[t]])
                if NSTOP != 3:
                    norm_T(xt, xkeys, g2, "g2", hTf2, "hTf2", nbufs)
                    nxt = None
                    if b + 1 < NQB:
                        def nxt(b=b):
                            load_o(b + 1)
                            wc_s.prefetch((b + 1) * 8)
                            norm_pre(xin2[(b + 1) % 2], [("xin", (b + 1) % 2, t) for t in range(4)], gm, "gm", nbufs)
                    ffn_block(xt, xkeys, hTf2, "hTf2", 2, wgu_s, wd_s, wbase, AT, sg, hook=nxt, hook_f=4)
                elif b + 1 < NQB:
                    load_o(b + 1)
                    norm_pre(xin2[(b + 1) % 2], [("xin", (b + 1) % 2, t) for t in range(4)], gm, "gm", nbufs)
                S.add("gpsimd", lambda e, blk=blk, xt=xt: e.dma_start(
                    out=sq["y"][blk, :].rearrange("(t p) d -> p t d", p=128), in_=xt),
                    reads=xkeys, writes=[("y", b)], dma=True)
            S.barrier()
        S.run(block)
    return nc


def _rot_table(pos):
    pos = np.asarray(pos, dtype=np.int64)
    row = (pos // 64).astype(np.float32)
    col = (pos % 64).astype(np.float32)
    invA = (np.float32(10000.0) ** (-np.arange(0, 32, 2, dtype=np.float32) / np.float32(32))).astype(np.float32)
    angA = np.concatenate([row[:, None] * invA[None, :], col[:, None] * invA[None, :]], axis=-1).astype(np.float32)
    invP = (np.float32(500000.0) ** (-np.arange(0, 16, 2, dtype=np.float32) / np.float32(16))).astype(np.float32)
    angP = (pos.astype(np.float32)[:, None] * invP[None, :]).astype(np.float32)
    tab = np.concatenate([np.cos(angA.astype(np.float64)), np.sin(angA.astype(np.float64)),
                          np.cos(angP.astype(np.float64)), np.sin(angP.astype(np.float64))], axis=-1)
    return np.ascontiguousarray(tab.astype(np.float32))


def _win_perm():
    qa = [(kv * 4 + g) * 64 + d for g in range(4) for kv in range(2) for d in range(64)]
    ka = list(range(512, 640))
    va = list(range(640, 768))
    qb = list(range(768, 1280))
    kb = list(range(1280, 1792))
    vb = list(range(1792, 2304))
    return np.array(qa + ka + kb + qb + va + vb, dtype=np.int64)


def make_in_maps(inputs, TS, TPK, TPQ, n_cores=8):
    f = lambda a: np.ascontiguousarray(np.asarray(a, dtype=np.float32))
    xp_all = f(inputs["x_prompt"])
    xs_all = f(inputs["x_sample"])
    nq = TPK // TPQ
    common = {}
    for nm in ("ffn1_norm", "mix_norm", "ffn2_norm", "a_q_norm", "a_k_norm", "b_q_norm", "b_k_norm",
               "b_lambda_q1", "b_lambda_k1", "b_lambda_q2", "b_lambda_k2", "b_out_norm"):
        common[nm] = f(inputs[nm]).reshape(1, -1)
    for nm in ("ffn1_w_gate", "ffn1_w_up", "ffn1_w_down", "ffn2_w_gate", "ffn2_w_up", "ffn2_w_down",
               "w_branch_gate", "w_o_a", "w_o_b", "w_out"):
        a = f(inputs[nm])
        common[nm] = np.ascontiguousarray(a.reshape(a.shape[-2], a.shape[-1]))
    win = f(inputs["w_in"]).reshape(D, INW)
    common["w_in"] = np.ascontiguousarray(win[:, _win_perm()])
    tab_s = _rot_table(np.arange(TS))
    in_maps = []
    for c in range(n_cores):
        pi, q = (c // nq) % xp_all.shape[0], c % nq
        pos_p = np.roll(np.arange(TPK), -q * TPQ)
        m = dict(common)
        m["xs"] = np.ascontiguousarray(xs_all[c])
        m["xp"] = np.ascontiguousarray(xp_all[pi][pos_p])
        m["tab_s"] = tab_s
        m["tab_p"] = _rot_table(pos_p)
        in_maps.append(m)
    return in_maps


_NC_CACHE = {}


def kernel(**inputs):
    TS, TPK, TPQ = 4096, 8192, 2048
    key = (TS, TPK, TPQ)
    if key not in _NC_CACHE:
        _NC_CACHE[key] = build(TS, TPK, TPQ)
    nc = _NC_CACHE[key]
    in_maps = make_in_maps(inputs, TS, TPK, TPQ)
    res = run_bass_kernel_spmd(nc, in_maps, core_ids=list(range(8)))
    y_prompt = np.empty((2, TPK, D), dtype=np.float32)
    y_sample = np.empty((8, TS, D), dtype=np.float32)
    for c in range(8):
        r = res.results[c]
        y_sample[c] = r["ys"]
        pi, q = c // 4, c % 4
        y_prompt[pi, q * TPQ:(q + 1) * TPQ] = r["yp"]
    return (y_prompt, y_sample)
```

```python
import math
import types
from contextlib import ExitStack

import numpy as np
import concourse.bass as bass
import concourse.mybir as mybir
from concourse.bass_utils import run_bass_kernel_spmd

F32 = mybir.dt.float32
BF16 = mybir.dt.bfloat16
I32 = mybir.dt.int32
U8 = mybir.dt.uint8
AF = mybir.ActivationFunctionType
ALU = mybir.AluOpType
AX = mybir.AxisListType

D = 1024
DFF = 2816
NF = DFF // 128
KD = D // 128
INW = 2304
NH = 26
QKW = NH * 64
VW = 640
EPS = 1e-6
SCALE = 0.125
LAM_INIT = 0.8 - 0.6 * math.exp(-0.3 * 0)

ENGS = ("sync", "scalar", "vector", "gpsimd", "tensor")
DMAQ = ("sync", "scalar", "gpsimd")


def _snap(fn):
    if fn is None or getattr(fn, "__closure__", None) is None:
        return fn
    cells = []
    for c in fn.__closure__:
        try:
            cells.append(types.CellType(c.cell_contents))
        except ValueError:
            cells.append(c)
    g = types.FunctionType(fn.__code__, fn.__globals__, fn.__name__, fn.__defaults__, tuple(cells))
    g.__kwdefaults__ = fn.__kwdefaults__
    return g


class Op:
    __slots__ = ("eng", "emit", "deps", "dma", "sigsem", "sigval", "signal")

    def __init__(self, eng, emit, dma):
        self.eng = eng
        self.emit = _snap(emit)
        self.dma = dma
        self.deps = set()
        self.sigsem = None
        self.sigval = 0
        self.signal = False


class Sched:
    def __init__(self, nc, stack, n_dma_sems=12):
        self.nc = nc
        self.ops = {e: [] for e in ENGS}
        self.last_w = {}
        self.readers = {}
        self.csem = {e: stack.enter_context(nc.semaphore("c_" + e)) for e in ENGS if e != "sync"}
        self.dsem = {q: [stack.enter_context(nc.semaphore("d_%s_%d" % (q, i))) for i in range(n_dma_sems)]
                     for q in DMAQ}
        self.dcount = {q: 0 for q in DMAQ}
        self.dlast = {q: {} for q in DMAQ}
        self.since_barrier_dma = []
        self.last_op = {}

    def add(self, eng, emit, reads=(), writes=(), dma=False):
        op = Op(eng, emit, dma)
        deps = op.deps
        for r in reads:
            w = self.last_w.get(r)
            if w is not None:
                deps.add(w)
        for w_ in writes:
            w = self.last_w.get(w_)
            if w is not None:
                deps.add(w)
            rd = self.readers.get(w_)
            if rd:
                deps.update(rd[0].values())
                deps.update(rd[1])
        for r in reads:
            rd = self.readers.get(r)
            if rd is None:
                rd = self.readers[r] = ({}, [])
            if dma:
                rd[1].append(op)
            else:
                rd[0][eng] = op
        for w_ in writes:
            self.last_w[w_] = op
            self.readers[w_] = ({}, [])
        deps.discard(op)
        if dma:
            n = self.dcount[eng]
            sems = self.dsem[eng]
            i = n % len(sems)
            op.sigsem = sems[i]
            op.sigval = 16 * (n // len(sems) + 1)
            prev = self.dlast[eng].get(i)
            if prev is not None:
                deps.add(prev)
            self.dlast[eng][i] = op
            self.dcount[eng] = n + 1
            self.since_barrier_dma.append(op)
        self.ops[eng].append(op)
        if not dma:
            self.last_op[eng] = op
        return op

    def barrier(self):
        deps = set(self.since_barrier_dma)
        for o in self.last_op.values():
            if o is not None and o.emit is not None:
                deps.add(o)
        self.since_barrier_dma = []
        for e in ENGS:
            op = Op(e, None, False)
            op.deps = set(deps)
            self.ops[e].append(op)
        self.last_w = {}
        self.readers = {}

    def finalize(self):
        for e in ENGS:
            for op in self.ops[e]:
                for d in op.deps:
                    if d.dma:
                        continue
                    if d.eng == "tensor" and op.eng == "tensor" and not op.dma:
                        continue
                    d.signal = True
        for e in ENGS:
            if e == "sync":
                continue
            c = 0
            for op in self.ops[e]:
                if op.dma or not op.signal:
                    continue
                assert op.emit is not None
                c += 1
                op.sigsem = self.csem[e]
                op.sigval = c

    def emit_engine(self, ename, eobj):
        waited = {}
        for op in self.ops[ename]:
            needs = {}
            for d in op.deps:
                if (not d.dma) and d.eng == "tensor" and ename == "tensor" and not op.dma:
                    continue
                k = d.sigsem
                if needs.get(k, 0) < d.sigval:
                    needs[k] = d.sigval
            for s, v in needs.items():
                if waited.get(s, 0) < v:
                    eobj.wait_ge(s, v)
                    waited[s] = v
            if op.emit is not None:
                ins = op.emit(eobj)
                if op.dma:
                    ins.then_inc(op.sigsem, 16)
                elif op.signal:
                    ins.then_inc(op.sigsem, 1)

    def run(self, block):
        self.finalize()
        s = self

        @block.sync
        def _(e):
            s.emit_engine("sync", e)

        @block.scalar
        def _(e):
            s.emit_engine("scalar", e)

        @block.vector
        def _(e):
            s.emit_engine("vector", e)

        @block.gpsimd
        def _(e):
            s.emit_engine("gpsimd", e)

        @block.tensor
        def _(e):
            s.emit_engine("tensor", e)


class Arena:
    def __init__(self, t, size):
        self.t = t
        self.size = size
        self.cur = 0

    def alloc(self, shape, dt):
        esz = 4 if dt in (F32, I32) else 2
        n = int(np.prod(shape[1:])) * esz
        off = self.cur
        self.cur = off + (n + 63) // 64 * 64
        assert self.cur <= self.size, ("SBUF arena overflow", self.cur, self.size)
        ap = self.t[:, off:off + n].bitcast(dt)
        if len(shape) == 3:
            ap = ap.rearrange("p (a b) -> p a b", a=shape[1], b=shape[2])
        elif len(shape) == 4:
            ap = ap.rearrange("p (a b c) -> p a b c", a=shape[1], b=shape[2], c=shape[3])
        return ap


def build(TS, TPK, TPQ, debug=False, stop_after=0):
    nc = bass.Bass("TRN2", target_bir_lowering=False)

    def din(name, shape, dt=F32):
        return nc.dram_tensor(name, list(shape), dt, kind="ExternalInput").ap()

    def dout(name, shape, dt=F32):
        return nc.dram_tensor(name, list(shape), dt, kind="ExternalOutput").ap()

    def dscr(name, shape, dt):
        return nc.dram_tensor(name, list(shape), dt, kind="ExternalOutput" if debug else "Internal").ap()

    xs_d = din("xs", [TS, D])
    xp_d = din("xp", [TPK, D])
    tabs_d = din("tab_s", [TS, 80])
    tabp_d = din("tab_p", [TPK, 80])
    W = {}
    for j in (1, 2):
        W["g%d" % j] = din("ffn%d_w_gate" % j, [D, DFF])
        W["u%d" % j] = din("ffn%d_w_up" % j, [D, DFF])
        W["d%d" % j] = din("ffn%d_w_down" % j, [DFF, D])
        W["n%d" % j] = din("ffn%d_norm" % j, [1, D])
    W["nm"] = din("mix_norm", [1, D])
    W["in"] = din("w_in", [D, INW])
    W["bg"] = din("w_branch_gate", [D, 2 * D])
    W["oa"] = din("w_o_a", [512, D])
    W["ob"] = din("w_o_b", [512, D])
    W["out"] = din("w_out", [D, D])
    for nm in ("a_q_norm", "a_k_norm", "b_q_norm", "b_k_norm", "b_lambda_q1", "b_lambda_k1",
               "b_lambda_q2", "b_lambda_k2"):
        W[nm] = din(nm, [1, 64])
    W["b_out_norm"] = din("b_out_norm", [1, 128])
    ys_d = dout("ys", [TS, D])
    yp_d = dout("yp", [TPQ, D])

    wgu_d = {j: dscr("wgu%d" % j, [NF, 128, 2 * KD * 128], BF16) for j in (1, 2)}
    wd_d = {j: dscr("wd%d" % j, [2, NF, 128, 512], BF16) for j in (1, 2)}
    win_d = dscr("win_bf", [128, KD, INW], BF16)
    wc_d = dscr("wc_bf", [8, 128, 24, 128], BF16)
    wout_d = dscr("wout_bf", [128, KD, D], BF16)

    seqs = []
    for nm, xd, tabd, T, TQ, yd in (("s", xs_d, tabs_d, TS, TS, ys_d), ("p", xp_d, tabp_d, TPK, TPQ, yp_d)):
        seqs.append(dict(
            nm=nm, x=xd, tab=tabd, T=T, TQ=TQ, y=yd,
            x1=dscr("x1_" + nm, [TQ, D], F32),
            KTA=dscr("kta_" + nm, [128, T], BF16), KTB=dscr("ktb_" + nm, [4, 128, T], BF16),
            VA=dscr("va_" + nm, [T, 128], BF16), VB=dscr("vb_" + nm, [T, 512], BF16),
            QTA=dscr("qta_" + nm, [4, 128, TQ], BF16), QTB=dscr("qtb_" + nm, [4, 128, TQ], BF16),
            OAT=dscr("oat_" + nm, [512, TQ], BF16), OBT=dscr("obt_" + nm, [512, TQ], BF16)))

    ARENA_BYTES = 207 * 1024
    with ExitStack() as st:
        arena_t = st.enter_context(nc.sbuf_tensor("arena", [128, ARENA_BYTES], U8))
        pp = [st.enter_context(nc.psum_tensor("pp%d" % i, [128, 1024], F32)) for i in range(4)]
        block = st.enter_context(nc.Block())
        S = Sched(nc, st)
        A = Arena(arena_t, ARENA_BYTES)

        def bank(i):
            return pp[i // 2][:, (i % 2) * 512:(i % 2 + 1) * 512]

        def bank_bf(i):
            return bank(i).bitcast(BF16)

        def PS(i):
            return ("ps", i)

        _rr = [0]

        def rr_eng(choices=("vector", "scalar", "gpsimd")):
            _rr[0] += 1
            return choices[_rr[0] % len(choices)]

        def copy_op(eng, out, in_, reads, writes):
            if eng == "scalar":
                S.add("scalar", lambda e: e.copy(out=out, in_=in_), reads=reads, writes=writes)
            else:
                S.add(eng, lambda e: e.tensor_copy(out=out, in_=in_), reads=reads, writes=writes)

        ident = A.alloc([128, 128], BF16)
        ones_bf = A.alloc([128, 128], BF16)
        ones_f = A.alloc([128, 128], F32)
        g1 = A.alloc([128, D], F32)
        gm = A.alloc([128, D], F32)
        g2 = A.alloc([128, D], F32)
        gall = A.alloc([128, NH, 64], F32)
        lamv = A.alloc([128, 4, 64], F32)
        neglam = A.alloc([128, 1], F32)
        gout = A.alloc([128, 1], F32)
        sm1 = A.alloc([128, 8], F32)
        persist_mark = A.cur

        S.add("gpsimd", lambda e: e.memset(ident, 0.0), writes=["ident"])
        S.add("gpsimd", lambda e: e.affine_select(
            out=ident, in_=ident, compare_op=ALU.not_equal, fill=1.0, base=0, pattern=[[-1, 128]],
            channel_multiplier=1), reads=["ident"], writes=["ident"])
        S.add("gpsimd", lambda e: e.memset(ones_bf, 1.0), writes=["ones_bf"])
        S.add("gpsimd", lambda e: e.memset(ones_f, 1.0), writes=["ones_f"])
        for dst, src, key in ((g1, W["n1"], "g1"), (gm, W["nm"], "gm"), (g2, W["n2"], "g2")):
            S.add("sync", lambda e, dst=dst, src=src: e.dma_start(out=dst, in_=src.broadcast_to([128, D])),
                  writes=[key], dma=True)
        for h0, h1, nm in ((0, 8, "a_q_norm"), (8, 10, "a_k_norm"), (10, 18, "b_k_norm"), (18, 26, "b_q_norm")):
            for h in range(h0, h1):
                S.add("sync", lambda e, h=h, nm=nm: e.dma_start(out=gall[:, h, :], in_=W[nm].broadcast_to([128, 64])),
                      writes=[("gall", h)], dma=True)
        gall_keys = [("gall", h) for h in range(NH)]
        for i, nm in enumerate(("b_lambda_q1", "b_lambda_k1", "b_lambda_q2", "b_lambda_k2")):
            S.add("sync", lambda e, i=i, nm=nm: e.dma_start(out=lamv[:, i, :], in_=W[nm].broadcast_to([128, 64])),
                  writes=[("lamv", i)], dma=True)
        S.add("sync", lambda e: e.dma_start(out=gout, in_=W["b_out_norm"].rearrange("o n -> n o")),
              writes=["gout"], dma=True)
        S.add("vector", lambda e: e.tensor_tensor(out=lamv[:, 0, :], in0=lamv[:, 0, :], in1=lamv[:, 1, :], op=ALU.mult),
              reads=[("lamv", 0), ("lamv", 1)], writes=[("lamv", 0)])
        S.add("vector", lambda e: e.tensor_tensor(out=lamv[:, 2, :], in0=lamv[:, 2, :], in1=lamv[:, 3, :], op=ALU.mult),
              reads=[("lamv", 2), ("lamv", 3)], writes=[("lamv", 2)])
        S.add("vector", lambda e: e.tensor_reduce(out=sm1[:, 0:1], in_=lamv[:, 0, :], axis=AX.X, op=ALU.add),
              reads=[("lamv", 0)], writes=["sm1a"])
        S.add("vector", lambda e: e.tensor_reduce(out=sm1[:, 1:2], in_=lamv[:, 2, :], axis=AX.X, op=ALU.add),
              reads=[("lamv", 2)], writes=["sm1b"])
        S.add("scalar", lambda e: e.activation(out=sm1[:, 2:4], in_=sm1[:, 0:2], func=AF.Exp),
              reads=["sm1a", "sm1b"], writes=["sm1e"])
        S.add("vector", lambda e: e.tensor_tensor(out=sm1[:, 4:5], in0=sm1[:, 3:4], in1=sm1[:, 2:3], op=ALU.subtract),
              reads=["sm1e"], writes=["sm1d"])
        S.add("vector", lambda e: e.tensor_scalar(out=neglam, in0=sm1[:, 4:5], scalar1=-LAM_INIT, scalar2=None, op0=ALU.add),
              reads=["sm1d"], writes=["neglam"])
        S.add("vector", lambda e: e.tensor_scalar(out=gout, in0=gout, scalar1=1.0 - LAM_INIT, scalar2=None, op0=ALU.mult),
              reads=["gout"], writes=["gout"])

        def make_conv(stage, stage_bf, Tg, engines):
            cnt = [0]
            rr = [0]

            def eng():
                rr[0] += 1
                return engines[rr[0] % len(engines)]

            def load_cast(src_ap, n, C, dst_bf=None, dst_key=None):
                slot = cnt[0] % 2
                cnt[0] += 1
                sv = stage[slot][:, 0:n * C].rearrange("p (a c) -> p a c", a=n, c=C)
                S.add("sync", lambda e: e.dma_start(out=sv, in_=src_ap.rearrange("(a p) c -> p a c", p=128)),
                      writes=[("stg", slot)], dma=True)
                if dst_bf is None:
                    bv = stage_bf[slot][:, 0:n * C].rearrange("p (a c) -> p a c", a=n, c=C)
                    copy_op(eng(), bv, sv, [("stg", slot)], [("stb", slot)])
                    return bv, ("stb", slot)
                copy_op(eng(), dst_bf, sv[:, 0, :].rearrange("p (f c) -> p f c", f=NF, c=128), [("stg", slot)], [dst_key])
                return dst_bf, dst_key

            def steps_ffn(j):
                st_ = []
                Tv = Tg.rearrange("p f (k c) -> p f k c", k=KD, c=128)
                for gi, key in enumerate(("g%d" % j, "u%d" % j)):
                    for k in range(KD):
                        st_.append(lambda gi=gi, key=key, k=k: load_cast(
                            W[key][k * 128:(k + 1) * 128, :], 1, DFF, dst_bf=Tv[:, :, k, :], dst_key=("Tg", k)))

                    def store_g(gi=gi):
                        for f0 in range(0, NF, 6):
                            f1 = min(NF, f0 + 6)
                            S.add("gpsimd", lambda e, f0=f0, f1=f1: e.dma_start(
                                out=wgu_d[j][f0:f1, :, gi * KD * 128:(gi + 1) * KD * 128].rearrange("f p x -> p f x"),
                                in_=Tg[:, f0:f1, :]),
                                reads=[("Tg", k_) for k_ in range(KD)], writes=[("wgu", j, gi, f0)], dma=True)
                    st_.append(store_g)
                for f0 in range(0, NF, 2):
                    def wd_step(f0=f0):
                        bv, key = load_cast(W["d%d" % j][f0 * 128:(f0 + 2) * 128, :], 2, D)
                        for half in range(2):
                            S.add("gpsimd", lambda e, half=half: e.dma_start(
                                out=wd_d[j][half, f0:f0 + 2].rearrange("f p x -> p f x"),
                                in_=bv[:, :, half * 512:(half + 1) * 512]),
                                reads=[key], writes=[("wd", j, f0, half)], dma=True)
                    st_.append(wd_step)
                return st_

            def steps_win():
                st_ = []
                for k in range(KD):
                    def f(k=k):
                        bv, key = load_cast(W["in"][k * 128:(k + 1) * 128, :], 1, INW)
                        S.add("gpsimd", lambda e: e.dma_start(out=win_d[:, k, :], in_=bv[:, 0, :]),
                              reads=[key], writes=[("win_d", k)], dma=True)
                    st_.append(f)
                return st_

            def steps_mix():
                st_ = []
                for k in range(KD):
                    def f(k=k):
                        bv, key = load_cast(W["bg"][k * 128:(k + 1) * 128, :], 1, 2 * D)
                        for ab in range(2):
                            S.add("gpsimd", lambda e, ab=ab: e.dma_start(
                                out=wc_d[:, :, ab * 8 + k, :].rearrange("c p j -> p c j"),
                                in_=bv[:, 0, ab * D:(ab + 1) * D].rearrange("p (c j) -> p c j", c=8, j=128)),
                                reads=[key], writes=[("wc_d", ab, k)], dma=True)
                    st_.append(f)
                for oi, key_w in enumerate(("oa", "ob")):
                    for k0 in range(0, 4, 2):
                        def f(oi=oi, key_w=key_w, k0=k0):
                            bv, key = load_cast(W[key_w][k0 * 128:(k0 + 2) * 128, :], 2, D)
                            for kk in range(2):
                                S.add("gpsimd", lambda e, kk=kk: e.dma_start(
                                    out=wc_d[:, :, 16 + oi * 4 + k0 + kk, :].rearrange("c p j -> p c j"),
                                    in_=bv[:, kk, :].rearrange("p (c j) -> p c j", c=8, j=128)),
                                    reads=[key], writes=[("wc_d", 2 + oi, k0 + kk)], dma=True)
                        st_.append(f)
                for k0 in range(0, KD, 2):
                    def f(k0=k0):
                        bv, key = load_cast(W["out"][k0 * 128:(k0 + 2) * 128, :], 2, D)
                        S.add("gpsimd", lambda e: e.dma_start(out=wout_d[:, k0:k0 + 2, :], in_=bv),
                              reads=[key], writes=[("wout_d", k0)], dma=True)
                    st_.append(f)
                return st_
            return steps_ffn, steps_win, steps_mix

        _stage = [A.alloc([128, 3072], F32) for _ in range(2)]
        _stage_bf = [A.alloc([128, 3072], BF16) for _ in range(2)]
        _Tg = A.alloc([128, NF, KD * 128], BF16)
        p_ffn, p_win, p_mix = make_conv(_stage, _stage_bf, _Tg, ("vector", "scalar", "gpsimd"))
        for stp in p_ffn(1) + p_win():
            stp()
        S.barrier()
        bg_steps = []

        def rsqrt_ops(eng, v, y, t, vk, yk, tk):
            S.add(eng, lambda e: e.tensor_scalar(out=y.bitcast(I32), in0=v.bitcast(I32), scalar1=1, scalar2=None,
                                                 op0=ALU.arith_shift_right), reads=[vk], writes=[yk])
            S.add(eng, lambda e: e.tensor_scalar(out=y.bitcast(I32), in0=y.bitcast(I32), scalar1=-1,
                                                 scalar2=0x5f3759df, op0=ALU.mult, op1=ALU.add), reads=[yk], writes=[yk])
            for _ in range(3):
                S.add(eng, lambda e: e.tensor_tensor(out=t, in0=y, in1=y, op=ALU.mult), reads=[yk], writes=[tk])
                S.add(eng, lambda e: e.scalar_tensor_tensor(out=t, in0=t, scalar=-0.5, in1=v, op0=ALU.mult, op1=ALU.mult),
                      reads=[tk, vk], writes=[tk])
                S.add(eng, lambda e: e.scalar_tensor_tensor(out=y, in0=t, scalar=1.5, in1=y, op0=ALU.add, op1=ALU.mult),
                      reads=[tk, yk], writes=[yk])

        def rsqrt_steps(eng, v, y, t, vk, yk, tk):
            st_ = []
            st_.append(lambda: S.add(eng, lambda e: e.tensor_scalar(out=y.bitcast(I32), in0=v.bitcast(I32), scalar1=1, scalar2=None,
                                                                    op0=ALU.arith_shift_right), reads=[vk], writes=[yk]))
            st_.append(lambda: S.add(eng, lambda e: e.tensor_scalar(out=y.bitcast(I32), in0=y.bitcast(I32), scalar1=-1,
                                                                    scalar2=0x5f3759df, op0=ALU.mult, op1=ALU.add), reads=[yk], writes=[yk]))
            for _ in range(3):
                st_.append(lambda: S.add(eng, lambda e: e.tensor_tensor(out=t, in0=y, in1=y, op=ALU.mult), reads=[yk], writes=[tk]))
                st_.append(lambda: S.add(eng, lambda e: e.scalar_tensor_tensor(out=t, in0=t, scalar=-0.5, in1=v, op0=ALU.mult, op1=ALU.mult),
                                         reads=[tk, vk], writes=[tk]))
                st_.append(lambda: S.add(eng, lambda e: e.scalar_tensor_tensor(out=y, in0=t, scalar=1.5, in1=y, op0=ALU.add, op1=ALU.mult),
                                         reads=[tk, yk], writes=[yk]))
            return st_

        def record_ops(fn):
            lst = []

            def fake(eng, emit, reads=(), writes=(), dma=False):
                lst.append((eng, _snap(emit), list(reads), list(writes), dma))
            S.add = fake
            try:
                fn()
            finally:
                del S.add
            return [(lambda a=a: S.add(a[0], a[1], reads=a[2], writes=a[3], dma=a[4])) for a in lst]

        class Stream:
            def __init__(self, name, slots, plan):
                self.name = name
                self.slots = slots
                self.plan = plan
                self.issued = 0

            def _issue_to(self, n):
                while self.issued < min(n, len(self.plan)):
                    i = self.issued
                    sl = i % len(self.slots)
                    S.add("sync", self.plan[i](self.slots[sl]), writes=[(self.name, sl)], dma=True)
                    self.issued += 1

            def get(self, i):
                self._issue_to(i + len(self.slots))
                sl = i % len(self.slots)
                return self.slots[sl], (self.name, sl)

            def prefetch(self, i):
                self._issue_to(i + len(self.slots))

        def norm_pre(xt, xkeys, gain, gkey, bufs):
            junk, ss4, vv4, rstd4, tt4, hb = bufs
            for t in range(4):
                jo = junk if junk is not None else hb[:, t, :]
                jw = ["junk"] if junk is not None else [("hb", t)]
                S.add("scalar", lambda e, t=t, jo=jo: e.activation(out=jo, in_=xt[:, t, :], func=AF.Square,
                                                                   accum_out=ss4[:, t:t + 1]),
                      reads=[xkeys[t]], writes=jw + ["ss4"])
            S.add("vector", lambda e: e.tensor_scalar(out=vv4, in0=ss4, scalar1=1.0 / D, scalar2=EPS, op0=ALU.mult,
                                                      op1=ALU.add), reads=["ss4"], writes=["vv4"])
            rsqrt_ops("vector", vv4, rstd4, tt4, "vv4", "rstd4", "tt4")
            for t in range(4):
                S.add("vector", lambda e, t=t: e.scalar_tensor_tensor(
                    out=hb[:, t, :], in0=xt[:, t, :], scalar=rstd4[:, t:t + 1], in1=gain, op0=ALU.mult, op1=ALU.mult),
                    reads=[xkeys[t], "rstd4", gkey], writes=[("hb", t)])

        def norm_tr(hTb, hkey, bufs, banks=(0, 1)):
            hb = bufs[5]
            for k in range(KD):
                bi = banks[k % 2]

                def emit(e, k=k, bi=bi):
                    for t in range(4):
                        ins = e.transpose(out=bank_bf(bi)[:, t * 128:(t + 1) * 128],
                                          in_=hb[:, t, k * 128:(k + 1) * 128], identity=ident)
                    return ins
                S.add("tensor", emit, reads=[("hb", t) for t in range(4)] + ["ident"], writes=[PS(bi)])
                copy_op("scalar", hTb[:, k, :], bank_bf(bi)[:, 0:512], [PS(bi)], [(hkey, k)])

        def norm_T(xt, xkeys, gain, gkey, hTb, hkey, bufs):
            norm_pre(xt, xkeys, gain, gkey, bufs)
            norm_tr(hTb, hkey, bufs)

        def ffn_block(xt, xkeys, hTb, hkey, j, wgu_s, wd_s, wbase, AT, sg, hook=None, hook_f=8, hooks=None):
            for f in range(NF):
                wsl, wkey = wgu_s.get(wbase[0] + f)
                gb_, ub_ = (f % 2), 2 + (f % 2)

                def emit_g(e, wsl=wsl, b=gb_):
                    for k in range(KD):
                        ins = e.matmul(bank(b), lhsT=wsl[:, 0, k, :], rhs=hTb[:, k, :], start=(k == 0), stop=(k == KD - 1))
                    return ins

                def emit_u(e, wsl=wsl, b=ub_):
                    for k in range(KD):
                        ins = e.matmul(bank(b), lhsT=wsl[:, 1, k, :], rhs=hTb[:, k, :], start=(k == 0), stop=(k == KD - 1))
                    return ins
                hk = [(hkey, k) for k in range(KD)]
                S.add("tensor", emit_g, reads=hk + [wkey], writes=[PS(gb_)])
                S.add("tensor", emit_u, reads=hk + [wkey], writes=[PS(ub_)])
                sgl = sg[f % 2]
                S.add("scalar", lambda e, sgl=sgl, b=gb_: e.activation(out=sgl, in_=bank(b), func=AF.Silu),
                      reads=[PS(gb_)], writes=[("sg", f % 2)])
                S.add("vector", lambda e, sgl=sgl, b=ub_, f=f: e.tensor_tensor(out=AT[:, f, :], in0=sgl, in1=bank(b), op=ALU.mult),
                      reads=[("sg", f % 2), PS(ub_)], writes=[("AT", f)])
                if hook is not None and f == hook_f:
                    hook()
                if hooks is not None and f in hooks:
                    hooks[f]()
            wbase[0] += NF
            for half in range(2):
                yb0 = 4 if half == 0 else 0
                for f0 in range(0, NF, 2):
                    dsl, dkey = wd_s.get(wbase[1] + (half * NF + f0) // 2)

                    def emit_d(e, dsl=dsl, f0=f0, yb0=yb0):
                        for ff in range(2):
                            f = f0 + ff
                            for t in range(4):
                                ins = e.matmul(bank(yb0 + t), lhsT=AT[:, f, t * 128:(t + 1) * 128], rhs=dsl[:, ff, :],
                                               start=(f == 0), stop=(f == NF - 1))
                        return ins
                    S.add("tensor", emit_d, reads=[("AT", f0), ("AT", f0 + 1), dkey], writes=[PS(yb0 + t) for t in range(4)])
                for t in range(4):
                    sl = xt[:, t, half * 512:(half + 1) * 512]
                    S.add("vector", lambda e, sl=sl, t=t, yb0=yb0: e.scalar_tensor_tensor(
                        out=sl, in0=bank(yb0 + t), scalar=0.5, in1=sl, op0=ALU.mult, op1=ALU.add),
                        reads=[PS(yb0 + t), xkeys[t]], writes=[xkeys[t]])
            wbase[1] += NF

        def wgu_plan(j, nblocks):
            plan = []
            for _ in range(nblocks):
                for f in range(NF):
                    plan.append(lambda dst, f=f: (lambda e: e.dma_start(
                        out=dst, in_=wgu_d[j][f].rearrange("p (g k c) -> p g k c", g=2, k=KD, c=128))))
            return plan

        def wd_plan(j, nblocks):
            plan = []
            for _ in range(nblocks):
                for half in range(2):
                    for f0 in range(0, NF, 2):
                        plan.append(lambda dst, half=half, f0=f0: (lambda e: e.dma_start(
                            out=dst, in_=wd_d[j][half, f0:f0 + 2].rearrange("f p x -> p f x"))))
            return plan

        NSTOP = stop_after
        for si, sq in enumerate(seqs):
            T, TQ = sq["T"], sq["TQ"]
            NBLK = T // 512
            NQB = TQ // 512
            NKB = T // 128
            A.cur = persist_mark
            xin = A.alloc([128, 4, D], F32)
            hb = A.alloc([128, 4, D], BF16)
            hTf = A.alloc([128, KD, 512], BF16)
            hTm = A.alloc([128, KD, 512], BF16)
            AT = A.alloc([128, NF, 512], BF16)
            wgu_slots = [A.alloc([128, 2, KD, 128], BF16) for _ in range(4)]
            wd_slots = [A.alloc([128, 2, 512], BF16) for _ in range(3)]
            win_sb = A.alloc([128, KD, INW], BF16)
            xq2 = [A.alloc([128, NH, 64], F32) for _ in range(2)]
            tq = A.alloc([128, NH, 64], F32)
            qkn = A.alloc([128, 4, QKW], BF16)
            qkT = A.alloc([128, 13, 512], BF16)
            vst = A.alloc([128, 4, VW], BF16)
            tabt = [A.alloc([128, 4, 80], F32) for _ in range(3)]
            sg = [A.alloc([128, 512], F32) for _ in range(2)]
            ss4 = A.alloc([128, 4], F32)
            vv4 = A.alloc([128, 4], F32)
            rstd4 = A.alloc([128, 4], F32)
            tt4 = A.alloc([128, 4], F32)
            ssq = A.alloc([128, NH], F32)
            vvq = A.alloc([128, NH], F32)
            rq = A.alloc([128, NH], F32)
            ttq = A.alloc([128, NH], F32)
            rA = [A.alloc([128, 10, 32], F32) for _ in range(4)]
            rB = [A.alloc([128, 16, 8], F32) for _ in range(4)]
            nbufs = (None, ss4, vv4, rstd4, tt4, hb)
            xkeys = [("xin", t) for t in range(4)]

            wgu_s = Stream("wgu", wgu_slots, wgu_plan(1, NBLK))
            wd_s = Stream("wd", wd_slots, wd_plan(1, NBLK))
            wbase = [0, 0]
            own_chunks = [(0, 512), (512, 512), (1024, 512), (1536, 512), (2048, 256)]
            rest_chunks = [(512, 512), (1024, 128), (1664, 512), (2176, 128)]

            def load_x(b, sq=sq, xin=xin):
                S.add("sync", lambda e: e.dma_start(
                    out=xin, in_=sq["x"][b * 512:(b + 1) * 512, :].rearrange("(t p) d -> p t d", p=128)),
                    writes=xkeys, dma=True)

            def load_tab(b, sq=sq, tabt=tabt):
                S.add("sync", lambda e: e.dma_start(
                    out=tabt[b % 3], in_=sq["tab"][b * 512:(b + 1) * 512, :].rearrange("(t p) d -> p t d", p=128)),
                    writes=[("tab", b % 3)], dma=True)

            load_x(0)
            load_tab(0)
            for k0 in range(0, KD, 4):
                S.add("sync", lambda e, k0=k0: e.dma_start(out=win_sb[:, k0:k0 + 4, :], in_=win_d[:, k0:k0 + 4, :]),
                      writes=[("win_sb", k0)], dma=True)
            win_keys = [("win_sb", 0), ("win_sb", 4)]
            bank_ctr = [0]
            t3_pending = [None]

            def run_t3():
                if t3_pending[0] is not None:
                    f = t3_pending[0]
                    t3_pending[0] = None
                    f()

            def proj_mm(b, t, chunks):
                xq = xq2[t % 2]
                xqf = xq.rearrange("p h d -> p (h d)")
                for ci, (c0, ncol) in enumerate(chunks):
                    bi = 2 + bank_ctr[0] % 6
                    bank_ctr[0] += 1

                    def emit_p(e, c0=c0, ncol=ncol, bi=bi):
                        for k in range(KD):
                            ins = e.matmul(bank(bi)[:, 0:ncol], lhsT=hTm[:, k, t * 128:(t + 1) * 128],
                                           rhs=win_sb[:, k, c0:c0 + ncol], start=(k == 0), stop=(k == KD - 1))
                        return ins
                    S.add("tensor", emit_p, reads=[("hTm", k) for k in range(KD)] + win_keys, writes=[PS(bi)])
                    a0, a1 = c0, min(c0 + ncol, QKW)
                    b0, b1 = max(c0, QKW), c0 + ncol
                    if a1 > a0:
                        S.add("scalar", lambda e, bi=bi, a0=a0, a1=a1, c0=c0: e.copy(
                            out=xqf[:, a0:a1], in_=bank(bi)[:, a0 - c0:a1 - c0]),
                            reads=[PS(bi)], writes=[("xq", t % 2, ci)])
                    if b1 > b0:
                        S.add("scalar", lambda e, bi=bi, b0=b0, b1=b1, c0=c0: e.copy(
                            out=vst[:, t, b0 - QKW:b1 - QKW], in_=bank(bi)[:, b0 - c0:b1 - c0]),
                            reads=[PS(bi)], writes=[("vst", t)])

            def qk_chain(b, t, chunks, h0, h1, tsl, tkey):
                xq = xq2[t % 2]
                xqk = [("xq", t % 2, ci) for ci in range(len(chunks))]
                H = h1 - h0
                X = xq[:, h0:h1, :]
                Tq = tq[:, h0:h1, :]
                S.add("scalar", lambda e: e.activation(out=Tq, in_=X, func=AF.Square), reads=xqk, writes=["tq"])
                S.add("vector", lambda e: e.tensor_reduce(out=ssq[:, h0:h1], in_=Tq, axis=AX.X, op=ALU.add),
                      reads=["tq"], writes=["ssq"])
                S.add("vector", lambda e: e.tensor_scalar(out=vvq[:, h0:h1], in0=ssq[:, h0:h1], scalar1=1.0 / 64,
                                                          scalar2=EPS, op0=ALU.mult, op1=ALU.add),
                      reads=["ssq"], writes=["vvq"])
                rsqrt_ops("vector", vvq[:, h0:h1], rq[:, h0:h1], ttq[:, h0:h1], "vvq", "rq", "ttq")
                S.add("gpsimd", lambda e: e.tensor_tensor(out=X, in0=X, in1=gall[:, h0:h1, :], op=ALU.mult),
                      reads=xqk + gall_keys + ["tq"], writes=xqk)
                S.add("vector", lambda e: e.tensor_tensor(
                    out=X, in0=X, in1=rq[:, h0:h1].unsqueeze(2).broadcast_to([128, H, 64]), op=ALU.mult),
                    reads=xqk + ["rq"], writes=xqk)
                qv = qkn[:, t, :].rearrange("p (h d) -> p h d", h=NH, d=64)
                for (r0, r1, m, cofs, rbuf, rkey) in ((h0, 10, 32, 0, rA, "rA"), (10, h1, 8, 64, rB, "rB")):
                    Hr = r1 - r0
                    x1 = xq[:, r0:r1, 0:m]
                    x2 = xq[:, r0:r1, m:2 * m]
                    cs = tsl[:, t, cofs:cofs + m].unsqueeze(1).broadcast_to([128, Hr, m])
                    sn = tsl[:, t, cofs + m:cofs + 2 * m].unsqueeze(1).broadcast_to([128, Hr, m])
                    bA, bB, bC, bD = [rb[:, 0:Hr, :] for rb in rbuf]
                    S.add("vector", lambda e, bA=bA, x1=x1, cs=cs: e.tensor_tensor(out=bA, in0=x1, in1=cs, op=ALU.mult),
                          reads=xqk + [tkey], writes=[(rkey, 0)])
                    S.add("vector", lambda e, bB=bB, x2=x2, sn=sn: e.tensor_tensor(out=bB, in0=x2, in1=sn, op=ALU.mult),
                          reads=xqk + [tkey], writes=[(rkey, 1)])
                    S.add("vector", lambda e, bA=bA, bB=bB, r0=r0, r1=r1, m=m: e.tensor_tensor(
                        out=qv[:, r0:r1, 0:m], in0=bA, in1=bB, op=ALU.subtract),
                        reads=[(rkey, 0), (rkey, 1)], writes=[("qkn", t, rkey, 0)])
                    S.add("gpsimd", lambda e, bC=bC, x2=x2, cs=cs: e.tensor_tensor(out=bC, in0=x2, in1=cs, op=ALU.mult),
                          reads=xqk + [tkey], writes=[(rkey, 2)])
                    S.add("gpsimd", lambda e, bD=bD, x1=x1, sn=sn: e.tensor_tensor(out=bD, in0=x1, in1=sn, op=ALU.mult),
                          reads=xqk + [tkey], writes=[(rkey, 3)])
                    S.add("gpsimd", lambda e, bC=bC, bD=bD, r0=r0, r1=r1, m=m: e.tensor_tensor(
                        out=qv[:, r0:r1, m:2 * m], in0=bC, in1=bD, op=ALU.add),
                        reads=[(rkey, 2), (rkey, 3)], writes=[("qkn", t, rkey, 1)])
                    if m == 8:
                        S.add("gpsimd", lambda e, r0=r0, r1=r1: e.tensor_copy(out=qv[:, r0:r1, 16:64], in_=xq[:, r0:r1, 16:64]),
                              reads=xqk, writes=[("qkn", t, rkey, 2)])

            def make_t3(b, own, sq=sq):
                def t3():
                    qkn_keys = [("qkn", t, rk, i) for t in range(4) for rk, n in (("rA", 2), ("rB", 3)) for i in range(n)]
                    groups = list(range(13)) if own else list(range(4, 9))
                    for r in groups:
                        bi = 4 + r % 2

                        def emit_t(e, r=r, bi=bi):
                            for t in range(4):
                                ins = e.transpose(out=bank_bf(bi)[:, t * 128:(t + 1) * 128],
                                                  in_=qkn[:, t, r * 128:(r + 1) * 128], identity=ident)
                            return ins
                        S.add("tensor", emit_t, reads=qkn_keys + ["ident"], writes=[PS(bi)])
                        copy_op("scalar" if r % 2 == 0 else "vector", qkT[:, r, :], bank_bf(bi)[:, 0:512],
                                [PS(bi)], [("qkT", r)])
                    blk = slice(b * 512, (b + 1) * 512)
                    if own:
                        S.add("gpsimd", lambda e: e.dma_start(
                            out=sq["QTA"][:, :, blk].rearrange("g r t -> r g t"), in_=qkT[:, 0:4, :]),
                            reads=[("qkT", r) for r in range(0, 4)], writes=[("QTA", b)], dma=True)
                        S.add("gpsimd", lambda e: e.dma_start(
                            out=sq["QTB"][:, :, blk].rearrange("g r t -> r g t"), in_=qkT[:, 9:13, :]),
                            reads=[("qkT", r) for r in range(9, 13)], writes=[("QTB", b)], dma=True)
                    S.add("gpsimd", lambda e: e.dma_start(out=sq["KTA"][:, blk], in_=qkT[:, 4, :]),
                          reads=[("qkT", 4)], writes=[("KTA", b)], dma=True)
                    S.add("gpsimd", lambda e: e.dma_start(
                        out=sq["KTB"][:, :, blk].rearrange("g r t -> r g t"), in_=qkT[:, 5:9, :]),
                        reads=[("qkT", r) for r in range(5, 9)], writes=[("KTB", b)], dma=True)
                return t3

            norm_pre(xin, xkeys, g1, "g1", nbufs)
            norm_tr(hTf, "hTf", nbufs)
            deferred = {}
            for b in range(NBLK):
                own = b * 512 < TQ
                if b + 1 < NBLK:
                    load_tab(b + 1)
                ffn_block(xin, xkeys, hTf, "hTf", 1, wgu_s, wd_s, wbase, AT, sg, hooks=deferred)
                deferred = {}
                if own:
                    S.add("gpsimd", lambda e, b=b: e.dma_start(
                        out=sq["x1"][b * 512:(b + 1) * 512, :].rearrange("(t p) d -> p t d", p=128), in_=xin),
                        reads=xkeys, writes=[("x1", b)], dma=True)
                norm_pre(xin, xkeys, gm, "gm", nbufs)
                norm_tr(hTm, "hTm", nbufs)
                if b + 1 < NBLK:
                    load_x(b + 1)
                    norm_pre(xin, xkeys, g1, "g1", nbufs)
                chunks = own_chunks if own else rest_chunks
                h0, h1 = (0, NH) if own else (8, 18)
                tsl = tabt[b % 3]
                tkey = ("tab", b % 3)
                proj_mm(b, 0, chunks)
                qk_chain(b, 0, chunks, h0, h1, tsl, tkey)
                proj_mm(b, 1, chunks)
                qk_chain(b, 1, chunks, h0, h1, tsl, tkey)
                if b + 1 < NBLK:
                    norm_tr(hTf, "hTf", nbufs)
                proj_mm(b, 2, chunks)
                proj_mm(b, 3, chunks)
                blk = slice(b * 512, (b + 1) * 512)
                vkeys = [("vst", t) for t in range(4)]
                S.add("gpsimd", lambda e, blk=blk: e.dma_start(
                    out=sq["VA"][blk, :].rearrange("(t p) c -> p t c", p=128), in_=vst[:, :, 0:128]),
                    reads=vkeys, writes=[("VA", b)], dma=True)
                S.add("gpsimd", lambda e, blk=blk: e.dma_start(
                    out=sq["VB"][blk, :].rearrange("(t p) c -> p t c", p=128), in_=vst[:, :, 128:VW]),
                    reads=vkeys, writes=[("VB", b)], dma=True)

                steps = record_ops(lambda: (qk_chain(b, 2, chunks, h0, h1, tsl, tkey),
                                            qk_chain(b, 3, chunks, h0, h1, tsl, tkey)))
                t3fn = make_t3(b, own)

                def pop_some(steps=steps, n=6):
                    for _ in range(min(n, len(steps))):
                        steps.pop(0)()

                def pop_all_then_t3(steps=steps, t3fn=t3fn):
                    while steps:
                        steps.pop(0)()
                    t3fn()
                deferred = {f: pop_some for f in range(1, 15)}
                deferred[15] = pop_all_then_t3
            for f in sorted(deferred):
                deferred[f]()
            S.barrier()
            if NSTOP == 1:
                continue

            A.cur = persist_mark
            KTA_sb = A.alloc([128, T], BF16)
            VAaug = A.alloc([128, NKB, 2, 65], BF16)
            KTB_sb = [A.alloc([128, T], BF16) for _ in range(2)]
            VB_sb = [A.alloc([128, NKB, 128], BF16) for _ in range(2)]
            QT = [A.alloc([128, 512], BF16) for _ in range(3)]
            ET = [A.alloc([128, 1024], BF16) for _ in range(4)]
            zacc = A.alloc([128, 512], F32)
            rz1 = A.alloc([128, 512], F32)
            rz2 = A.alloc([128, 512], F32)
            bcs = A.alloc([128, 512], F32)
            t1 = A.alloc([128, 512], F32)
            t2 = A.alloc([128, 512], F32)
            obf = A.alloc([128, 512], F32)
            sqf = A.alloc([128, 512], F32)
            vvb = A.alloc([128, 512], F32)
            rsd = A.alloc([128, 512], F32)
            ttb = A.alloc([128, 512], F32)
            obb = [A.alloc([128, 512], BF16) for _ in range(2)]

            S.add("sync", lambda e: e.dma_start(out=KTA_sb, in_=sq["KTA"]), writes=["KTA_sb"], dma=True)
            S.add("gpsimd", lambda e: e.memset(VAaug[:, :, :, 64:65], 1.0), writes=["VAones"])
            KG = 8
            for kb0 in range(0, NKB, KG):
                kb1 = min(NKB, kb0 + KG)
                for kv in range(2):
                    S.add("sync", lambda e, kb0=kb0, kb1=kb1, kv=kv: e.dma_start(
                        out=VAaug[:, kb0:kb1, kv, 0:64],
                        in_=sq["VA"][kb0 * 128:kb1 * 128, kv * 64:(kv + 1) * 64].rearrange("(k p) d -> p k d", p=128)),
                        writes=[("VAaug", kb0, kv)], dma=True)
            va_keys = [("VAaug", kb0, kv) for kb0 in range(0, NKB, KG) for kv in range(2)] + ["VAones"]

            def load_B(h, sq=sq, KTB_sb=KTB_sb, VB_sb=VB_sb, NKB=NKB):
                sl = h % 2
                S.add("sync", lambda e: e.dma_start(out=KTB_sb[sl], in_=sq["KTB"][h]), writes=[("KTB_sb", sl)], dma=True)
                for kb0 in range(0, NKB, KG):
                    kb1 = min(NKB, kb0 + KG)
                    S.add("sync", lambda e, kb0=kb0, kb1=kb1: e.dma_start(
                        out=VB_sb[sl][:, kb0:kb1, :],
                        in_=sq["VB"][kb0 * 128:kb1 * 128, h * 128:(h + 1) * 128].rearrange("(k p) c -> p k c", p=128)),
                        writes=[("VB_sb", sl, kb0)], dma=True)

            ctr = {"qt": 0, "et": 0, "sp": 0, "kb": 0}
            osb = A.alloc([128, 4, 512], F32)
            if si == 0:
                b_stage = [A.alloc([128, 3072], F32) for _ in range(2)]
                b_stage_bf = [A.alloc([128, 3072], BF16) for _ in range(2)]
                b_Tg = A.alloc([128, NF, KD * 128], BF16)
                b_ffn, _b_win, b_mix = make_conv(b_stage, b_stage_bf, b_Tg, ("gpsimd",))
                bg_steps.extend(b_ffn(2) + b_mix())
            pending_fin = [None]
            fin_steps = []

            def run_pending():
                if pending_fin[0] is not None:
                    f = pending_fin[0]
                    pending_fin[0] = None
                    r = f()
                    if r:
                        fin_steps.extend(r)

            def flush_steps():
                while fin_steps:
                    fin_steps.pop(0)()

            ring = {"pairs": [0, 1, 3]}

            def ring_take():
                p = ring["pairs"][ctr["sp"] % len(ring["pairs"])]
                ctr["sp"] += 1
                return p

            def attn_unit(qsrc, lhs_lo, lhs_hi, lkeys, pv_emit, pv_reads, pv_writes, post_exp=None):
                qi = ctr["qt"] % 3
                ctr["qt"] += 1
                qs, qkey = QT[qi], ("QT", qi)
                S.add("sync", lambda e: e.dma_start(out=qs, in_=qsrc), writes=[qkey], dma=True)
                pend = []
                lag = len(ring["pairs"]) - 1
                for kb in range(NKB):
                    sp = ring_take()
                    ksl = slice(kb * 128, (kb + 1) * 128)

                    def emit_qk(e, sp=sp, ksl=ksl):
                        e.matmul(pp[sp][:, 0:512], lhsT=lhs_lo[:, ksl], rhs=qs[0:64, :], start=True, stop=True)
                        return e.matmul(pp[sp][:, 512:1024], lhsT=lhs_hi[:, ksl], rhs=qs[64:128, :], start=True, stop=True)
                    S.add("tensor", emit_qk, reads=lkeys + [qkey], writes=[PS(2 * sp), PS(2 * sp + 1)])
                    ei = ctr["et"] % 4
                    ctr["et"] += 1
                    es, ekey = ET[ei], ("ET", ei)
                    S.add("scalar", lambda e, sp=sp, es=es: e.activation(out=es, in_=pp[sp][:, :], func=AF.Exp, scale=SCALE),
                          reads=[PS(2 * sp), PS(2 * sp + 1)], writes=[ekey])
                    if post_exp is not None:
                        post_exp(kb, es, ekey)
                    pend.append((pv_emit(kb, es), pv_reads + [ekey]))
                    if len(pend) > lag:
                        p0 = pend.pop(0)
                        S.add("tensor", p0[0], reads=p0[1], writes=pv_writes)
                    if kb == min(3, NKB - 1):
                        run_pending()
                    if kb >= 3 and fin_steps:
                        fin_steps.pop(0)()
                    ctr["kb"] += 1
                    if bg_steps and ctr["kb"] % 16 == 8:
                        bg_steps.pop(0)()
                for p0 in pend:
                    S.add("tensor", p0[0], reads=p0[1], writes=pv_writes)
                flush_steps()

            load_B(0)
            for g in range(4):
                for qb in range(NQB):
                    qsl = slice(qb * 512, (qb + 1) * 512)

                    def pv_emit_A(kb, es):
                        def f(e):
                            e.matmul(bank(4)[0:65, :], lhsT=VAaug[:, kb, 0, :], rhs=es[:, 0:512],
                                     start=(kb == 0), stop=(kb == NKB - 1))
                            return e.matmul(bank(5)[0:65, :], lhsT=VAaug[:, kb, 1, :], rhs=es[:, 512:1024],
                                            start=(kb == 0), stop=(kb == NKB - 1))
                        return f
                    attn_unit(sq["QTA"][g][:, qsl], KTA_sb[0:64, :], KTA_sb[64:128, :], ["KTA_sb"],
                              pv_emit_A, va_keys, [PS(4), PS(5)])
                    for kv in range(2):
                        S.add("vector", lambda e, kv=kv: e.tensor_copy(out=osb[0:65, kv, :], in_=bank(4 + kv)[0:65, :]),
                              reads=[PS(4 + kv)], writes=[("osb", kv)])

                    def fin_A(g=g, qb=qb, qsl=qsl):
                        sp = ring_take()
                        S.add("vector", lambda e: e.reciprocal(out=rz1[64:65, :], in_=osb[64:65, 0, :]),
                              reads=[("osb", 0)], writes=["rz1"])
                        S.add("vector", lambda e: e.reciprocal(out=rz2[64:65, :], in_=osb[64:65, 1, :]),
                              reads=[("osb", 1)], writes=["rz2"])

                        def emit_bc(e, sp=sp):
                            e.matmul(pp[sp][0:64, 0:512], lhsT=ones_f[64:65, 0:64], rhs=rz1[64:65, :], start=True, stop=True)
                            return e.matmul(pp[sp][0:64, 512:1024], lhsT=ones_f[64:65, 0:64], rhs=rz2[64:65, :],
                                            start=True, stop=True)
                        S.add("tensor", emit_bc, reads=["rz1", "rz2", "ones_f"], writes=[PS(2 * sp), PS(2 * sp + 1)])
                        for kv in range(2):
                            S.add("vector", lambda e, kv=kv, sp=sp: e.tensor_tensor(
                                out=obb[kv][0:64, :], in0=osb[0:64, kv, :], in1=pp[sp][0:64, kv * 512:(kv + 1) * 512],
                                op=ALU.mult), reads=[("osb", kv), PS(2 * sp + kv)], writes=[("obb", kv)])
                            hq = kv * 4 + g
                            S.add("gpsimd", lambda e, kv=kv, hq=hq: e.dma_start(
                                out=sq["OAT"][hq * 64:(hq + 1) * 64, qsl], in_=obb[kv][0:64, :]),
                                reads=[("obb", kv)], writes=[("OAT", hq, qb)], dma=True)
                    pending_fin[0] = fin_A
            ring["pairs"] = [0, 1]
            for h in range(4):
                sl = h % 2
                if h + 1 < 4:
                    load_B(h + 1)
                vb_keys = [("VB_sb", sl, kb0) for kb0 in range(0, NKB, KG)]
                for qb in range(NQB):
                    qsl = slice(qb * 512, (qb + 1) * 512)

                    def pv_emit_B(kb, es, sl=sl):
                        def f(e):
                            e.matmul(bank(4), lhsT=VB_sb[sl][:, kb, :], rhs=es[:, 0:512], start=(kb == 0), stop=(kb == NKB - 1))
                            e.matmul(bank(5), lhsT=VB_sb[sl][:, kb, :], rhs=es[:, 512:1024], start=(kb == 0), stop=(kb == NKB - 1))
                            return e.matmul(bank(6), lhsT=ones_bf, rhs=es[:, 0:512], start=(kb == 0), stop=(kb == NKB - 1))
                        return f

                    def post_exp_B(kb, es, ekey):
                        src = es[:, 512:1024]
                        if kb == 0:
                            S.add("vector", lambda e: e.tensor_copy(out=zacc, in_=src), reads=[ekey], writes=["zacc"])
                        else:
                            S.add("vector", lambda e: e.tensor_tensor(out=zacc, in0=zacc, in1=src, op=ALU.add),
                                  reads=[ekey, "zacc"], writes=["zacc"])
                    attn_unit(sq["QTB"][h][:, qsl], KTB_sb[sl][0:64, :], KTB_sb[sl][64:128, :], [("KTB_sb", sl)],
                              pv_emit_B, vb_keys + ["ones_bf"], [PS(4), PS(5), PS(6)], post_exp=post_exp_B)
                    S.add("tensor", lambda e: e.matmul(bank(7), lhsT=ones_f, rhs=zacc, start=True, stop=True),
                          reads=["zacc", "ones_f"], writes=[PS(7)])
                    S.add("scalar", lambda e: e.copy(out=osb[:, 0, :], in_=bank(4)), reads=[PS(4)], writes=[("osb", 0)])
                    S.add("scalar", lambda e: e.copy(out=osb[:, 1, :], in_=bank(5)), reads=[PS(5)], writes=[("osb", 1)])
                    S.add("vector", lambda e: e.reciprocal(out=rz1, in_=bank(6)), reads=[PS(6)], writes=["rz1"])
                    S.add("vector", lambda e: e.reciprocal(out=rz2, in_=bank(7)), reads=[PS(7)], writes=["rz2"])

                    def fin_B(h=h, qb=qb, qsl=qsl):
                        st_ = []
                        st_.append(lambda: S.add("vector", lambda e: e.tensor_tensor(out=t1, in0=osb[:, 0, :], in1=rz1, op=ALU.mult),
                                                 reads=[("osb", 0), "rz1"], writes=["t1"]))
                        st_.append(lambda: S.add("gpsimd", lambda e: e.tensor_tensor(out=t2, in0=osb[:, 1, :], in1=rz2, op=ALU.mult),
                                                 reads=[("osb", 1), "rz2"], writes=["t2"]))
                        st_.append(lambda: S.add("vector", lambda e: e.scalar_tensor_tensor(
                            out=obf, in0=t2, scalar=neglam[:, 0:1], in1=t1, op0=ALU.mult, op1=ALU.add),
                            reads=["t1", "t2", "neglam"], writes=["obf"]))
                        st_.append(lambda: S.add("gpsimd", lambda e: e.tensor_tensor(out=sqf, in0=obf, in1=obf, op=ALU.mult),
                                                 reads=["obf"], writes=["sqf"]))
                        slot = {}

                        def ss_mm():
                            sp = ring_take()
                            slot["sp"] = sp
                            S.add("tensor", lambda e: e.matmul(pp[sp][:, 0:512], lhsT=ones_f, rhs=sqf, start=True, stop=True),
                                  reads=["sqf", "ones_f"], writes=[PS(2 * sp), PS(2 * sp + 1)])
                        st_.append(ss_mm)

                        def vv():
                            sp = slot["sp"]
                            S.add("vector", lambda e: e.tensor_scalar(out=vvb, in0=pp[sp][:, 0:512], scalar1=1.0 / 128,
                                                                      scalar2=EPS, op0=ALU.mult, op1=ALU.add),
                                  reads=[PS(2 * sp)], writes=["vvb"])
                        st_.append(vv)
                        st_.extend(rsqrt_steps("vector", vvb, rsd, ttb, "vvb", "rsd", "ttb"))
                        st_.append(lambda: S.add("vector", lambda e: e.scalar_tensor_tensor(
                            out=obb[0], in0=obf, scalar=gout[:, 0:1], in1=rsd, op0=ALU.mult, op1=ALU.mult),
                            reads=["obf", "rsd", "gout"], writes=[("obb", 0)]))
                        st_.append(lambda: S.add("gpsimd", lambda e: e.dma_start(
                            out=sq["OBT"][h * 128:(h + 1) * 128, qsl], in_=obb[0]),
                            reads=[("obb", 0)], writes=[("OBT", h, qb)], dma=True))
                        return st_
                    pending_fin[0] = fin_B
            run_pending()
            flush_steps()
            while bg_steps:
                bg_steps.pop(0)()
            S.barrier()
            if NSTOP == 2:
                continue

            A.cur = persist_mark
            xin2 = [A.alloc([128, 4, D], F32) for _ in range(2)]
            hb = A.alloc([128, 4, D], BF16)
            hT = A.alloc([128, KD, 512], BF16)
            hTf2 = A.alloc([128, KD, 512], BF16)
            AT = A.alloc([128, NF, 512], BF16)
            wgu_slots = [A.alloc([128, 2, KD, 128], BF16) for _ in range(4)]
            wd_slots = [A.alloc([128, 2, 512], BF16) for _ in range(4)]
            wc_slots = [A.alloc([128, 24, 128], BF16) for _ in range(3)]
            wout_sb = A.alloc([128, KD, D], BF16)
            oaT = A.alloc([128, 4, 512], BF16)
            obT = A.alloc([128, 4, 512], BF16)
            mT = A.alloc([128, KD, 512], BF16)
            tha = [A.alloc([128, 512], F32) for _ in range(2)]
            thb = [A.alloc([128, 512], F32) for _ in range(2)]
            u1 = A.alloc([128, 512], F32)
            u2 = A.alloc([128, 512], F32)
            sg = [A.alloc([128, 512], F32) for _ in range(2)]
            junk = A.alloc([128, D], BF16)
            ss4 = A.alloc([128, 4], F32)
            vv4 = A.alloc([128, 4], F32)
            rstd4 = A.alloc([128, 4], F32)
            tt4 = A.alloc([128, 4], F32)
            nbufs = (junk, ss4, vv4, rstd4, tt4, hb)
            wgu_s = Stream("wgu", wgu_slots, wgu_plan(2, NQB))
            wd_s = Stream("wd", wd_slots, wd_plan(2, NQB))
            wbase = [0, 0]
            wc_plan = []
            for _ in range(NQB):
                for c in range(8):
                    wc_plan.append(lambda dst, c=c: (lambda e: e.dma_start(out=dst, in_=wc_d[c])))
            wc_s = Stream("wc", wc_slots, wc_plan)
            for k0 in range(0, KD, 4):
                S.add("sync", lambda e, k0=k0: e.dma_start(out=wout_sb[:, k0:k0 + 4, :], in_=wout_d[:, k0:k0 + 4, :]),
                      writes=[("wout_sb", k0)], dma=True)
            wout_keys = [("wout_sb", 0), ("wout_sb", 4)]

            def load_x1(b, sq=sq, xin2=xin2):
                S.add("sync", lambda e: e.dma_start(
                    out=xin2[b % 2], in_=sq["x1"][b * 512:(b + 1) * 512, :].rearrange("(t p) d -> p t d", p=128)),
                    writes=[("xin", b % 2, t) for t in range(4)], dma=True)

            def load_o(b, sq=sq, oaT=oaT, obT=obT):
                blk = slice(b * 512, (b + 1) * 512)
                S.add("sync", lambda e: e.dma_start(
                    out=oaT, in_=sq["OAT"][:, blk].rearrange("(c p) t -> p c t", p=128)), writes=["oaT"], dma=True)
                S.add("sync", lambda e: e.dma_start(
                    out=obT, in_=sq["OBT"][:, blk].rearrange("(c p) t -> p c t", p=128)), writes=["obT"], dma=True)

            load_x1(0)
            norm_pre(xin2[0], [("xin", 0, t) for t in range(4)], gm, "gm", nbufs)
            for b in range(NQB):
                blk = slice(b * 512, (b + 1) * 512)
                xt = xin2[b % 2]
                xkeys = [("xin", b % 2, t) for t in range(4)]
                if b + 1 < NQB:
                    load_x1(b + 1)
                if b == 0:
                    load_o(0)
                norm_tr(hT, "hT", nbufs)
                hk = [("hT", k) for k in range(KD)]
                for c in range(8):
                    wsl, wkey = wc_s.get(b * 8 + c)
                    bs = (c % 2) * 4

                    def emit_g(e, wsl=wsl, bs=bs):
                        for k in range(KD):
                            e.matmul(bank(bs), lhsT=wsl[:, k, :], rhs=hT[:, k, :], start=(k == 0), stop=(k == KD - 1))
                        for k in range(KD):
                            e.matmul(bank(bs + 1), lhsT=wsl[:, 8 + k, :], rhs=hT[:, k, :], start=(k == 0), stop=(k == KD - 1))
                        for k in range(4):
                            e.matmul(bank(bs + 2), lhsT=wsl[:, 16 + k, :], rhs=oaT[:, k, :], start=(k == 0), stop=(k == 3))
                        for k in range(4):
                            ins = e.matmul(bank(bs + 3), lhsT=wsl[:, 20 + k, :], rhs=obT[:, k, :], start=(k == 0), stop=(k == 3))
                        return ins
                    S.add("tensor", emit_g, reads=hk + [wkey, "oaT", "obT"], writes=[PS(bs + i) for i in range(4)])
                    ta, tb = tha[c % 2], thb[c % 2]
                    S.add("scalar", lambda e, ta=ta, bs=bs: e.activation(out=ta, in_=bank(bs), func=AF.Tanh, scale=0.5),
                          reads=[PS(bs)], writes=[("tha", c % 2)])
                    S.add("scalar", lambda e, tb=tb, bs=bs: e.activation(out=tb, in_=bank(bs + 1), func=AF.Tanh, scale=0.5),
                          reads=[PS(bs + 1)], writes=[("thb", c % 2)])
                    S.add("vector", lambda e, ta=ta, bs=bs: e.scalar_tensor_tensor(
                        out=u1, in0=ta, scalar=1.0, in1=bank(bs + 2), op0=ALU.add, op1=ALU.mult),
                        reads=[("tha", c % 2), PS(bs + 2)], writes=["u1"])
                    S.add("vector", lambda e, tb=tb, bs=bs: e.scalar_tensor_tensor(
                        out=u2, in0=tb, scalar=1.0, in1=bank(bs + 3), op0=ALU.add, op1=ALU.mult),
                        reads=[("thb", c % 2), PS(bs + 3)], writes=["u2"])
                    S.add("gpsimd", lambda e, c=c: e.tensor_tensor(out=mT[:, c, :], in0=u1, in1=u2, op=ALU.add),
                          reads=["u1", "u2"], writes=[("mT", c)])
                mk = [("mT", c) for c in range(8)]
                for t in range(4):
                    for half in range(2):
                        bi = (t * 2 + half) % 8

                        def emit_o(e, t=t, half=half, bi=bi):
                            for k in range(KD):
                                ins = e.matmul(bank(bi), lhsT=mT[:, k, t * 128:(t + 1) * 128],
                                               rhs=wout_sb[:, k, half * 512:(half + 1) * 512], start=(k == 0), stop=(k == KD - 1))
                            return ins
                        S.add("tensor", emit_o, reads=mk + wout_keys, writes=[PS(bi)])
                        xsl = xt[:, t, half * 512:(half + 1) * 512]
                        S.add("vector", lambda e, xsl=xsl, bi=bi: e.scalar_tensor_tensor(
                            out=xsl, in0=bank(bi), scalar=0.5, in1=xsl, op0=ALU.mult, op1=ALU.add),
                            reads=[PS(bi), xkeys[t]], writes=[xkeys[t]])
                if NSTOP != 3:
                    norm_T(xt, xkeys, g2, "g2", hTf2, "hTf2", nbufs)
                    nxt = None
                    if b + 1 < NQB:
                        def nxt(b=b):
                            load_o(b + 1)
                            wc_s.prefetch((b + 1) * 8)
                            norm_pre(xin2[(b + 1) % 2], [("xin", (b + 1) % 2, t) for t in range(4)], gm, "gm", nbufs)
                    ffn_block(xt, xkeys, hTf2, "hTf2", 2, wgu_s, wd_s, wbase, AT, sg, hook=nxt, hook_f=4)
                elif b + 1 < NQB:
                    load_o(b + 1)
                    norm_pre(xin2[(b + 1) % 2], [("xin", (b + 1) % 2, t) for t in range(4)], gm, "gm", nbufs)
                S.add("gpsimd", lambda e, blk=blk, xt=xt: e.dma_start(
                    out=sq["y"][blk, :].rearrange("(t p) d -> p t d", p=128), in_=xt),
                    reads=xkeys, writes=[("y", b)], dma=True)
            S.barrier()
        S.run(block)
    return nc


def _rot_table(pos):
    pos = np.asarray(pos, dtype=np.int64)
    row = (pos // 64).astype(np.float32)
    col = (pos % 64).astype(np.float32)
    invA = (np.float32(10000.0) ** (-np.arange(0, 32, 2, dtype=np.float32) / np.float32(32))).astype(np.float32)
    angA = np.concatenate([row[:, None] * invA[None, :], col[:, None] * invA[None, :]], axis=-1).astype(np.float32)
    invP = (np.float32(500000.0) ** (-np.arange(0, 16, 2, dtype=np.float32) / np.float32(16))).astype(np.float32)
    angP = (pos.astype(np.float32)[:, None] * invP[None, :]).astype(np.float32)
    tab = np.concatenate([np.cos(angA.astype(np.float64)), np.sin(angA.astype(np.float64)),
                          np.cos(angP.astype(np.float64)), np.sin(angP.astype(np.float64))], axis=-1)
    return np.ascontiguousarray(tab.astype(np.float32))


def _win_perm():
    qa = [(kv * 4 + g) * 64 + d for g in range(4) for kv in range(2) for d in range(64)]
    ka = list(range(512, 640))
    va = list(range(640, 768))
    qb = list(range(768, 1280))
    kb = list(range(1280, 1792))
    vb = list(range(1792, 2304))
    return np.array(qa + ka + kb + qb + va + vb, dtype=np.int64)


def make_in_maps(inputs, TS, TPK, TPQ, n_cores=8):
    f = lambda a: np.ascontiguousarray(np.asarray(a, dtype=np.float32))
    xp_all = f(inputs["x_prompt"])
    xs_all = f(inputs["x_sample"])
    nq = TPK // TPQ
    common = {}
    for nm in ("ffn1_norm", "mix_norm", "ffn2_norm", "a_q_norm", "a_k_norm", "b_q_norm", "b_k_norm",
               "b_lambda_q1", "b_lambda_k1", "b_lambda_q2", "b_lambda_k2", "b_out_norm"):
        common[nm] = f(inputs[nm]).reshape(1, -1)
    for nm in ("ffn1_w_gate", "ffn1_w_up", "ffn1_w_down", "ffn2_w_gate", "ffn2_w_up", "ffn2_w_down",
               "w_branch_gate", "w_o_a", "w_o_b", "w_out"):
        a = f(inputs[nm])
        common[nm] = np.ascontiguousarray(a.reshape(a.shape[-2], a.shape[-1]))
    win = f(inputs["w_in"]).reshape(D, INW)
    common["w_in"] = np.ascontiguousarray(win[:, _win_perm()])
    tab_s = _rot_table(np.arange(TS))
    in_maps = []
    for c in range(n_cores):
        pi, q = (c // nq) % xp_all.shape[0], c % nq
        pos_p = np.roll(np.arange(TPK), -q * TPQ)
        m = dict(common)
        m["xs"] = np.ascontiguousarray(xs_all[c])
        m["xp"] = np.ascontiguousarray(xp_all[pi][pos_p])
        m["tab_s"] = tab_s
        m["tab_p"] = _rot_table(pos_p)
        in_maps.append(m)
    return in_maps


_NC_CACHE = {}


def kernel(**inputs):
    TS, TPK, TPQ = 4096, 8192, 2048
    key = (TS, TPK, TPQ)
    if key not in _NC_CACHE:
        _NC_CACHE[key] = build(TS, TPK, TPQ)
    nc = _NC_CACHE[key]
    in_maps = make_in_maps(inputs, TS, TPK, TPQ)
    res = run_bass_kernel_spmd(nc, in_maps, core_ids=list(range(8)))
    y_prompt = np.empty((2, TPK, D), dtype=np.float32)
    y_sample = np.empty((8, TS, D), dtype=np.float32)
    for c in range(8):
        r = res.results[c]
        y_sample[c] = r["ys"]
        pi, q = c // 4, c % 4
        y_prompt[pi, q * TPQ:(q + 1) * TPQ] = r["yp"]
    return (y_prompt, y_sample)
```
